# Optimizing a Trainium2 kernel written in Bass

```python
import math
import jax, jax.numpy as jnp
from jax import lax
import numpy as np

D_MODEL = 1024
BATCH = 32
SEQ = 2048
DEPTH = 2
DEC_BATCH = 32
DEC_SEQ = 64
PAST_LEN = 2048

CHUNK = 64
HEAD_DIM = 64
ROT_DIM = HEAD_DIM // 4
ROPE_THETA = 500000.0
A_HEADS = 4
A_LEFT_CHUNKS = 8
A_WINDOW = A_LEFT_CHUNKS * CHUNK
A_BAND = A_WINDOW + CHUNK
REL_CLIP = 128
B_HEADS = 4
IDX_HEADS = 8
IDX_DIM = 64
TOPK_MAX = 256
C_HEADS = 4
N_MEM = 256
MEM_HEADS = 4
MEM_HEAD_DIM = D_MODEL // MEM_HEADS
D_FF = ((8 * D_MODEL // 3 + 127) // 128) * 128
QBLOCK = 128
N_BRANCH = 3
ALPHA = (2.0 * DEPTH) ** 0.25
BETA = (8.0 * DEPTH) ** -0.25
LN_EPS = 1e-5
SUBLN_EPS = 1e-5
A_WIDTH = A_HEADS * HEAD_DIM
B_WIDTH = B_HEADS * HEAD_DIM
C_WIDTH = C_HEADS * 2 * HEAD_DIM
IN_SIZES = (A_WIDTH, A_WIDTH, A_WIDTH,
            B_WIDTH, HEAD_DIM, HEAD_DIM, IDX_HEADS * IDX_DIM, IDX_DIM, IDX_HEADS,
            C_WIDTH, C_WIDTH, C_WIDTH,
            N_BRANCH * D_MODEL)
IN_COLS = sum(IN_SIZES)

kernel_name = 'hybrid_streaming_encoder_step'


def layer_norm(x, g, b):
    xf = x.astype(jnp.float32)
    mu = jnp.mean(xf, axis=-1, keepdims=True)
    var = jnp.mean(jnp.square(xf - mu), axis=-1, keepdims=True)
    return ((xf - mu) * lax.rsqrt(var + LN_EPS) * g.astype(jnp.float32) + b.astype(jnp.float32)).astype(x.dtype)


def swiglu(x, w_gu, w_d):
    g, u = jnp.split(x @ w_gu, 2, axis=-1)
    return (jax.nn.silu(g) * u) @ w_d


def rope(x, pos):
    half = ROT_DIM // 2
    inv_freq = ROPE_THETA ** (-jnp.arange(half, dtype=jnp.float32) / half)
    ang = pos.astype(jnp.float32)[:, None] * inv_freq
    ang = ang.reshape((1, pos.shape[0]) + (1,) * (x.ndim - 3) + (half,))
    cos, sin = jnp.cos(ang), jnp.sin(ang)
    xr = x[..., :ROT_DIM].astype(jnp.float32)
    x1, x2 = xr[..., :half], xr[..., half:]
    rot = jnp.concatenate([x1 * cos - x2 * sin, x2 * cos + x1 * sin], axis=-1)
    return jnp.concatenate([rot.astype(x.dtype), x[..., ROT_DIM:]], axis=-1)


def rel_bias_lookup(rel_bias, dist):
    idx = jnp.clip(dist, -REL_CLIP, REL_CLIP) + REL_CLIP
    return rel_bias[:, idx].astype(jnp.float32)


def band_attn_prompt(q, k, v, rel_bias):
    B, S = q.shape[:2]
    nc = S // CHUNK
    qc = q.reshape(B, nc, CHUNK, A_HEADS, HEAD_DIM)
    padw = ((0, 0), (A_WINDOW, 0), (0, 0), (0, 0))
    kp = jnp.pad(k, padw).reshape(B, nc + A_LEFT_CHUNKS, CHUNK, A_HEADS, HEAD_DIM)
    vp = jnp.pad(v, padw).reshape(B, nc + A_LEFT_CHUNKS, CHUNK, A_HEADS, HEAD_DIM)
    kb = jnp.concatenate([kp[:, j:j + nc] for j in range(A_LEFT_CHUNKS + 1)], axis=2)
    vb = jnp.concatenate([vp[:, j:j + nc] for j in range(A_LEFT_CHUNKS + 1)], axis=2)
    dist = jnp.arange(CHUNK)[:, None] + A_WINDOW - jnp.arange(A_BAND)[None, :]
    bias = rel_bias_lookup(rel_bias, dist)
    s = jnp.einsum('bcqhd,bckhd->bchqk', qc, kb).astype(jnp.float32) * HEAD_DIM ** -0.5 + bias
    valid = (jnp.arange(nc)[:, None] + jnp.arange(A_BAND)[None, :] // CHUNK) >= A_LEFT_CHUNKS
    s = jnp.where(valid[None, :, None, None, :], s, -jnp.inf)
    p = jax.nn.softmax(s, axis=-1)
    o = jnp.einsum('bchqk,bckhd->bcqhd', p.astype(vb.dtype), vb)
    return o.reshape(B, S, A_HEADS, HEAD_DIM)


def band_attn_step(q, k, v, pk, pv, rel_bias, qpos):
    W = pk.shape[1]
    kk = jnp.concatenate([pk, k], axis=1)
    vv = jnp.concatenate([pv, v], axis=1)
    kpos = jnp.concatenate([qpos[0] - W + jnp.arange(W, dtype=qpos.dtype), qpos])
    bias = rel_bias_lookup(rel_bias, qpos[:, None] - kpos[None, :])
    s = jnp.einsum('bqhd,bkhd->bhqk', q, kk).astype(jnp.float32) * HEAD_DIM ** -0.5 + bias
    p = jax.nn.softmax(s, axis=-1)
    return jnp.einsum('bhqk,bkhd->bqhd', p.astype(vv.dtype), vv)


def dsa_block(qpos_b, q_b, iq_b, iw_b, k, v, ik, kpos, topk):
    isc = jax.nn.relu(jnp.einsum('bqhe,bke->bqhk', iq_b, ik).astype(jnp.float32))
    score = jnp.einsum('bqh,bqhk->bqk', iw_b.astype(jnp.float32), isc)
    adm = (kpos // CHUNK)[None, :] <= (qpos_b // CHUNK)[:, None]
    score = jnp.where(adm[None], score, -jnp.inf)
    top_val, top_idx = lax.top_k(score, topk)
    gather = jax.vmap(lambda rows, idx: rows[idx])
    k_sel = gather(k, top_idx)
    v_sel = gather(v, top_idx)
    s = jnp.einsum('bqhd,bqkd->bqhk', q_b, k_sel).astype(jnp.float32) * HEAD_DIM ** -0.5
    s = jnp.where(jnp.isfinite(top_val)[:, :, None, :], s, -jnp.inf)
    p = jax.nn.softmax(s, axis=-1)
    return jnp.einsum('bqhk,bqkd->bqhd', p.astype(v_sel.dtype), v_sel)


def diff_attn_block(qpos_b, q_b, k, v, kpos, lam):
    s = jnp.einsum('bqhid,bkhid->bhiqk', q_b, k).astype(jnp.float32) * HEAD_DIM ** -0.5
    adm = (kpos // CHUNK)[None, :] <= (qpos_b // CHUNK)[:, None]
    s = jnp.where(adm, s, -jnp.inf)
    p = jax.nn.softmax(s, axis=-1)
    a = p[:, :, 0] - lam * p[:, :, 1]
    return jnp.einsum('bhqk,bkhe->bqhe', a.astype(v.dtype), v)


def over_query_blocks(fn, qpos, *qs):
    T = qpos.shape[0]
    if T <= QBLOCK:
        return fn(qpos, *qs)
    nb = T // QBLOCK

    def to_blocks(a):
        return jnp.moveaxis(a.reshape((a.shape[0], nb, QBLOCK) + a.shape[2:]), 1, 0)

    out = lax.map(lambda args: fn(*args), (qpos.reshape(nb, QBLOCK),) + tuple(to_blocks(a) for a in qs))
    out = jnp.moveaxis(out, 0, 1)
    return out.reshape((out.shape[0], T) + out.shape[3:])


def memory_attn(x, mem_k, mem_v, w_q, w_o):
    B, T, _ = x.shape
    q = (x @ w_q).reshape(B, T, MEM_HEADS, MEM_HEAD_DIM)
    s = jnp.einsum('bqhd,bkhd->bhqk', q, mem_k).astype(jnp.float32) * MEM_HEAD_DIM ** -0.5
    p = jax.nn.softmax(s, axis=-1)
    o = jnp.einsum('bhqk,bkhd->bqhd', p.astype(mem_v.dtype), mem_v)
    return o.reshape(B, T, D_MODEL) @ w_o


def trunk_layer(x, qpos, past, mem_k, mem_v, l, ln_g, ln_b, f1_gu, f1_d, f2_gu, f2_d,
                w_in, rel_bias, lam_p, subln_g, wb_a, wb_b, wb_c, w_out, wm_q, wm_o):
    B, T, _ = x.shape
    f32 = jnp.float32
    x = layer_norm(ALPHA * x + 0.5 * swiglu(x, f1_gu, f1_d), ln_g[0], ln_b[0])
    z = x @ w_in
    (aq, ak, av, bq, bk, bv, iq, ik, iw, cq, ck, cv, gates) = jnp.split(
        z, np.cumsum(IN_SIZES)[:-1].tolist(), axis=-1)
    hs = (B, T, A_HEADS, HEAD_DIM)
    aq, ak, av = aq.reshape(hs), ak.reshape(hs), av.reshape(hs)
    bq = rope(bq.reshape(B, T, B_HEADS, HEAD_DIM), qpos)
    bk = rope(bk, qpos)
    iq = rope(iq.reshape(B, T, IDX_HEADS, IDX_DIM), qpos)
    ik = rope(ik, qpos)
    iw = iw * (IDX_HEADS ** -0.5)
    cq = rope(cq.reshape(B, T, C_HEADS, 2, HEAD_DIM), qpos)
    ck = rope(ck.reshape(B, T, C_HEADS, 2, HEAD_DIM), qpos).reshape(B, T, C_HEADS, 2 * HEAD_DIM)
    cv = cv.reshape(B, T, C_HEADS, 2 * HEAD_DIM)
    if past is None:
        ya = band_attn_prompt(aq, ak, av, rel_bias)
        keep = min(A_WINDOW, T)
        a_new_k, a_new_v = ak[:, T - keep:], av[:, T - keep:]
        kpos = qpos
        b_k_all, b_v_all, i_k_all, c_k_all, c_v_all = bk, bv, ik, ck, cv
    else:
        pa_k, pa_v, pb_k, pb_v, pb_i, pc_k, pc_v = past
        ya = band_attn_step(aq, ak, av, pa_k, pa_v, rel_bias, qpos)
        a_new_k, a_new_v = ak, av
        kpos = jnp.arange(pb_k.shape[1] + T, dtype=qpos.dtype)
        b_k_all = jnp.concatenate([pb_k, bk], axis=1)
        b_v_all = jnp.concatenate([pb_v, bv], axis=1)
        i_k_all = jnp.concatenate([pb_i, ik], axis=1)
        c_k_all = jnp.concatenate([pc_k, ck], axis=1)
        c_v_all = jnp.concatenate([pc_v, cv], axis=1)
    topk = min(TOPK_MAX, kpos.shape[0] // 4)
    yb = over_query_blocks(
        lambda qp, q_, iq_, iw_: dsa_block(qp, q_, iq_, iw_, b_k_all, b_v_all, i_k_all, kpos, topk),
        qpos, bq, iq, iw)
    lam_init = 0.8 - 0.6 * math.exp(-0.3 * l)
    lp = lam_p.astype(f32)
    lam = jnp.exp(jnp.sum(lp[0] * lp[1])) - jnp.exp(jnp.sum(lp[2] * lp[3])) + lam_init
    c_k5 = c_k_all.reshape(c_k_all.shape[:3] + (2, HEAD_DIM))
    yc = over_query_blocks(lambda qp, q_: diff_attn_block(qp, q_, c_k5, c_v_all, kpos, lam), qpos, cq)
    ycf = yc.astype(f32)
    yc = (ycf * lax.rsqrt(jnp.mean(ycf * ycf, axis=-1, keepdims=True) + SUBLN_EPS)
          * subln_g.astype(f32) * (1.0 - lam_init)).astype(x.dtype)
    g_a, g_b, g_c = jnp.split(jax.nn.sigmoid(gates.astype(f32)).astype(x.dtype), N_BRANCH, axis=-1)
    merged = (g_a * (ya.reshape(B, T, A_WIDTH) @ wb_a)
              + g_b * (yb.reshape(B, T, B_WIDTH) @ wb_b)
              + g_c * (yc.reshape(B, T, C_WIDTH) @ wb_c))
    x = layer_norm(ALPHA * x + merged @ w_out, ln_g[1], ln_b[1])
    x = layer_norm(ALPHA * x + memory_attn(x, mem_k, mem_v, wm_q, wm_o), ln_g[2], ln_b[2])
    x = layer_norm(ALPHA * x + 0.5 * swiglu(x, f2_gu, f2_d), ln_g[3], ln_b[3])
    return x, (a_new_k, a_new_v, bk, bv, ik, ck, cv)


def setup_inputs(seed: int = 0) -> dict:
    key = jax.random.key(seed)
    ks = iter(jax.random.split(key, 40))

    def nrm(shape, scale):
        return scale * jax.random.normal(next(ks), shape, jnp.float32)

    win = min(A_WINDOW, PAST_LEN)
    D = D_MODEL
    return {
        'x_prompt': nrm((BATCH, SEQ, D), 1.0),
        'x_sample': nrm((DEC_BATCH, DEC_SEQ, D), 1.0),
        'cache_a_k': nrm((DEPTH, DEC_BATCH, win, A_HEADS, HEAD_DIM), 1.0),
        'cache_a_v': nrm((DEPTH, DEC_BATCH, win, A_HEADS, HEAD_DIM), 1.0),
        'cache_b_k': nrm((DEPTH, DEC_BATCH, PAST_LEN, HEAD_DIM), 1.0),
        'cache_b_v': nrm((DEPTH, DEC_BATCH, PAST_LEN, HEAD_DIM), 1.0),
        'cache_b_idx': nrm((DEPTH, DEC_BATCH, PAST_LEN, IDX_DIM), 1.0),
        'cache_c_k': nrm((DEPTH, DEC_BATCH, PAST_LEN, C_HEADS, 2 * HEAD_DIM), 1.0),
        'cache_c_v': nrm((DEPTH, DEC_BATCH, PAST_LEN, C_HEADS, 2 * HEAD_DIM), 1.0),
        'cache_mem_k': nrm((DEPTH, DEC_BATCH, N_MEM, MEM_HEADS, MEM_HEAD_DIM), 1.0),
        'cache_mem_v': nrm((DEPTH, DEC_BATCH, N_MEM, MEM_HEADS, MEM_HEAD_DIM), 1.0),
        'mem_prompt': nrm((BATCH, N_MEM, D), 1.0),
        'ln_g': 1.0 + nrm((DEPTH, 4, D), 0.02),
        'ln_b': nrm((DEPTH, 4, D), 0.02),
        'ffn1_w_gu': nrm((DEPTH, D, 2 * D_FF), D ** -0.5),
        'ffn1_w_d': nrm((DEPTH, D_FF, D), BETA * D_FF ** -0.5),
        'ffn2_w_gu': nrm((DEPTH, D, 2 * D_FF), D ** -0.5),
        'ffn2_w_d': nrm((DEPTH, D_FF, D), BETA * D_FF ** -0.5),
        'w_in': nrm((DEPTH, D, IN_COLS), D ** -0.5),
        'a_rel_bias': nrm((DEPTH, A_HEADS, 2 * REL_CLIP + 1), 0.2),
        'c_lambda': nrm((DEPTH, 4, HEAD_DIM), 0.1),
        'c_subln_g': 1.0 + nrm((DEPTH, 2 * HEAD_DIM), 0.02),
        'w_branch_a': nrm((DEPTH, A_WIDTH, D), A_WIDTH ** -0.5),
        'w_branch_b': nrm((DEPTH, B_WIDTH, D), B_WIDTH ** -0.5),
        'w_branch_c': nrm((DEPTH, C_WIDTH, D), C_WIDTH ** -0.5),
        'w_out': nrm((DEPTH, D, D), BETA * D ** -0.5),
        'w_mem_q': nrm((DEPTH, D, D), D ** -0.5),
        'w_mem_k': nrm((DEPTH, D, D), D ** -0.5),
        'w_mem_v': nrm((DEPTH, D, D), D ** -0.5),
        'w_mem_o': nrm((DEPTH, D, D), BETA * D ** -0.5),
    }


def reference(x_prompt, x_sample, cache_a_k, cache_a_v, cache_b_k, cache_b_v, cache_b_idx,
              cache_c_k, cache_c_v, cache_mem_k, cache_mem_v, mem_prompt,
              ln_g, ln_b, ffn1_w_gu, ffn1_w_d, ffn2_w_gu, ffn2_w_d, w_in, a_rel_bias,
              c_lambda, c_subln_g, w_branch_a, w_branch_b, w_branch_c, w_out,
              w_mem_q, w_mem_k, w_mem_v, w_mem_o):
    Bp, S, _ = x_prompt.shape
    T = x_sample.shape[1]
    P = cache_b_k.shape[2]
    pos_p = jnp.arange(S, dtype=jnp.int32)
    pos_s = P + jnp.arange(T, dtype=jnp.int32)
    yp, ys = x_prompt, x_sample
    st_p, st_s, mk_p, mv_p = [], [], [], []
    for l in range(DEPTH):
        lw = (ln_g[l], ln_b[l], ffn1_w_gu[l], ffn1_w_d[l], ffn2_w_gu[l], ffn2_w_d[l], w_in[l],
              a_rel_bias[l], c_lambda[l], c_subln_g[l], w_branch_a[l], w_branch_b[l], w_branch_c[l],
              w_out[l], w_mem_q[l], w_mem_o[l])
        mk = (mem_prompt @ w_mem_k[l]).reshape(Bp, N_MEM, MEM_HEADS, MEM_HEAD_DIM)
        mv = (mem_prompt @ w_mem_v[l]).reshape(Bp, N_MEM, MEM_HEADS, MEM_HEAD_DIM)
        yp, sp = trunk_layer(yp, pos_p, None, mk, mv, l, *lw)
        past = (cache_a_k[l], cache_a_v[l], cache_b_k[l], cache_b_v[l], cache_b_idx[l],
                cache_c_k[l], cache_c_v[l])
        ys, ss = trunk_layer(ys, pos_s, past, cache_mem_k[l], cache_mem_v[l], l, *lw)
        st_p.append(sp)
        st_s.append(ss)
        mk_p.append(mk)
        mv_p.append(mv)
    (new_a_k_p, new_a_v_p, new_b_k_p, new_b_v_p, new_b_idx_p, new_c_k_p, new_c_v_p) = [
        jnp.stack([s[i] for s in st_p]) for i in range(7)]
    (new_a_k_s, new_a_v_s, new_b_k_s, new_b_v_s, new_b_idx_s, new_c_k_s, new_c_v_s) = [
        jnp.stack([s[i] for s in st_s]) for i in range(7)]
    new_mem_k_p = jnp.stack(mk_p)
    new_mem_v_p = jnp.stack(mv_p)
    return (yp, ys, new_a_k_p, new_a_v_p, new_b_k_p, new_b_v_p, new_b_idx_p, new_c_k_p, new_c_v_p,
            new_mem_k_p, new_mem_v_p, new_a_k_s, new_a_v_s, new_b_k_s, new_b_v_s, new_b_idx_s,
            new_c_k_s, new_c_v_s)
```

```python
import contextlib
import math
import numpy as np
import concourse.bass as bass
import concourse.mybir as mybir
from concourse.bass_utils import run_bass_kernel_spmd

F32 = mybir.dt.float32
BF16 = mybir.dt.bfloat16
AF = mybir.ActivationFunctionType
ALU = mybir.AluOpType

NCORES = 8
D = 1024
SEQ = 2048
TS = 64
PAST = 2048
DEPTH = 2
DFF = 2816
NFC = DFF // 128
INC = 6344
ALPHA = (2.0 * DEPTH) ** 0.25
LN_EPS = 1e-5
NEG = -30000.0
cAq, cAk, cAv, cBq, cBk, cBv, cIq, cIk, cIw, cCq, cCk, cCv, cG = (
    0, 256, 512, 768, 1024, 1088, 1152, 1664, 1728, 1736, 2248, 2760, 3272)
ZBLK = [(0, 512), (512, 512), (1024, 512), (1536, 200), (1736, 512), (2248, 512), (2760, 512)]

ENGS = ("pe", "act", "dve", "pool", "sp")
NDMASEM = 8


def C(method, *args, **kw):
    return lambda e: getattr(e, method)(*args, **kw)


class Ins:
    __slots__ = ("eng", "idx", "fn", "dma", "deps", "sig", "slot", "val", "waits")

    def __init__(self, eng, idx, fn, dma):
        self.eng, self.idx, self.fn, self.dma = eng, idx, fn, dma
        self.deps = ()
        self.sig = False
        self.slot = 0
        self.val = 0
        self.waits = ()


class Trk:
    def __init__(self):
        self.streams = {e: [] for e in ENGS}
        self.wr = {}
        self.rd = {}
        self.ndma = {e: 0 for e in ENGS}
        self.pending = {e: [] for e in ENGS}
        self.lastc = {e: None for e in ENGS}
        self.lastd = {e: {} for e in ENGS}

    def barrier(self):
        deps = []
        for e in ENGS:
            deps += list(self.lastd[e].values())
            if self.lastc[e] is not None:
                deps.append(self.lastc[e])
        for e in ENGS:
            self.pending[e] = list(deps)

    def add(self, eng, fn, reads=(), writes=(), dma=False):
        st = self.streams[eng]
        ins = Ins(eng, len(st), fn, dma)
        deps = {}
        for d in self.pending[eng]:
            deps[id(d)] = d
        self.pending[eng] = []
        for k in reads:
            w = self.wr.get(k)
            if w is not None:
                deps[id(w)] = w
        for k in writes:
            w = self.wr.get(k)
            if w is not None:
                deps[id(w)] = w
            for r in self.rd.get(k, ()):
                deps[id(r)] = r
        for k in reads:
            self.rd.setdefault(k, []).append(ins)
        for k in writes:
            self.wr[k] = ins
            self.rd[k] = []
        if dma:
            j = self.ndma[eng]
            self.ndma[eng] = j + 1
            ins.slot = j % NDMASEM
            ins.val = 16 * (j // NDMASEM + 1)
            self.lastd[eng][ins.slot] = ins
        else:
            self.lastc[eng] = ins
        ds = []
        for d in deps.values():
            if d is ins:
                continue
            if (not dma) and (not d.dma) and d.eng == "pe" and eng == "pe":
                continue
            ds.append(d)
        ins.deps = ds
        st.append(ins)
        return ins

    def finalize(self):
        for e in ENGS:
            seen = {}
            seen_dma = set()
            prev = {}
            for ins in self.streams[e]:
                waits = []
                if ins.dma:
                    p = prev.get(ins.slot)
                    if p is not None and id(p) not in seen_dma:
                        waits.append(p)
                        seen_dma.add(id(p))
                    prev[ins.slot] = ins
                for d in ins.deps:
                    if d.dma:
                        if id(d) in seen_dma:
                            continue
                        seen_dma.add(id(d))
                        waits.append(d)
                    else:
                        if seen.get(d.eng, -1) >= d.idx:
                            continue
                        seen[d.eng] = d.idx
                        d.sig = True
                        waits.append(d)
                ins.waits = waits
        for e in ENGS:
            c = 0
            for ins in self.streams[e]:
                if not ins.dma and ins.sig:
                    c += 1
                    ins.val = c

    def emit(self, nc):
        self.finalize()
        final_waits = []
        for e in ENGS:
            last = {}
            for ins in self.streams[e]:
                if ins.dma:
                    last[ins.slot] = ins
            final_waits += list(last.values())
        with contextlib.ExitStack() as es:
            csem = {e: es.enter_context(nc.semaphore("c_" + e)) for e in ENGS}
            dsem = {e: [es.enter_context(nc.semaphore("d_%s%d" % (e, i))) for i in range(NDMASEM)]
                    for e in ENGS if self.ndma[e] > 0}
            block = es.enter_context(nc.Block())

            def run(e, eng):
                for ins in self.streams[e]:
                    for d in ins.waits:
                        if d.dma:
                            eng.wait_ge(dsem[d.eng][d.slot], d.val)
                        else:
                            eng.wait_ge(csem[d.eng], d.val)
                    r = ins.fn(eng)
                    if ins.dma:
                        r.then_inc(dsem[e][ins.slot], 16)
                    elif ins.sig:
                        r.then_inc(csem[e], 1)
                if e == "sp":
                    for d in final_waits:
                        eng.wait_ge(dsem[d.eng][d.slot], d.val)

            block.tensor(lambda eng: run("pe", eng))
            block.scalar(lambda eng: run("act", eng))
            block.vector(lambda eng: run("dve", eng))
            block.gpsimd(lambda eng: run("pool", eng))
            block.sync(lambda eng: run("sp", eng))


def build_program(units, n_layers=DEPTH, stop=None):
    nc = bass.Bass("TRN2", target_bir_lowering=False)
    T = Trk()
    es = contextlib.ExitStack()

    def din(name, shape):
        return nc.dram_tensor(name, list(shape), F32, kind="ExternalInput").ap()

    def dout(name, shape):
        return nc.dram_tensor(name, list(shape), F32, kind="ExternalOutput").ap()

    NB = 4
    I = {}
    I["x_prompt"] = din("x_prompt", (NB, SEQ, D))
    I["x_sample"] = din("x_sample", (NB, TS, D))
    I["cache_a_k"] = din("cache_a_k", (DEPTH, NB, 512, 256))
    I["cache_a_v"] = din("cache_a_v", (DEPTH, NB, 512, 256))
    I["cache_b_k"] = din("cache_b_k", (DEPTH, NB, PAST, 64))
    I["cache_b_v"] = din("cache_b_v", (DEPTH, NB, PAST, 64))
    I["cache_b_idx"] = din("cache_b_idx", (DEPTH, NB, PAST, 64))
    I["cache_c_k"] = din("cache_c_k", (DEPTH, NB, PAST, 512))
    I["cache_c_v"] = din("cache_c_v", (DEPTH, NB, PAST, 512))
    I["cache_mem_k"] = din("cache_mem_k", (DEPTH, NB, 256, 1024))
    I["cache_mem_v"] = din("cache_mem_v", (DEPTH, NB, 256, 1024))
    I["mem_prompt"] = din("mem_prompt", (NB, 256, D))
    I["ln_g"] = din("ln_g", (DEPTH, 4, D))
    I["ln_b"] = din("ln_b", (DEPTH, 4, D))
    I["ffn1_w_gu"] = din("ffn1_w_gu", (DEPTH, D, 2 * DFF))
    I["ffn1_w_d"] = din("ffn1_w_d", (DEPTH, DFF, D))
    I["ffn2_w_gu"] = din("ffn2_w_gu", (DEPTH, D, 2 * DFF))
    I["ffn2_w_d"] = din("ffn2_w_d", (DEPTH, DFF, D))
    I["w_in"] = din("w_in", (DEPTH, D, INC))
    I["biasA"] = din("biasA", (DEPTH, 128, 5 * 4 * 128))
    I["c_lambda"] = din("c_lambda", (DEPTH, 256))
    I["c_subln_g"] = din("c_subln_g", (DEPTH, 128))
    I["w_branch_a"] = din("w_branch_a", (DEPTH, 256, D))
    I["w_branch_b"] = din("w_branch_b", (DEPTH, 256, D))
    I["w_branch_c"] = din("w_branch_c", (DEPTH, 512, D))
    I["w_out"] = din("w_out", (DEPTH, D, D))
    I["w_mem_q"] = din("w_mem_q", (DEPTH, D, D))
    I["w_mem_k"] = din("w_mem_k", (DEPTH, D, D))
    I["w_mem_v"] = din("w_mem_v", (DEPTH, D, D))
    I["w_mem_o"] = din("w_mem_o", (DEPTH, D, D))
    I["rope_cs"] = din("rope_cs", (17 * 128, 16))

    O = {}
    O["y_p"] = dout("y_p", (NB, SEQ, D))
    O["y_s"] = dout("y_s", (NB, TS, D))
    O["a_k_p"] = dout("a_k_p", (DEPTH, NB, 512, 256))
    O["a_v_p"] = dout("a_v_p", (DEPTH, NB, 512, 256))
    O["b_k_p"] = dout("b_k_p", (DEPTH, NB, SEQ, 64))
    O["b_v_p"] = dout("b_v_p", (DEPTH, NB, SEQ, 64))
    O["b_i_p"] = dout("b_i_p", (DEPTH, NB, SEQ, 64))
    O["c_k_p"] = dout("c_k_p", (DEPTH, NB, SEQ, 512))
    O["c_v_p"] = dout("c_v_p", (DEPTH, NB, SEQ, 512))
    O["m_k_p"] = dout("m_k_p", (DEPTH, NB, 256, 1024))
    O["m_v_p"] = dout("m_v_p", (DEPTH, NB, 256, 1024))
    O["a_k_s"] = dout("a_k_s", (DEPTH, NB, TS, 256))
    O["a_v_s"] = dout("a_v_s", (DEPTH, NB, TS, 256))
    O["b_k_s"] = dout("b_k_s", (DEPTH, NB, TS, 64))
    O["b_v_s"] = dout("b_v_s", (DEPTH, NB, TS, 64))
    O["b_i_s"] = dout("b_i_s", (DEPTH, NB, TS, 64))
    O["c_k_s"] = dout("c_k_s", (DEPTH, NB, TS, 512))
    O["c_v_s"] = dout("c_v_s", (DEPTH, NB, TS, 512))
    xmid = nc.dram_tensor("xmid", [SEQ, D], F32, kind="Internal").ap()

    def sb(name, shape, dt=F32):
        return es.enter_context(nc.sbuf_tensor(name, list(shape), dt))

    NKT = 17
    akT2 = sb("akT2", [128, 2, NKT * 128], BF16)
    av_aug = sb("av_aug", [128, NKT, 4 * 65], BF16)
    bkT2 = sb("bkT2", [128, NKT * 128], BF16)
    bv_aug = sb("bv_aug", [128, NKT, 65], BF16)
    ikT2 = sb("ikT2", [128, NKT * 128], BF16)
    ckT2 = sb("ckT2", [128, 4, NKT * 128], BF16)
    cv_aug = sb("cv_aug", [128, NKT, 4 * 129], BF16)
    mkT = sb("mkT", [128, 8, 256], BF16)
    mv_aug = sb("mv_aug", [128, 2, 4 * 257], BF16)
    x_tm = sb("x_tm", [128, 4, D], F32)
    xT = sb("xT", [128, 8, 512], BF16)
    yT = sb("yT", [128, 8, 512], BF16)
    wslot = [sb("wslot%d" % i, [128, 4096], BF16) for i in range(3)]
    lng = sb("lng", [128, D], F32)
    lnb = sb("lnb", [128, D], F32)
    biasA = sb("biasA_sb", [128, 5 * 4 * 128], BF16)
    ropet = sb("ropet", [128, 17, 16], F32)
    ident = sb("ident", [128, 128], BF16)
    idf = sb("idf", [128, 128], F32)
    gsub = sb("gsub", [128, 128], F32)
    lamt = sb("lamt", [128, 256], F32)
    lams = sb("lams", [128, 8], F32)
    arena = sb("arena", [128, 62000], mybir.dt.uint8)
    small = sb("small", [128, 64], F32)
    stat = sb("stat", [128, 16], F32)

    def carve(off, shape, dt):
        n = int(np.prod(shape[1:])) * (2 if dt == BF16 else 4)
        v = arena[:, off:off + n].bitcast(dt)
        if len(shape) == 3:
            v = v.rearrange("p (a b) -> p a b", b=shape[2])
        return v, off + n

    xn, o = carve(0, [128, D], F32)
    xbf, o = carve(o, [128, D], BF16)
    base = o
    hT, o = carve(base, [128, NFC, 512], BF16)
    sgt = []
    for i in range(2):
        v, o = carve(o, [128, 512], F32)
        sgt.append(v)
    ffn_end = o
    o2 = base
    zst = []
    for i in range(2):
        v, o2 = carve(o2, [128, 512], F32)
        zst.append(v)
    z16 = []
    for i in range(2):
        v, o2 = carve(o2, [128, 512], BF16)
        z16.append(v)
    rtmp, o2 = carve(o2, [128, 4, 128], F32)
    aqT2, o2 = carve(o2, [128, 2, 512], BF16)
    bqT2, o2 = carve(o2, [128, 2, 512], BF16)
    iqT2, o2 = carve(o2, [128, 4, 512], BF16)
    cqT2, o2 = carve(o2, [128, 4, 512], BF16)
    score, o2 = carve(o2, [128, NKT * 128], F32)
    sel, o2 = carve(o2, [128, NKT * 128], BF16)
    relu_t = []
    for i in range(2):
        v, o2 = carve(o2, [128, 512], F32)
        relu_t.append(v)
    PT = []
    for i in range(2):
        v, o2 = carve(o2, [128, 512], BF16)
        PT.append(v)
    tmpA, o2 = carve(o2, [128, 512], F32)
    ycf, o2 = carve(o2, [128, 512], F32)
    yct, o2 = carve(o2, [128, 128], F32)
    y16, o2 = carve(o2, [128, 512], BF16)
    iw_sb, o2 = carve(o2, [128, 4, 8], F32)
    selT = []
    for i in range(2):
        v, o2 = carve(o2, [128, 128], BF16)
        selT.append(v)
    attn_end = o2
    o3 = base
    mT, o3 = carve(o3, [128, 8, 512], BF16)
    sgm = []
    for i in range(3):
        v, o3 = carve(o3, [128, 512], BF16)
        sgm.append(v)
    mt1, o3 = carve(o3, [128, 512], F32)
    mt2, o3 = carve(o3, [128, 512], F32)
    qmT, o3 = carve(o3, [128, 8, 512], BF16)
    om16, o3 = carve(o3, [128, D], BF16)
    assert o3 <= 37632 + base, o3
    assert max(ffn_end, attn_end, o3) <= 62000, (ffn_end, attn_end, o3)

    ps = [es.enter_context(nc.psum_tensor("ps%d" % i, [128, 512], F32)) for i in range(6)]
    pt = [es.enter_context(nc.psum_tensor("pt%d" % i, [128, 1024], BF16)) for i in range(2)]

    def PS(i):
        return ("ps", i)

    def PTK(i):
        return ("pt", i)

    wctr = [0]

    def wload(parts):
        s = wctr[0] % 3
        wctr[0] += 1
        wt = wslot[s]
        for dstf, src in parts:
            T.add("pool", C("dma_start", out=dstf(wt), in_=src), writes=[("w", s)], dma=True)
        return wt, ("w", s)

    def w3(wt, k, c):
        return wt[:, 0:k * c].rearrange("p (k c) -> p k c", c=c)

    psrot = [0]

    def nextps(lo, hi):
        i = lo + psrot[0] % (hi - lo)
        psrot[0] += 1
        return i

    ptrot = [0]

    def nextpt():
        i = ptrot[0] % 2
        ptrot[0] += 1
        return i

    def transposes(src16, src_key, R, blocks, dst_fn, dst_keys, eng="act"):
        b = nextpt()
        for i, (c0, ncol) in enumerate(blocks):
            T.add("pe", C("transpose",
                pt[b][0:ncol, i * 128:i * 128 + R], src16[0:R, c0:c0 + ncol], ident[0:R, 0:R]),
                reads=[src_key, "ident"], writes=[PTK(b)])
        return b

    def evac(eng, out_ap, in_ap, reads, writes):
        if eng == "act":
            T.add("act", C("copy", out_ap, in_ap), reads=reads, writes=writes)
        else:
            T.add(eng, C("tensor_copy", out_ap, in_ap), reads=reads, writes=writes)

    T.add("pool", C("memset", idf[:], 0.0), writes=["idf"])
    T.add("pool", C("affine_select", out=idf[:], in_=idf[:], compare_op=ALU.not_equal, fill=1.0,
                                            base=0, pattern=[[-1, 128]], channel_multiplier=1),
          reads=["idf"], writes=["idf"])
    T.add("dve", C("tensor_copy", ident[:], idf[:]), reads=["idf"], writes=["ident"])
    T.add("sp", C("dma_start", out=ropet[:], in_=I["rope_cs"].rearrange("(t p) c -> p t c", p=128)),
          writes=["ropet"], dma=True)
    T.add("pool", C("memset", av_aug[:], 1.0), writes=[("av", k) for k in range(NKT)])
    T.add("pool", C("memset", bv_aug[:], 1.0), writes=[("bv", k) for k in range(NKT)])
    T.add("pool", C("memset", cv_aug[:], 1.0), writes=[("cv", k) for k in range(NKT)])
    T.add("pool", C("memset", mv_aug[:], 1.0), writes=[("mv", k) for k in range(2)])

    def layer_consts(l):
        lam_init = 0.8 - 0.6 * math.exp(-0.3 * l)
        T.add("pool", C("dma_start", out=biasA[:], in_=I["biasA"][l]), writes=["biasA"], dma=True)
        T.add("sp", C("dma_start", out=lamt[:], in_=I["c_lambda"][l].partition_broadcast(128)),
              writes=["lamt"], dma=True)
        T.add("sp", C("dma_start", out=gsub[:], in_=I["c_subln_g"][l].partition_broadcast(128)),
              writes=["gsub"], dma=True)
        T.add("dve", C("tensor_tensor", out=lamt[:, 0:64], in0=lamt[:, 0:64], in1=lamt[:, 64:128], op=ALU.mult),
              reads=["lamt"], writes=["lamt"])
        T.add("dve", C("tensor_tensor", out=lamt[:, 128:192], in0=lamt[:, 128:192], in1=lamt[:, 192:256], op=ALU.mult),
              reads=["lamt"], writes=["lamt"])
        T.add("dve", C("reduce_sum", out=lams[:, 0:1], in_=lamt[:, 0:64], axis=mybir.AxisListType.X),
              reads=["lamt"], writes=["lams"])
        T.add("dve", C("reduce_sum", out=lams[:, 1:2], in_=lamt[:, 128:192], axis=mybir.AxisListType.X),
              reads=["lamt"], writes=["lams"])
        T.add("act", C("activation", out=lams[:, 2:4], in_=lams[:, 0:2], func=AF.Exp),
              reads=["lams"], writes=["lams"])
        T.add("dve", C("tensor_tensor", out=lams[:, 4:5], in0=lams[:, 3:4], in1=lams[:, 2:3], op=ALU.subtract),
              reads=["lams"], writes=["lams"])
        T.add("dve", C("tensor_scalar", out=lams[:, 4:5], in0=lams[:, 4:5], scalar1=-lam_init, scalar2=None,
                                               op0=ALU.add), reads=["lams"], writes=["lams"])
        T.add("dve", C("tensor_scalar", out=gsub[:], in0=gsub[:], scalar1=(1.0 - lam_init), scalar2=None,
                                               op0=ALU.mult), reads=["gsub"], writes=["gsub"])

    def layer_norm(l, idx, tt, R, dst_kind, dst_ap):
        xk = ("x_tm", tt)
        src = x_tm[0:R, tt, :]
        T.add("dve", C("bn_stats", stat[0:R, 0:6], src[:, 0:512]), reads=[xk], writes=["stat"])
        T.add("dve", C("bn_stats", stat[0:R, 6:12], src[:, 512:1024]), reads=[xk], writes=["stat"])
        T.add("dve", C("bn_aggr", stat[0:R, 12:14], stat[0:R, 0:12]), reads=["stat"], writes=["stat"])
        T.add("dve", C("tensor_scalar", out=stat[0:R, 14:15], in0=stat[0:R, 13:14], scalar1=LN_EPS, scalar2=None,
                                               op0=ALU.add), reads=["stat"], writes=["stat"])
        T.add("act", C("activation", out=stat[0:R, 14:15], in_=stat[0:R, 14:15], func=AF.Sqrt),
              reads=["stat"], writes=["stat"])
        T.add("dve", C("reciprocal", out=stat[0:R, 15:16], in_=stat[0:R, 14:15]), reads=["stat"], writes=["stat"])
        T.add("dve", C("tensor_scalar", out=xn[0:R, :], in0=src, scalar1=stat[0:R, 12:13], scalar2=stat[0:R, 15:16],
                                               op0=ALU.subtract, op1=ALU.mult), reads=[xk, "stat"], writes=["xn"])
        T.add("pool", C("tensor_tensor", out=xn[0:R, :], in0=xn[0:R, :], in1=lng[0:R, :], op=ALU.mult),
              reads=["xn", "lng"], writes=["xn"])
        T.add("pool", C("tensor_tensor", out=xn[0:R, :], in0=xn[0:R, :], in1=lnb[0:R, :], op=ALU.add),
              reads=["xn", "lnb"], writes=["xn"])
        T.add("act", C("mul", x_tm[0:R, tt, :], xn[0:R, :], ALPHA), reads=["xn"], writes=[xk])
        T.add("dve", C("tensor_copy", xbf[0:R, :], xn[0:R, :]), reads=["xn"], writes=["xbf"])
        if dst_kind == "dram":
            T.add("sp", C("dma_start", out=dst_ap[0], in_=xn[0:R, :]), reads=["xn"], writes=[dst_ap[1]], dma=True)
        b = transposes(xbf, "xbf", R, [(k * 128, 128) for k in range(8)], None, None)
        evac("act", xT[:, :, tt * R:(tt + 1) * R],
             pt[b][:, :].rearrange("p (k c) -> p k c", c=128)[:, :, 0:R],
             reads=[PTK(b)], writes=[("xT", tt)])

    def load_ln(l, idx):
        T.add("sp", C("dma_start", out=lng[:], in_=I["ln_g"][l, idx].partition_broadcast(128)),
              writes=["lng"], dma=True)
        T.add("sp", C("dma_start", out=lnb[:], in_=I["ln_b"][l, idx].partition_broadcast(128)),
              writes=["lnb"], dma=True)

    def ffn(l, which, nt, R, ln_idx, final_dst):
        N = nt * R
        wgu = I["ffn%d_w_gu" % which][l]
        wd = I["ffn%d_w_d" % which][l]
        load_ln(l, ln_idx)
        for j in range(NFC):
            wt, wk = wload([
                (lambda w: w3(w, 8, 256)[:, :, 0:128], wgu[:, j * 128:(j + 1) * 128].rearrange("(k p) c -> p k c", p=128)),
                (lambda w: w3(w, 8, 256)[:, :, 128:256],
                 wgu[:, DFF + j * 128:DFF + (j + 1) * 128].rearrange("(k p) c -> p k c", p=128)),
            ])
            wv = w3(wt, 8, 256)
            st_ = j % 2
            gb, ub = 2 * st_, 2 * st_ + 1
            for k in range(8):
                T.add("pe", C("matmul", ps[gb][:, 0:N], wv[:, k, 0:128], xT[:, k, 0:N], start=(k == 0), stop=(k == 7)),
                      reads=[wk] + [("xT", t) for t in range(nt)], writes=[PS(gb)])
            for k in range(8):
                T.add("pe", C("matmul", ps[ub][:, 0:N], wv[:, k, 128:256], xT[:, k, 0:N], start=(k == 0), stop=(k == 7)),
                      reads=[wk] + [("xT", t) for t in range(nt)], writes=[PS(ub)])
            T.add("act", C("activation", out=sgt[st_][:, 0:N], in_=ps[gb][:, 0:N], func=AF.Silu),
                  reads=[PS(gb)], writes=[("sgt", st_)])
            T.add("dve", C("tensor_tensor", out=hT[:, j, 0:N], in0=sgt[st_][:, 0:N], in1=ps[ub][:, 0:N], op=ALU.mult),
                  reads=[("sgt", st_), PS(ub)], writes=[("hT", j)])
        groups = [(0, 8), (8, 8), (16, 6)]
        for half in range(2):
            for (j0, nj) in groups:
                wt, wk = wload([(lambda w, nj=nj: w3(w, nj, 512),
                                 wd[j0 * 128:(j0 + nj) * 128, half * 512:(half + 1) * 512].rearrange("(j p) c -> p j c", p=128))])
                wv = w3(wt, nj, 512)
                for jj in range(nj):
                    j = j0 + jj
                    for tt in range(nt):
                        bk = 2 + tt
                        T.add("pe", C("matmul",
                            ps[bk][0:R, :], hT[:, j, tt * R:(tt + 1) * R], wv[:, jj, :], start=(j == 0), stop=(j == NFC - 1)),
                            reads=[wk, ("hT", j)], writes=[PS(bk)])
            for tt in range(nt):
                bk = 2 + tt
                T.add("dve", C("scalar_tensor_tensor",
                    out=x_tm[0:R, tt, half * 512:(half + 1) * 512], in0=ps[bk][0:R, :], scalar=0.5,
                    in1=x_tm[0:R, tt, half * 512:(half + 1) * 512], op0=ALU.mult, op1=ALU.add),
                    reads=[PS(bk), ("x_tm", tt)], writes=[("x_tm", tt)])
        for tt in range(nt):
            if final_dst is not None:
                layer_norm(l, ln_idx, tt, R, "dram", final_dst(tt))
            else:
                layer_norm(l, ln_idx, tt, R, None, None)

    def rope(stage, skey, R, c0, nh, gt):
        import os
        if os.environ.get("KDBG_NOROPE"):
            return
        v = stage[0:R, c0:c0 + nh * 64].rearrange("p (h d) -> p h d", d=64)
        x1 = v[:, :, 0:8]
        x2 = v[:, :, 8:16]
        cos = ropet[0:R, gt, 0:8].unsqueeze(1).to_broadcast([R, nh, 8])
        sin = ropet[0:R, gt, 8:16].unsqueeze(1).to_broadcast([R, nh, 8])
        t = [rtmp[0:R, i, 0:nh * 8].rearrange("p (h d) -> p h d", d=8) for i in range(4)]
        T.add("dve", C("tensor_tensor", out=t[0], in0=x1, in1=cos, op=ALU.mult), reads=[skey, "ropet"], writes=["rtmp"])
        T.add("dve", C("tensor_tensor", out=t[1], in0=x2, in1=sin, op=ALU.mult), reads=[skey, "ropet"], writes=["rtmp"])
        T.add("dve", C("tensor_tensor", out=t[2], in0=x2, in1=cos, op=ALU.mult), reads=[skey, "ropet"], writes=["rtmp"])
        T.add("dve", C("tensor_tensor", out=t[3], in0=x1, in1=sin, op=ALU.mult), reads=[skey, "ropet"], writes=["rtmp"])
        T.add("dve", C("tensor_tensor", out=x1, in0=t[0], in1=t[1], op=ALU.subtract), reads=["rtmp"], writes=[skey])
        T.add("dve", C("tensor_tensor", out=x2, in0=t[2], in1=t[3], op=ALU.add), reads=["rtmp"], writes=[skey])

    def win_phase(l, kind, b, tb, nt, R):
        win = I["w_in"][l]
        sfx = "_p" if kind == "p" else "_s"
        zr = [0]
        import os
        for bi, (c0, ncol) in enumerate(ZBLK):
            if bi > int(os.environ.get("KDBG_WIN", "99")):
                break
            wt, wk = wload([(lambda w, ncol=ncol: w3(w, 8, ncol),
                             win[:, c0:c0 + ncol].rearrange("(k p) c -> p k c", p=128))])
            wv = w3(wt, 8, ncol)
            wmode = os.environ.get("KDBG_WINMODE", "all")
            if wmode == "load":
                T.add("dve", C("tensor_copy", small[:, 0:8], wv[:, 0, 0:8]), reads=[wk], writes=["small"])
                continue
            for tt in range(nt):
                gt = tb * 4 + tt if kind == "p" else 16
                at = gt if kind == "p" else 4
                tok0 = gt * 128 if kind == "p" else 0
                pb = nextps(0, 2)
                for k in range(8):
                    T.add("pe", C("matmul", ps[pb][0:R, 0:ncol], xT[:, k, tt * R:(tt + 1) * R], wv[:, k, 0:ncol],
                                                                    start=(k == 0), stop=(k == 7)),
                          reads=[wk, ("xT", tt)], writes=[PS(pb)])
                si = zr[0] % 2
                zr[0] += 1
                st_, sk = zst[si], ("zst", si)
                z6, zk = z16[si], ("z16", si)
                T.add("act", C("copy", st_[0:R, 0:ncol], ps[pb][0:R, 0:ncol]),
                      reads=[PS(pb)], writes=[sk])
                if wmode == "mm":
                    continue

                def outd(name, col, n, rows=None, st_=st_, sk=sk):
                    dst = O[name + sfx][l, b, tok0:tok0 + R, :] if rows is None else rows
                    T.add("sp", C("dma_start", out=dst, in_=st_[0:R, col:col + n]), reads=[sk], dma=True)

                def cast(dc, sc, n, st_=st_, sk=sk, z6=z6, zk=zk, eng="dve"):
                    T.add(eng, C("tensor_copy", z6[0:R, dc:dc + n], st_[0:R, sc:sc + n]), reads=[sk], writes=[zk])

                ksl = slice(gt * 128, gt * 128 + R)
                asl = slice(at * 128, at * 128 + R)
                qsl = slice(tt * R, (tt + 1) * R)
                if bi == 0:
                    if kind == "s":
                        outd("a_k", 256, 256)
                    elif gt >= 12:
                        outd("a_k", 256, 256, rows=O["a_k_p"][l, b, (gt - 12) * 128:(gt - 11) * 128, :])
                    cast(0, 0, 512)
                    pb2 = transposes(z6, zk, R, [(0, 128), (128, 128), (256, 128), (384, 128)], None, None)
                    pv = pt[pb2][:, :].rearrange("p (k c) -> p k c", c=128)
                    evac("act", aqT2[:, :, qsl], pv[:, 0:2, 0:R], [PTK(pb2)], [("aqT", tt)])
                    evac("act", akT2[:, :, asl], pv[:, 2:4, 0:R], [PTK(pb2)], [("akT", at)])
                elif bi == 1:
                    if kind == "s":
                        outd("a_v", 0, 256)
                    elif gt >= 12:
                        outd("a_v", 0, 256, rows=O["a_v_p"][l, b, (gt - 12) * 128:(gt - 11) * 128, :])
                    T.add("dve", C("tensor_copy",
                        av_aug[0:R, at, :].rearrange("p (h d) -> p h d", d=65)[:, :, 0:64],
                        st_[0:R, 0:256].rearrange("p (h d) -> p h d", d=64)), reads=[sk], writes=[("av", at)])
                    rope(st_, sk, R, 256, 4, gt)
                    cast(0, 256, 256)
                    pb2 = transposes(z6, zk, R, [(0, 128), (128, 128)], None, None)
                    pv = pt[pb2][:, :].rearrange("p (k c) -> p k c", c=128)
                    evac("act", bqT2[:, :, qsl], pv[:, 0:2, 0:R], [PTK(pb2)], [("bqT", tt)])
                elif bi == 2:
                    rope(st_, sk, R, 0, 1, gt)
                    rope(st_, sk, R, 128, 6, gt)
                    outd("b_k", 0, 64)
                    outd("b_v", 64, 64)
                    T.add("dve", C("tensor_copy", bv_aug[0:R, gt, 0:64], st_[0:R, 64:128]),
                          reads=[sk], writes=[("bv", gt)])
                    cast(0, 0, 64)
                    cast(64, 0, 64)
                    cast(128, 128, 384)
                    pb2 = transposes(z6, zk, R, [(0, 128), (128, 128), (256, 128), (384, 128)], None, None)
                    pv = pt[pb2][:, :].rearrange("p (k c) -> p k c", c=128)
                    evac("act", bkT2[:, ksl], pt[pb2][:, 0:R], [PTK(pb2)], [("bkT", gt)])
                    evac("act", iqT2[:, 0:3, qsl], pv[:, 1:4, 0:R], [PTK(pb2)], [("iqT", tt)])
                elif bi == 3:
                    rope(st_, sk, R, 0, 3, gt)
                    outd("b_i", 128, 64)
                    T.add("act", C("mul", iw_sb[0:R, tt, :], st_[0:R, 192:200], 8.0 ** -0.5),
                          reads=[sk], writes=[("iw", tt)])
                    cast(0, 0, 192)
                    cast(192, 128, 64)
                    pb2 = transposes(z6, zk, R, [(0, 128), (128, 128)], None, None)
                    evac("act", iqT2[:, 3, qsl], pt[pb2][:, 0:R], [PTK(pb2)], [("iqT", tt)])
                    evac("act", ikT2[:, ksl], pt[pb2][:, 128:128 + R], [PTK(pb2)], [("ikT", gt)])
                elif bi == 4:
                    rope(st_, sk, R, 0, 8, gt)
                    cast(0, 0, 512)
                    pb2 = transposes(z6, zk, R, [(0, 128), (128, 128), (256, 128), (384, 128)], None, None)
                    pv = pt[pb2][:, :].rearrange("p (k c) -> p k c", c=128)
                    evac("act", cqT2[:, :, qsl], pv[:, 0:4, 0:R], [PTK(pb2)], [("cqT", tt)])
                elif bi == 5:
                    rope(st_, sk, R, 0, 8, gt)
                    outd("c_k", 0, 512)
                    cast(0, 0, 512)
                    pb2 = transposes(z6, zk, R, [(0, 128), (128, 128), (256, 128), (384, 128)], None, None)
                    pv = pt[pb2][:, :].rearrange("p (k c) -> p k c", c=128)
                    evac("act", ckT2[:, :, ksl], pv[:, 0:4, 0:R], [PTK(pb2)], [("ckT", gt)])
                else:
                    outd("c_v", 0, 512)
                    T.add("dve", C("tensor_copy",
                        cv_aug[0:R, gt, :].rearrange("p (h d) -> p h d", d=129)[:, :, 0:128],
                        st_[0:R, 0:512].rearrange("p (h d) -> p h d", d=128)), reads=[sk], writes=[("cv", gt)])

    def ingest_cache(l, b):
        for kt in range(16):
            rows = slice(kt * 128, (kt + 1) * 128)
            ksl = rows
            zi = kt % 2
            z6, zk = z16[zi], ("z16", zi)
            T.add("pool", C("dma_start", out=z6[:, 0:512], in_=I["cache_c_k"][l, b, rows, :]), writes=[zk], dma=True)
            pb2 = transposes(z6, zk, 128, [(0, 128), (128, 128), (256, 128), (384, 128)], None, None)
            pv = pt[pb2][:, :].rearrange("p (k c) -> p k c", c=128)
            evac("act", ckT2[:, :, ksl], pv[:, 0:4, :], [PTK(pb2)], [("ckT", kt)])
            T.add("pool", C("dma_start",
                out=cv_aug[:, kt, :].rearrange("p (h d) -> p h d", d=129)[:, :, 0:128],
                in_=I["cache_c_v"][l, b, rows, :].rearrange("p (h d) -> p h d", d=128)), writes=[("cv", kt)], dma=True)
            T.add("pool", C("dma_start", out=bv_aug[:, kt, 0:64], in_=I["cache_b_v"][l, b, rows, :]),
                  writes=[("bv", kt)], dma=True)
            zi2 = (kt + 1) % 2
            z6b, zkb = z16[zi2], ("z16", zi2)
            for c in range(2):
                T.add("pool", C("dma_start", out=z6b[:, c * 64:(c + 1) * 64], in_=I["cache_b_k"][l, b, rows, :]),
                      writes=[zkb], dma=True)
                T.add("pool", C("dma_start", out=z6b[:, 128 + c * 64:128 + (c + 1) * 64],
                                                                 in_=I["cache_b_idx"][l, b, rows, :]), writes=[zkb], dma=True)
            pb3 = transposes(z6b, zkb, 128, [(0, 128), (128, 128)], None, None)
            evac("act", bkT2[:, ksl], pt[pb3][:, 0:128], [PTK(pb3)], [("bkT", kt)])
            evac("act", ikT2[:, ksl], pt[pb3][:, 128:256], [PTK(pb3)], [("ikT", kt)])
        for kt in range(4):
            rows = slice(kt * 128, (kt + 1) * 128)
            zi = kt % 2
            z6, zk = z16[zi], ("z16", zi)
            T.add("pool", C("dma_start", out=z6[:, 0:256], in_=I["cache_a_k"][l, b, rows, :]), writes=[zk], dma=True)
            pb2 = transposes(z6, zk, 128, [(0, 128), (128, 128)], None, None)
            pv = pt[pb2][:, :].rearrange("p (k c) -> p k c", c=128)
            evac("act", akT2[:, :, rows], pv[:, 0:2, :], [PTK(pb2)], [("akT", kt)])
            T.add("pool", C("dma_start",
                out=av_aug[:, kt, :].rearrange("p (h d) -> p h d", d=65)[:, :, 0:64],
                in_=I["cache_a_v"][l, b, rows, :].rearrange("p (h d) -> p h d", d=64)), writes=[("av", kt)], dma=True)

    def mem_prep(l, kind, b):
        if kind == "s":
            for kt in range(2):
                rows = slice(kt * 128, (kt + 1) * 128)
                T.add("pool", C("dma_start", out=om16[:, :], in_=I["cache_mem_k"][l, b, rows, :]), writes=["om16"], dma=True)
                pb2 = transposes(om16, "om16", 128, [(k * 128, 128) for k in range(8)], None, None)
                evac("act", mkT[:, :, rows], pt[pb2][:, :].rearrange("p (k c) -> p k c", c=128), [PTK(pb2)], [("mkT", kt)])
                T.add("pool", C("dma_start",
                    out=mv_aug[:, kt, :].rearrange("p (h d) -> p h d", d=257)[:, :, 0:256],
                    in_=I["cache_mem_v"][l, b, rows, :].rearrange("p (h d) -> p h d", d=256)), writes=[("mv", kt)], dma=True)
            return
        for kt in range(2):
            rows = slice(kt * 128, (kt + 1) * 128)
            T.add("pool", C("dma_start", out=om16[:, :], in_=I["mem_prompt"][b, rows, :]), writes=["om16"], dma=True)
            pb2 = transposes(om16, "om16", 128, [(k * 128, 128) for k in range(8)], None, None)
            evac("act", qmT[:, :, rows], pt[pb2][:, :].rearrange("p (k c) -> p k c", c=128), [PTK(pb2)], [("qmT", kt)])
        for wi, wname in enumerate(("w_mem_k", "w_mem_v")):
            for half in range(2):
                wt, wk = wload([(lambda w: w3(w, 8, 512),
                                 I[wname][l][:, half * 512:(half + 1) * 512].rearrange("(k p) c -> p k c", p=128))])
                wv = w3(wt, 8, 512)
                for kt in range(2):
                    rows = slice(kt * 128, (kt + 1) * 128)
                    pb = nextps(0, 2)
                    for k in range(8):
                        T.add("pe", C("matmul", ps[pb][:, :], qmT[:, k, rows], wv[:, k, :], start=(k == 0), stop=(k == 7)),
                              reads=[wk, ("qmT", kt)], writes=[PS(pb)])
                    T.add("act", C("copy", mt1[:, :], ps[pb][:, :]), reads=[PS(pb)], writes=["mt1"])
                    dst = O["m_k_p" if wi == 0 else "m_v_p"][l, b, rows, half * 512:(half + 1) * 512]
                    T.add("sp", C("dma_start", out=dst, in_=mt1[:, :]), reads=["mt1"], dma=True)
                    if wi == 0:
                        T.add("dve", C("tensor_copy", y16[:, :], mt1[:, :]), reads=["mt1"], writes=["y16"])
                        pb2 = transposes(y16, "y16", 128, [(k * 128, 128) for k in range(4)], None, None)
                        evac("act", mkT[:, half * 4:half * 4 + 4, rows],
                             pt[pb2][:, 0:512].rearrange("p (k c) -> p k c", c=128), [PTK(pb2)], [("mkT", kt)])
                    else:
                        T.add("dve", C("tensor_copy",
                            mv_aug[:, kt, :].rearrange("p (h d) -> p h d", d=257)[:, 2 * half:2 * half + 2, 0:256],
                            mt1[:, :].rearrange("p (h d) -> p h d", d=256)), reads=["mt1"], writes=[("mv", kt)])

    def norm_heads(acc, acckey, R, nh, dh, out16, okey, cols):
        av = acc[0:R, 0:nh * (dh + 1)].rearrange("p (h d) -> p h d", d=dh + 1)
        T.add("dve", C("reciprocal", out=small[0:R, 0:nh], in_=av[:, :, dh]), reads=[acckey], writes=["small"])
        for s in range(nh):
            T.add("dve", C("tensor_scalar", out=out16[0:R, cols[s]:cols[s] + dh], in0=av[:, s, 0:dh],
                                                        scalar1=small[0:R, s:s + 1], scalar2=None, op0=ALU.mult),
                  reads=[acckey, "small"], writes=[okey])

    def attn_phase(l, kind, tb, nt, R):
        N = nt * R
        prompt = kind == "p"
        nkt_blk = (tb * 4 + nt) if prompt else 17
        for h in range(4):
            started = set()
            for kt in range(nkt_blk):
                ksz = 128 if (prompt or kt < 16) else 64
                tt0 = max(0, kt - tb * 4) if prompt else 0
                cs0 = tt0 * R
                for i in range(2):
                    sbk = nextps(0, 2)
                    pti = sbk
                    T.add("pe", C("matmul",
                        ps[sbk][0:ksz, cs0:N], ckT2[i * 64:(i + 1) * 64, h, kt * 128:kt * 128 + ksz],
                        cqT2[i * 64:(i + 1) * 64, h, cs0:N], start=True, stop=True),
                        reads=[("ckT", kt)] + [("cqT", t) for t in range(tt0, nt)], writes=[PS(sbk)])
                    T.add("act", C("activation", out=PT[pti][0:ksz, cs0:N], in_=ps[sbk][0:ksz, cs0:N],
                                                                          func=AF.Exp, scale=0.125),
                          reads=[PS(sbk)], writes=[("PT", pti)])
                    if prompt and kt >= tb * 4:
                        T.add("pool", C("memset", PT[pti][64:128, cs0:cs0 + 64], 0.0),
                              reads=[], writes=[("PT", pti)])
                    for tt in range(tt0, nt):
                        a = tt * 2 + i
                        bk, col = 2 + a // 3, (a % 3) * 129
                        last = (tb * 4 + tt) if prompt else 16
                        first = bk not in started
                        started.add(bk)
                        T.add("pe", C("matmul",
                            ps[bk][0:R, col:col + 129], PT[pti][0:ksz, tt * R:(tt + 1) * R],
                            cv_aug[0:ksz, kt, h * 129:(h + 1) * 129], start=first, stop=first, skip_group_check=(not first)),
                            reads=[("PT", pti), ("cv", kt)], writes=[PS(bk)])
            for tt in range(nt):
                a0, a1 = tt * 2, tt * 2 + 1
                b0, c0 = 2 + a0 // 3, (a0 % 3) * 129
                b1, c1 = 2 + a1 // 3, (a1 % 3) * 129
                T.add("dve", C("reciprocal", out=small[0:R, 0:1], in_=ps[b0][0:R, c0 + 128:c0 + 129]),
                      reads=[PS(b0)], writes=["small"])
                T.add("dve", C("reciprocal", out=small[0:R, 1:2], in_=ps[b1][0:R, c1 + 128:c1 + 129]),
                      reads=[PS(b1)], writes=["small"])
                T.add("dve", C("tensor_tensor", out=small[0:R, 1:2], in0=small[0:R, 1:2], in1=lams[0:R, 4:5], op=ALU.mult),
                      reads=["small", "lams"], writes=["small"])
                T.add("dve", C("tensor_scalar", out=yct[0:R, :], in0=ps[b0][0:R, c0:c0 + 128], scalar1=small[0:R, 0:1],
                                                       scalar2=None, op0=ALU.mult), reads=[PS(b0), "small"], writes=["yct"])
                T.add("dve", C("scalar_tensor_tensor", out=ycf[0:R, 0:128], in0=ps[b1][0:R, c1:c1 + 128],
                                                                     scalar=small[0:R, 1:2], in1=yct[0:R, :],
                                                                     op0=ALU.mult, op1=ALU.add),
                      reads=[PS(b1), "small", "yct"], writes=["ycf"])
                T.add("act", C("activation", out=yct[0:R, :], in_=ycf[0:R, 0:128], func=AF.Square,
                                                    accum_out=small[0:R, 2:3]), reads=["ycf"], writes=["yct", "small"])
                T.add("dve", C("tensor_scalar", out=small[0:R, 2:3], in0=small[0:R, 2:3], scalar1=1.0 / 128.0,
                                                       scalar2=1e-5, op0=ALU.mult, op1=ALU.add), reads=["small"], writes=["small"])
                T.add("act", C("activation", out=small[0:R, 2:3], in_=small[0:R, 2:3], func=AF.Sqrt),
                      reads=["small"], writes=["small"])
                T.add("dve", C("reciprocal", out=small[0:R, 3:4], in_=small[0:R, 2:3]), reads=["small"], writes=["small"])
                T.add("dve", C("scalar_tensor_tensor", out=y16[0:R, 0:128], in0=ycf[0:R, 0:128],
                                                                     scalar=small[0:R, 3:4], in1=gsub[0:R, :],
                                                                     op0=ALU.mult, op1=ALU.mult),
                      reads=["ycf", "small", "gsub"], writes=["y16"])
                pb2 = transposes(y16, "y16", R, [(0, 128)], None, None)
                evac("act", yT[:, 4 + h, tt * R:(tt + 1) * R], pt[pb2][:, 0:R], [PTK(pb2)], [("yT", tt)])

        import os
        amode = os.environ.get("KDBG_ATT", "all")
        if amode == "C":
            return
        for tt in range(nt):
            gt = tb * 4 + tt if prompt else 16
            nkt = gt + 1 if prompt else 17
            Wk = (gt + 1) * 128 if prompt else PAST + TS
            qsl = slice(tt * R, (tt + 1) * R)
            nkb = (Wk + 511) // 512
            for kb in range(nkb):
                w = min(512, Wk - kb * 512)
                ksl = slice(kb * 512, kb * 512 + w)
                for ih in range(8):
                    hf, c = ih % 2, ih // 2
                    pb = nextps(0, 2)
                    T.add("pe", C("matmul", ps[pb][0:R, 0:w], iqT2[hf * 64:(hf + 1) * 64, c, qsl],
                                                                      ikT2[hf * 64:(hf + 1) * 64, ksl], start=True, stop=True),
                          reads=[("iqT", tt)] + [("ikT", t) for t in range(kb * 4, min(nkt, kb * 4 + 4))], writes=[PS(pb)])
                    ri = pb
                    T.add("act", C("activation", out=relu_t[ri][0:R, 0:w], in_=ps[pb][0:R, 0:w], func=AF.Relu),
                          reads=[PS(pb)], writes=[("relu", ri)])
                    if ih == 0:
                        T.add("dve", C("tensor_scalar", out=score[0:R, ksl], in0=relu_t[ri][0:R, 0:w],
                                                                      scalar1=iw_sb[0:R, tt, 0:1], scalar2=None, op0=ALU.mult),
                              reads=[("relu", ri), ("iw", tt)], writes=["score"])
                    else:
                        T.add("dve", C("scalar_tensor_tensor",
                            out=score[0:R, ksl], in0=relu_t[ri][0:R, 0:w], scalar=iw_sb[0:R, tt, ih:ih + 1],
                            in1=score[0:R, ksl], op0=ALU.mult, op1=ALU.add),
                            reads=[("relu", ri), ("iw", tt), "score"], writes=["score"])
            if prompt:
                T.add("dve", C("memset", score[0:64, Wk - 64:Wk], -1e30), reads=[], writes=["score"])
            if amode == "CB1":
                continue
            if Wk > 256:
                for r in range(32):
                    T.add("dve", C("max", out=small[0:R, 8:16], in_=score[0:R, 0:Wk]), reads=["score"], writes=["small8"])
                    T.add("dve", C("match_replace", out=score[0:R, 0:Wk], in_to_replace=small[0:R, 8:16],
                                                           in_values=score[0:R, 0:Wk], imm_value=-1e30),
                          reads=["score", "small8"], writes=["score"])
                T.add("dve", C("tensor_single_scalar", out=sel[0:R, 0:Wk], in_=score[0:R, 0:Wk], scalar=-1e29, op=ALU.is_lt),
                      reads=["score"], writes=["sel"])
                if prompt:
                    T.add("dve", C("memset", sel[0:64, Wk - 64:Wk], 0.0), reads=[], writes=["sel"])
            else:
                T.add("dve", C("tensor_single_scalar", out=sel[0:R, 0:Wk], in_=score[0:R, 0:Wk], scalar=-1e29, op=ALU.is_gt),
                      reads=["score"], writes=["sel"])
            if amode == "CB2":
                continue
            accB = 5
            for kt in range(nkt):
                ksz = 128 if (prompt or kt < 16) else 64
                kcs = slice(kt * 128, kt * 128 + ksz)
                pbm = transposes(sel, "sel", R, [(kt * 128, ksz)], None, None)
                evac("act", selT[pbm][0:ksz, 0:R], pt[pbm][0:ksz, 0:R], [PTK(pbm)], [("selT", pbm)])
                bmode = int(os.environ.get("KDBG_B", "9"))
                if bmode <= 1:
                    continue
                for hf in range(2):
                    for c in range(2):
                        T.add("pe", C("matmul", ps[hf][0:ksz, c * R:(c + 1) * R],
                                      bkT2[hf * 64:(hf + 1) * 64, kcs],
                                      bqT2[hf * 64:(hf + 1) * 64, c, qsl], start=True, stop=True),
                              reads=[("bkT", kt), ("bqT", tt)], writes=[PS(hf)])
                pti = kt % 2
                for hf in range(2):
                    T.add("act", C("activation", out=PT[pti][0:ksz, hf * 2 * R:(hf + 1) * 2 * R], in_=ps[hf][0:ksz, 0:2 * R],
                                   func=AF.Exp, scale=0.125), reads=[PS(hf)], writes=[("PT", pti)])
                if bmode <= 2:
                    continue
                T.add("dve", C("tensor_tensor",
                    out=PT[pti][0:ksz, 0:4 * R].rearrange("p (h q) -> p h q", q=R),
                    in0=PT[pti][0:ksz, 0:4 * R].rearrange("p (h q) -> p h q", q=R),
                    in1=selT[pbm][0:ksz, 0:R].unsqueeze(1).to_broadcast([ksz, 4, R]), op=ALU.mult),
                    reads=[("PT", pti), ("selT", pbm)], writes=[("PT", pti)])
                if bmode <= 3:
                    continue
                for s in range(4):
                    T.add("pe", C("matmul", ps[accB][0:R, s * 65:(s + 1) * 65], PT[pti][0:ksz, s * R:(s + 1) * R],
                                                                       bv_aug[0:ksz, kt, :], start=(kt == 0 and s == 0), stop=(kt == 0 and s == 0),
                                                                       skip_group_check=not (kt == 0 and s == 0)),
                          reads=[("PT", pti), ("bv", kt)], writes=[PS(accB)])
            if bmode <= 4:
                continue
            norm_heads(ps[accB], PS(accB), R, 4, 64, y16, "y16", [0, 128, 64, 192])
            pb2 = transposes(y16, "y16", R, [(0, 128), (128, 128)], None, None)
            evac("act", yT[:, 2:4, qsl], pt[pb2][:, 0:256].rearrange("p (k c) -> p k c", c=128)[:, :, 0:R], [PTK(pb2)], [("yT", tt)])
            if amode == "CB3":
                continue
            if prompt:
                akts = [(kt, kt - (gt - 4), 128) for kt in range(max(0, gt - 4), gt + 1)]
            else:
                akts = [(kt, kt, 128 if kt < 4 else 64) for kt in range(5)]
            accA = 5
            for n_, (kt, ti, ksz) in enumerate(akts):
                kcs = slice(kt * 128, kt * 128 + ksz)
                for hf in range(2):
                    for c in range(2):
                        T.add("pe", C("matmul", ps[hf][0:ksz, c * R:(c + 1) * R],
                                      akT2[hf * 64:(hf + 1) * 64, c, kcs],
                                      aqT2[hf * 64:(hf + 1) * 64, c, qsl], start=True, stop=True),
                              reads=[("akT", kt), ("aqT", tt)], writes=[PS(hf)])
                bfull = biasA[0:ksz, ti * 512:(ti + 1) * 512].rearrange("p (c f q) -> p c f q", f=2, q=128)
                for hf in range(2):
                    T.add("dve", C("scalar_tensor_tensor",
                                   out=tmpA[0:ksz, hf * 2 * R:(hf + 1) * 2 * R].rearrange("p (c q) -> p c q", q=R),
                                   in0=ps[hf][0:ksz, 0:2 * R].rearrange("p (c q) -> p c q", q=R), scalar=0.125,
                                   in1=bfull[:, :, hf, 0:R], op0=ALU.mult, op1=ALU.add),
                          reads=[PS(hf), "biasA"], writes=["tmpA"])
                pti = n_ % 2
                T.add("act", C("activation", out=PT[pti][0:ksz, 0:4 * R], in_=tmpA[0:ksz, 0:4 * R], func=AF.Exp),
                      reads=["tmpA"], writes=[("PT", pti)])
                for s_ in range(4):
                    h = [0, 2, 1, 3][s_]
                    T.add("pe", C("matmul", ps[accA][0:R, s_ * 65:(s_ + 1) * 65], PT[pti][0:ksz, s_ * R:(s_ + 1) * R],
                                  av_aug[0:ksz, kt, h * 65:(h + 1) * 65], start=(n_ == 0 and s_ == 0), stop=(n_ == 0 and s_ == 0),
                                  skip_group_check=not (n_ == 0 and s_ == 0)),
                          reads=[("PT", pti), ("av", kt)], writes=[PS(accA)])
            norm_heads(ps[accA], PS(accA), R, 4, 64, y16, "y16", [0, 128, 64, 192])
            pb2 = transposes(y16, "y16", R, [(0, 128), (128, 128)], None, None)
            evac("act", yT[:, 0:2, qsl], pt[pb2][:, 0:256].rearrange("p (k c) -> p k c", c=128)[:, :, 0:R], [PTK(pb2)], [("yT", tt)])

    def tok_proj(l, wname, srcT, srckeys, nt, R, scale):
        W = I[wname][l]
        for half in range(2):
            wt, wk = wload([(lambda w: w3(w, 8, 512), W[:, half * 512:(half + 1) * 512].rearrange("(k p) c -> p k c", p=128))])
            wv = w3(wt, 8, 512)
            for tt in range(nt):
                pb = 2 + (tt % 4)
                for k in range(8):
                    T.add("pe", C("matmul", ps[pb][0:R, :], srcT[:, k, tt * R:(tt + 1) * R], wv[:, k, :],
                                                                    start=(k == 0), stop=(k == 7)),
                          reads=[wk] + srckeys(tt), writes=[PS(pb)])
                T.add("dve", C("scalar_tensor_tensor",
                    out=x_tm[0:R, tt, half * 512:(half + 1) * 512], in0=ps[pb][0:R, :], scalar=scale,
                    in1=x_tm[0:R, tt, half * 512:(half + 1) * 512], op0=ALU.mult, op1=ALU.add),
                    reads=[PS(pb), ("x_tm", tt)], writes=[("x_tm", tt)])

    def merge_phase(l, nt, R):
        N = nt * R
        win = I["w_in"][l]
        load_ln(l, 1)
        for c in range(8):
            dc = slice(c * 128, (c + 1) * 128)
            parts = [(lambda w, g=g: w3(w, 32, 128)[:, g * 8:(g + 1) * 8, :],
                      win[:, cG + g * 1024 + c * 128:cG + g * 1024 + (c + 1) * 128].rearrange("(k p) c -> p k c", p=128))
                     for g in range(3)]
            parts.append((lambda w: w3(w, 32, 128)[:, 24:26, :], I["w_branch_a"][l][:, dc].rearrange("(k p) c -> p k c", p=128)))
            parts.append((lambda w: w3(w, 32, 128)[:, 26:28, :], I["w_branch_b"][l][:, dc].rearrange("(k p) c -> p k c", p=128)))
            parts.append((lambda w: w3(w, 32, 128)[:, 28:32, :], I["w_branch_c"][l][:, dc].rearrange("(k p) c -> p k c", p=128)))
            wgt, wgk = wload(parts)
            wgv = w3(wgt, 32, 128)
            wbk = wgk
            for g in range(3):
                gb = g
                for k in range(8):
                    T.add("pe", C("matmul", ps[gb][:, 0:N], wgv[:, g * 8 + k, :], xT[:, k, 0:N], start=(k == 0), stop=(k == 7)),
                          reads=[wgk] + [("xT", t) for t in range(nt)], writes=[PS(gb)])
                T.add("act", C("activation", out=sgm[g][:, 0:N], in_=ps[gb][:, 0:N], func=AF.Sigmoid),
                      reads=[PS(gb)], writes=[("sgm", g)])
            ks = [(0, 2), (2, 4), (4, 8)]
            for g in range(3):
                bb = 3 + g
                k0, k1 = ks[g]
                for k in range(k0, k1):
                    T.add("pe", C("matmul", ps[bb][:, 0:N], wgv[:, 24 + k, :], yT[:, k, 0:N],
                                                              start=(k == k0), stop=(k == k1 - 1)),
                          reads=[wbk] + [("yT", t) for t in range(nt)], writes=[PS(bb)])
            T.add("dve", C("tensor_tensor", out=mt1[:, 0:N], in0=sgm[0][:, 0:N], in1=ps[3][:, 0:N], op=ALU.mult),
                  reads=[("sgm", 0), PS(3)], writes=["mt1"])
            T.add("dve", C("tensor_tensor", out=mt2[:, 0:N], in0=sgm[1][:, 0:N], in1=ps[4][:, 0:N], op=ALU.mult),
                  reads=[("sgm", 1), PS(4)], writes=["mt2"])
            T.add("pool", C("tensor_tensor", out=mt1[:, 0:N], in0=mt1[:, 0:N], in1=mt2[:, 0:N], op=ALU.add),
                  reads=["mt1", "mt2"], writes=["mt1"])
            T.add("dve", C("tensor_tensor", out=mt2[:, 0:N], in0=sgm[2][:, 0:N], in1=ps[5][:, 0:N], op=ALU.mult),
                  reads=[("sgm", 2), PS(5)], writes=["mt2"])
            T.add("pool", C("tensor_tensor", out=mT[:, c, 0:N], in0=mt1[:, 0:N], in1=mt2[:, 0:N], op=ALU.add),
                  reads=["mt1", "mt2"], writes=[("mT", c)])
        tok_proj(l, "w_out", mT, lambda tt: [("mT", c) for c in range(8)], nt, R, 1.0)
        for tt in range(nt):
            layer_norm(l, 1, tt, R, None, None)

    def mem_phase(l, nt, R):
        N = nt * R
        load_ln(l, 2)
        wq = I["w_mem_q"][l]
        for half in range(2):
            wt, wk = wload([(lambda w: w3(w, 8, 512), wq[:, half * 512:(half + 1) * 512].rearrange("(k p) c -> p k c", p=128))])
            wv = w3(wt, 8, 512)
            for cc in range(4):
                c = half * 4 + cc
                pb = nextps(0, 2)
                for k in range(8):
                    T.add("pe", C("matmul", ps[pb][:, 0:N], wv[:, k, cc * 128:(cc + 1) * 128], xT[:, k, 0:N],
                                                                    start=(k == 0), stop=(k == 7)),
                          reads=[wk] + [("xT", t) for t in range(nt)], writes=[PS(pb)])
                T.add("act", C("copy", qmT[:, c, 0:N], ps[pb][:, 0:N]), reads=[PS(pb)], writes=[("qmT", c)])
        for h in range(4):
            for kt in range(2):
                sbk = nextps(0, 2)
                for c2 in range(2):
                    T.add("pe", C("matmul", ps[sbk][:, 0:N], mkT[:, h * 2 + c2, kt * 128:(kt + 1) * 128],
                                                                   qmT[:, h * 2 + c2, 0:N], start=(c2 == 0), stop=(c2 == 1)),
                          reads=[("mkT", kt), ("qmT", h * 2), ("qmT", h * 2 + 1)], writes=[PS(sbk)])
                pti = sbk
                T.add("act", C("activation", out=PT[pti][:, 0:N], in_=ps[sbk][:, 0:N], func=AF.Exp, scale=1.0 / 16.0),
                      reads=[PS(sbk)], writes=[("PT", pti)])
                for tt in range(nt):
                    bk = 2 + tt
                    T.add("pe", C("matmul", ps[bk][0:R, 0:257], PT[pti][:, tt * R:(tt + 1) * R],
                                                                         mv_aug[:, kt, h * 257:(h + 1) * 257], start=(kt == 0), stop=(kt == 1)),
                          reads=[("PT", pti), ("mv", kt)], writes=[PS(bk)])
            for tt in range(nt):
                bk = 2 + tt
                T.add("dve", C("reciprocal", out=small[0:R, 0:1], in_=ps[bk][0:R, 256:257]), reads=[PS(bk)], writes=["small"])
                T.add("dve", C("tensor_scalar", out=y16[0:R, 0:256], in0=ps[bk][0:R, 0:256], scalar1=small[0:R, 0:1],
                                                              scalar2=None, op0=ALU.mult), reads=[PS(bk), "small"], writes=["y16"])
                pb2 = transposes(y16, "y16", R, [(0, 128), (128, 128)], None, None)
                evac("act", yT[:, 2 * h:2 * h + 2, tt * R:(tt + 1) * R],
                     pt[pb2][:, 0:256].rearrange("p (k c) -> p k c", c=128)[:, :, 0:R], [PTK(pb2)], [("yT", tt)])
        tok_proj(l, "w_mem_o", yT, lambda tt: [("yT", tt)], nt, R, 1.0)
        for tt in range(nt):
            layer_norm(l, 2, tt, R, None, None)

    def load_block(src_rows, nt, R):
        for tt in range(nt):
            sap, skey = src_rows(tt)
            T.add("sp", C("dma_start", out=xn[0:R, :], in_=sap), reads=[skey], writes=["xn"], dma=True)
            T.add("act", C("mul", x_tm[0:R, tt, :], xn[0:R, :], ALPHA), reads=["xn"], writes=[("x_tm", tt)])
            T.add("dve", C("tensor_copy", xbf[0:R, :], xn[0:R, :]), reads=["xn"], writes=["xbf"])
            b = transposes(xbf, "xbf", R, [(k * 128, 128) for k in range(8)], None, None)
            evac("act", xT[:, :, tt * R:(tt + 1) * R], pt[b][:, :].rearrange("p (k c) -> p k c", c=128)[:, :, 0:R],
                 reads=[PTK(b)], writes=[("xT", tt)])

    class _Stop(Exception):
        pass

    phc = {}

    def ph(name):
        phc[name] = phc.get(name, 0) + 1
        if stop is not None:
            sn, _, cnt = stop.partition("@")
            if name == sn and phc[name] >= int(cnt or 1):
                raise _Stop()

    try:
        for (kind, b) in units:
            prompt = kind == "p"
            nblk, nt, R = (4, 4, 128) if prompt else (1, 1, 64)
            for l in range(n_layers):
                layer_consts(l)
                ph("consts")
                T.barrier()
                mem_prep(l, kind, b)
                ph("memprep")
                if not prompt:
                    T.barrier()
                    ingest_cache(l, b)
                    ph("ingest")
                for tb in range(nblk):
                    if l == 0:
                        if prompt:
                            src = lambda tt, tb=tb: (I["x_prompt"][b, (tb * 4 + tt) * 128:(tb * 4 + tt + 1) * 128, :], "xin")
                        else:
                            src = lambda tt: (I["x_sample"][b, :, :], "xin")
                    else:
                        src = lambda tt, tb=tb, R=R: (xmid[(tb * 4 + tt) * 128:(tb * 4 + tt) * 128 + R, :], ("xmid", tb * 4 + tt))
                    load_block(src, nt, R)
                    ph("load")
                    T.barrier()
                    ffn(l, 1, nt, R, 0, None)
                    ph("ffn1")
                    T.barrier()
                    win_phase(l, kind, b, tb, nt, R)
                    ph("win")
                    attn_phase(l, kind, tb, nt, R)
                    ph("attn")
                    T.barrier()
                    merge_phase(l, nt, R)
                    ph("merge")
                    mem_phase(l, nt, R)
                    ph("mem")
                    T.barrier()
                    if l == n_layers - 1:
                        if prompt:
                            fd = lambda tt, tb=tb: (O["y_p"][b, (tb * 4 + tt) * 128:(tb * 4 + tt + 1) * 128, :], "yout")
                        else:
                            fd = lambda tt: (O["y_s"][b, :, :], "yout")
                    else:
                        fd = lambda tt, tb=tb, R=R: (xmid[(tb * 4 + tt) * 128:(tb * 4 + tt) * 128 + R, :], ("xmid", tb * 4 + tt))
                    ffn(l, 2, nt, R, 3, fd)
                    ph("ffn2")
    except _Stop:
        pass
    T.emit(nc)
    es.close()
    return nc


def _bias_table(a_rel_bias):
    L = a_rel_bias.shape[0]
    kl = np.arange(640)[:, None]
    ql = np.arange(128)[None, :]
    idx = np.clip(ql - kl + 512, -128, 128) + 128
    masked = ((ql < 64) & (kl >= 576)) | ((ql >= 64) & (kl < 64))
    tab = a_rel_bias[:, :, idx]
    tab = np.where(masked[None, None], np.float32(NEG), tab).astype(np.float32)
    tab = tab.reshape(L, 4, 5, 128, 128).transpose(0, 3, 2, 1, 4)
    return np.ascontiguousarray(tab.reshape(L, 128, 5 * 4 * 128))


def _rope_table():
    pos = np.arange(17 * 128, dtype=np.float32)
    inv = (500000.0 ** (-np.arange(8, dtype=np.float32) / 8)).astype(np.float32)
    ang = pos[:, None] * inv[None, :]
    return np.concatenate([np.cos(ang), np.sin(ang)], axis=1).astype(np.float32)


_OUT_ORDER = ["y_p", "y_s", "a_k_p", "a_v_p", "b_k_p", "b_v_p", "b_i_p", "c_k_p", "c_v_p", "m_k_p", "m_v_p",
              "a_k_s", "a_v_s", "b_k_s", "b_v_s", "b_i_s", "c_k_s", "c_v_s"]


def make_in_maps(inp):
    f = lambda a: np.ascontiguousarray(np.asarray(a, dtype=np.float32))
    shared = {k: f(inp[k]) for k in ("ln_g", "ln_b", "ffn1_w_gu", "ffn1_w_d", "ffn2_w_gu", "ffn2_w_d", "w_in",
                                     "w_branch_a", "w_branch_b", "w_branch_c", "w_out", "w_mem_q", "w_mem_k",
                                     "w_mem_v", "w_mem_o", "c_subln_g")}
    shared["c_lambda"] = f(inp["c_lambda"]).reshape(DEPTH, 256)
    shared["biasA"] = _bias_table(f(inp["a_rel_bias"]))
    shared["rope_cs"] = _rope_table()
    maps = []
    for c in range(NCORES):
        s = slice(4 * c, 4 * c + 4)
        m = dict(shared)
        m["x_prompt"] = f(inp["x_prompt"][s])
        m["x_sample"] = f(inp["x_sample"][s])
        m["mem_prompt"] = f(inp["mem_prompt"][s])
        m["cache_a_k"] = f(inp["cache_a_k"][:, s]).reshape(DEPTH, 4, 512, 256)
        m["cache_a_v"] = f(inp["cache_a_v"][:, s]).reshape(DEPTH, 4, 512, 256)
        m["cache_b_k"] = f(inp["cache_b_k"][:, s])
        m["cache_b_v"] = f(inp["cache_b_v"][:, s])
        m["cache_b_idx"] = f(inp["cache_b_idx"][:, s])
        m["cache_c_k"] = f(inp["cache_c_k"][:, s]).reshape(DEPTH, 4, PAST, 512)
        m["cache_c_v"] = f(inp["cache_c_v"][:, s]).reshape(DEPTH, 4, PAST, 512)
        m["cache_mem_k"] = f(inp["cache_mem_k"][:, s]).reshape(DEPTH, 4, 256, 1024)
        m["cache_mem_v"] = f(inp["cache_mem_v"][:, s]).reshape(DEPTH, 4, 256, 1024)
        maps.append(m)
    return maps


def assemble(results):
    def cat(name, axis):
        return np.concatenate([r[name] for r in results], axis=axis)
    B = 4 * len(results)
    out = {
        "y_p": cat("y_p", 0), "y_s": cat("y_s", 0),
        "a_k_p": cat("a_k_p", 1).reshape(DEPTH, B, 512, 4, 64), "a_v_p": cat("a_v_p", 1).reshape(DEPTH, B, 512, 4, 64),
        "b_k_p": cat("b_k_p", 1), "b_v_p": cat("b_v_p", 1), "b_i_p": cat("b_i_p", 1),
        "c_k_p": cat("c_k_p", 1).reshape(DEPTH, B, SEQ, 4, 128), "c_v_p": cat("c_v_p", 1).reshape(DEPTH, B, SEQ, 4, 128),
        "m_k_p": cat("m_k_p", 1).reshape(DEPTH, B, 256, 4, 256), "m_v_p": cat("m_v_p", 1).reshape(DEPTH, B, 256, 4, 256),
        "a_k_s": cat("a_k_s", 1).reshape(DEPTH, B, TS, 4, 64), "a_v_s": cat("a_v_s", 1).reshape(DEPTH, B, TS, 4, 64),
        "b_k_s": cat("b_k_s", 1), "b_v_s": cat("b_v_s", 1), "b_i_s": cat("b_i_s", 1),
        "c_k_s": cat("c_k_s", 1).reshape(DEPTH, B, TS, 4, 128), "c_v_s": cat("c_v_s", 1).reshape(DEPTH, B, TS, 4, 128),
    }
    return tuple(np.ascontiguousarray(out[k], dtype=np.float32) for k in _OUT_ORDER)


def kernel(**inputs):
    units = [("p", b) for b in range(4)] + [("s", b) for b in range(4)]
    nc = build_program(units)
    in_maps = make_in_maps(inputs)
    res = run_bass_kernel_spmd(nc, in_maps, core_ids=list(range(NCORES)))
    return assemble(res.results)
```

```python
import contextlib
import math
import numpy as np
import concourse.bass as bass
import concourse.mybir as mybir
from concourse.bass_utils import run_bass_kernel_spmd

F32 = mybir.dt.float32
BF16 = mybir.dt.bfloat16
AF = mybir.ActivationFunctionType
ALU = mybir.AluOpType

NCORES = 8
D = 1024
SEQ = 2048
TS = 64
PAST = 2048
DEPTH = 2
DFF = 2816
NFC = DFF // 128
INC = 6344
ALPHA = (2.0 * DEPTH) ** 0.25
LN_EPS = 1e-5
NEG = -30000.0
cAq, cAk, cAv, cBq, cBk, cBv, cIq, cIk, cIw, cCq, cCk, cCv, cG = (
    0, 256, 512, 768, 1024, 1088, 1152, 1664, 1728, 1736, 2248, 2760, 3272)
ZBLK = [(0, 512), (512, 512), (1024, 512), (1536, 200), (1736, 512), (2248, 512), (2760, 512)]

ENGS = ("pe", "act", "dve", "pool", "sp")
NDMASEM = 8


def C(method, *args, **kw):
    return lambda e: getattr(e, method)(*args, **kw)


class Ins:
    __slots__ = ("eng", "idx", "fn", "dma", "deps", "sig", "slot", "val", "waits")

    def __init__(self, eng, idx, fn, dma):
        self.eng, self.idx, self.fn, self.dma = eng, idx, fn, dma
        self.deps = ()
        self.sig = False
        self.slot = 0
        self.val = 0
        self.waits = ()


class Trk:
    def __init__(self):
        self.streams = {e: [] for e in ENGS}
        self.wr = {}
        self.rd = {}
        self.ndma = {e: 0 for e in ENGS}
        self.pending = {e: [] for e in ENGS}
        self.lastc = {e: None for e in ENGS}
        self.lastd = {e: {} for e in ENGS}

    def barrier(self):
        deps = []
        for e in ENGS:
            deps += list(self.lastd[e].values())
            if self.lastc[e] is not None:
                deps.append(self.lastc[e])
        for e in ENGS:
            self.pending[e] = list(deps)

    def add(self, eng, fn, reads=(), writes=(), dma=False):
        st = self.streams[eng]
        ins = Ins(eng, len(st), fn, dma)
        deps = {}
        for d in self.pending[eng]:
            deps[id(d)] = d
        self.pending[eng] = []
        for k in reads:
            w = self.wr.get(k)
            if w is not None:
                deps[id(w)] = w
        for k in writes:
            w = self.wr.get(k)
            if w is not None:
                deps[id(w)] = w
            for r in self.rd.get(k, ()):
                deps[id(r)] = r
        for k in reads:
            self.rd.setdefault(k, []).append(ins)
        for k in writes:
            self.wr[k] = ins
            self.rd[k] = []
        if dma:
            j = self.ndma[eng]
            self.ndma[eng] = j + 1
            ins.slot = j % NDMASEM
            ins.val = 16 * (j // NDMASEM + 1)
            self.lastd[eng][ins.slot] = ins
        else:
            self.lastc[eng] = ins
        ds = []
        for d in deps.values():
            if d is ins:
                continue
            if (not dma) and (not d.dma) and d.eng == "pe" and eng == "pe":
                continue
            ds.append(d)
        ins.deps = ds
        st.append(ins)
        return ins

    def finalize(self):
        for e in ENGS:
            seen = {}
            seen_dma = set()
            prev = {}
            for ins in self.streams[e]:
                waits = []
                if ins.dma:
                    p = prev.get(ins.slot)
                    if p is not None and id(p) not in seen_dma:
                        waits.append(p)
                        seen_dma.add(id(p))
                    prev[ins.slot] = ins
                for d in ins.deps:
                    if d.dma:
                        if id(d) in seen_dma:
                            continue
                        seen_dma.add(id(d))
                        waits.append(d)
                    else:
                        if seen.get(d.eng, -1) >= d.idx:
                            continue
                        seen[d.eng] = d.idx
                        d.sig = True
                        waits.append(d)
                ins.waits = waits
        for e in ENGS:
            c = 0
            for ins in self.streams[e]:
                if not ins.dma and ins.sig:
                    c += 1
                    ins.val = c

    def emit(self, nc):
        self.finalize()
        final_waits = []
        for e in ENGS:
            last = {}
            for ins in self.streams[e]:
                if ins.dma:
                    last[ins.slot] = ins
            final_waits += list(last.values())
        with contextlib.ExitStack() as es:
            csem = {e: es.enter_context(nc.semaphore("c_" + e)) for e in ENGS}
            dsem = {e: [es.enter_context(nc.semaphore("d_%s%d" % (e, i))) for i in range(NDMASEM)]
                    for e in ENGS if self.ndma[e] > 0}
            block = es.enter_context(nc.Block())

            def run(e, eng):
                for ins in self.streams[e]:
                    for d in ins.waits:
                        if d.dma:
                            eng.wait_ge(dsem[d.eng][d.slot], d.val)
                        else:
                            eng.wait_ge(csem[d.eng], d.val)
                    r = ins.fn(eng)
                    if ins.dma:
                        r.then_inc(dsem[e][ins.slot], 16)
                    elif ins.sig:
                        r.then_inc(csem[e], 1)
                if e == "sp":
                    for d in final_waits:
                        eng.wait_ge(dsem[d.eng][d.slot], d.val)

            block.tensor(lambda eng: run("pe", eng))
            block.scalar(lambda eng: run("act", eng))
            block.vector(lambda eng: run("dve", eng))
            block.gpsimd(lambda eng: run("pool", eng))
            block.sync(lambda eng: run("sp", eng))


def build_program(units, n_layers=DEPTH, stop=None):
    nc = bass.Bass("TRN2", target_bir_lowering=False)
    T = Trk()
    es = contextlib.ExitStack()

    def din(name, shape):
        return nc.dram_tensor(name, list(shape), F32, kind="ExternalInput").ap()

    def dout(name, shape):
        return nc.dram_tensor(name, list(shape), F32, kind="ExternalOutput").ap()

    NB = 4
    I = {}
    I["x_prompt"] = din("x_prompt", (NB, SEQ, D))
    I["x_sample"] = din("x_sample", (NB, TS, D))
    I["cache_a_k"] = din("cache_a_k", (DEPTH, NB, 512, 256))
    I["cache_a_v"] = din("cache_a_v", (DEPTH, NB, 512, 256))
    I["cache_b_k"] = din("cache_b_k", (DEPTH, NB, PAST, 64))
    I["cache_b_v"] = din("cache_b_v", (DEPTH, NB, PAST, 64))
    I["cache_b_idx"] = din("cache_b_idx", (DEPTH, NB, PAST, 64))
    I["cache_c_k"] = din("cache_c_k", (DEPTH, NB, PAST, 512))
    I["cache_c_v"] = din("cache_c_v", (DEPTH, NB, PAST, 512))
    I["cache_mem_k"] = din("cache_mem_k", (DEPTH, NB, 256, 1024))
    I["cache_mem_v"] = din("cache_mem_v", (DEPTH, NB, 256, 1024))
    I["mem_prompt"] = din("mem_prompt", (NB, 256, D))
    I["ln_g"] = din("ln_g", (DEPTH, 4, D))
    I["ln_b"] = din("ln_b", (DEPTH, 4, D))
    I["ffn1_w_gu"] = din("ffn1_w_gu", (DEPTH, D, 2 * DFF))
    I["ffn1_w_d"] = din("ffn1_w_d", (DEPTH, DFF, D))
    I["ffn2_w_gu"] = din("ffn2_w_gu", (DEPTH, D, 2 * DFF))
    I["ffn2_w_d"] = din("ffn2_w_d", (DEPTH, DFF, D))
    I["w_in"] = din("w_in", (DEPTH, D, INC))
    I["biasA"] = din("biasA", (DEPTH, 128, 5 * 4 * 128))
    I["c_lambda"] = din("c_lambda", (DEPTH, 256))
    I["c_subln_g"] = din("c_subln_g", (DEPTH, 128))
    I["w_branch_a"] = din("w_branch_a", (DEPTH, 256, D))
    I["w_branch_b"] = din("w_branch_b", (DEPTH, 256, D))
    I["w_branch_c"] = din("w_branch_c", (DEPTH, 512, D))
    I["w_out"] = din("w_out", (DEPTH, D, D))
    I["w_mem_q"] = din("w_mem_q", (DEPTH, D, D))
    I["w_mem_k"] = din("w_mem_k", (DEPTH, D, D))
    I["w_mem_v"] = din("w_mem_v", (DEPTH, D, D))
    I["w_mem_o"] = din("w_mem_o", (DEPTH, D, D))
    I["rope_cs"] = din("rope_cs", (17 * 128, 16))

    O = {}
    O["y_p"] = dout("y_p", (NB, SEQ, D))
    O["y_s"] = dout("y_s", (NB, TS, D))
    O["a_k_p"] = dout("a_k_p", (DEPTH, NB, 512, 256))
    O["a_v_p"] = dout("a_v_p", (DEPTH, NB, 512, 256))
    O["b_k_p"] = dout("b_k_p", (DEPTH, NB, SEQ, 64))
    O["b_v_p"] = dout("b_v_p", (DEPTH, NB, SEQ, 64))
    O["b_i_p"] = dout("b_i_p", (DEPTH, NB, SEQ, 64))
    O["c_k_p"] = dout("c_k_p", (DEPTH, NB, SEQ, 512))
    O["c_v_p"] = dout("c_v_p", (DEPTH, NB, SEQ, 512))
    O["m_k_p"] = dout("m_k_p", (DEPTH, NB, 256, 1024))
    O["m_v_p"] = dout("m_v_p", (DEPTH, NB, 256, 1024))
    O["a_k_s"] = dout("a_k_s", (DEPTH, NB, TS, 256))
    O["a_v_s"] = dout("a_v_s", (DEPTH, NB, TS, 256))
    O["b_k_s"] = dout("b_k_s", (DEPTH, NB, TS, 64))
    O["b_v_s"] = dout("b_v_s", (DEPTH, NB, TS, 64))
    O["b_i_s"] = dout("b_i_s", (DEPTH, NB, TS, 64))
    O["c_k_s"] = dout("c_k_s", (DEPTH, NB, TS, 512))
    O["c_v_s"] = dout("c_v_s", (DEPTH, NB, TS, 512))
    xmid = nc.dram_tensor("xmid", [SEQ, D], F32, kind="Internal").ap()

    def sb(name, shape, dt=F32):
        return es.enter_context(nc.sbuf_tensor(name, list(shape), dt))

    NKT = 17
    akT2 = sb("akT2", [128, 2, NKT * 128], BF16)
    av_aug = sb("av_aug", [128, NKT, 4 * 65], BF16)
    bkT2 = sb("bkT2", [128, NKT * 128], BF16)
    bv_aug = sb("bv_aug", [128, NKT, 65], BF16)
    ikT2 = sb("ikT2", [128, NKT * 128], BF16)
    ckT2 = sb("ckT2", [128, 4, NKT * 128], BF16)
    cv_aug = sb("cv_aug", [128, NKT, 4 * 129], BF16)
    mkT = sb("mkT", [128, 8, 256], BF16)
    mv_aug = sb("mv_aug", [128, 2, 4 * 257], BF16)
    x_tm = sb("x_tm", [128, 4, D], F32)
    xT = sb("xT", [128, 8, 512], BF16)
    yT = sb("yT", [128, 8, 512], BF16)
    NWS = 4
    wslot = [sb("wslot%d" % i, [128, 4096], BF16) for i in range(NWS)]
    lng = sb("lng", [128, D], F32)
    lnb = sb("lnb", [128, D], F32)
    biasA = sb("biasA_sb", [128, 5 * 4 * 128], BF16)
    ropet = sb("ropet", [128, 17, 16], F32)
    ident = sb("ident", [128, 128], BF16)
    idf = sb("idf", [128, 128], F32)
    gsub = sb("gsub", [128, 128], F32)
    lamt = sb("lamt", [128, 256], F32)
    lams = sb("lams", [128, 8], F32)
    arena = sb("arena", [128, 52224], mybir.dt.uint8)
    small = sb("small", [128, 64], F32)
    stat = sb("stat", [128, 16], F32)

    def carve(off, shape, dt):
        n = int(np.prod(shape[1:])) * (2 if dt == BF16 else 4)
        v = arena[:, off:off + n].bitcast(dt)
        if len(shape) == 3:
            v = v.rearrange("p (a b) -> p a b", b=shape[2])
        return v, off + n

    xn, o = carve(0, [128, D], F32)
    xbf, o = carve(o, [128, D], BF16)
    base = o
    hT, o = carve(base, [128, NFC, 512], BF16)
    sgt = []
    for i in range(2):
        v, o = carve(o, [128, 512], F32)
        sgt.append(v)
    ffn_end = o
    o2 = base
    zst = []
    for i in range(2):
        v, o2 = carve(o2, [128, 512], F32)
        zst.append(v)
    z16 = []
    for i in range(2):
        v, o2 = carve(o2, [128, 512], BF16)
        z16.append(v)
    rtmp, o2 = carve(o2, [128, 4, 128], F32)
    aqT2, o2 = carve(o2, [128, 2, 512], BF16)
    bqT2, o2 = carve(o2, [128, 2, 512], BF16)
    iqT2, o2 = carve(o2, [128, 4, 512], BF16)
    cqT2, o2 = carve(o2, [128, 4, 512], BF16)
    score, o2 = carve(o2, [128, NKT * 128], F32)
    sel, o2 = carve(o2, [128, NKT * 128], BF16)
    relu_t = []
    for i in range(2):
        v, o2 = carve(o2, [128, 512], F32)
        relu_t.append(v)
    PT = []
    for i in range(2):
        v, o2 = carve(o2, [128, 512], BF16)
        PT.append(v)
    tmpA, o2 = carve(o2, [128, 512], F32)
    ycf, o2 = carve(o2, [128, 512], F32)
    yct, o2 = carve(o2, [128, 128], F32)
    y16, o2 = carve(o2, [128, 512], BF16)
    iw_sb, o2 = carve(o2, [128, 4, 8], F32)
    selT = []
    for i in range(2):
        v, o2 = carve(o2, [128, 128], BF16)
        selT.append(v)
    attn_end = o2
    o3 = base
    mT, o3 = carve(o3, [128, 8, 512], BF16)
    sgm = []
    for i in range(3):
        v, o3 = carve(o3, [128, 512], BF16)
        sgm.append(v)
    mt1, o3 = carve(o3, [128, 512], F32)
    mt2, o3 = carve(o3, [128, 512], F32)
    qmT, o3 = carve(o3, [128, 8, 512], BF16)
    om16, o3 = carve(o3, [128, D], BF16)
    assert o3 <= 37632 + base, o3
    assert max(ffn_end, attn_end, o3) <= 52224, (ffn_end, attn_end, o3)

    ps = [es.enter_context(nc.psum_tensor("ps%d" % i, [128, 512], F32)) for i in range(6)]
    pt = [es.enter_context(nc.psum_tensor("pt%d" % i, [128, 1024], BF16)) for i in range(2)]

    def PS(i):
        return ("ps", i)

    def PTK(i):
        return ("pt", i)

    wctr = [0]

    def wload(parts):
        s = wctr[0] % NWS
        wctr[0] += 1
        wt = wslot[s]
        for dstf, src in parts:
            T.add("pool", C("dma_start", out=dstf(wt), in_=src), writes=[("w", s)], dma=True)
        return wt, ("w", s)

    def w3(wt, k, c):
        return wt[:, 0:k * c].rearrange("p (k c) -> p k c", c=c)

    psrot = [0]

    def nextps(lo, hi):
        i = lo + psrot[0] % (hi - lo)
        psrot[0] += 1
        return i

    ptrot = [0]

    def nextpt():
        i = ptrot[0] % 2
        ptrot[0] += 1
        return i

    def transposes(src16, src_key, R, blocks, dst_fn, dst_keys, eng="act"):
        b = nextpt()
        for i, (c0, ncol) in enumerate(blocks):
            T.add("pe", C("transpose",
                pt[b][0:ncol, i * 128:i * 128 + R], src16[0:R, c0:c0 + ncol], ident[0:R, 0:R]),
                reads=[src_key, "ident"], writes=[PTK(b)])
        return b

    def evac(eng, out_ap, in_ap, reads, writes):
        if eng == "act":
            T.add("act", C("copy", out_ap, in_ap), reads=reads, writes=writes)
        else:
            T.add(eng, C("tensor_copy", out_ap, in_ap), reads=reads, writes=writes)

    T.add("pool", C("memset", idf[:], 0.0), writes=["idf"])
    T.add("pool", C("affine_select", out=idf[:], in_=idf[:], compare_op=ALU.not_equal, fill=1.0,
                                            base=0, pattern=[[-1, 128]], channel_multiplier=1),
          reads=["idf"], writes=["idf"])
    T.add("dve", C("tensor_copy", ident[:], idf[:]), reads=["idf"], writes=["ident"])
    T.add("sp", C("dma_start", out=ropet[:], in_=I["rope_cs"].rearrange("(t p) c -> p t c", p=128)),
          writes=["ropet"], dma=True)
    T.add("pool", C("memset", av_aug[:], 1.0), writes=[("av", k) for k in range(NKT)])
    T.add("pool", C("memset", bv_aug[:], 1.0), writes=[("bv", k) for k in range(NKT)])
    T.add("pool", C("memset", cv_aug[:], 1.0), writes=[("cv", k) for k in range(NKT)])
    T.add("pool", C("memset", mv_aug[:], 1.0), writes=[("mv", k) for k in range(2)])

    def layer_consts(l):
        lam_init = 0.8 - 0.6 * math.exp(-0.3 * l)
        T.add("pool", C("dma_start", out=biasA[:], in_=I["biasA"][l]), writes=["biasA"], dma=True)
        T.add("sp", C("dma_start", out=lamt[:], in_=I["c_lambda"][l].partition_broadcast(128)),
              writes=["lamt"], dma=True)
        T.add("sp", C("dma_start", out=gsub[:], in_=I["c_subln_g"][l].partition_broadcast(128)),
              writes=["gsub"], dma=True)
        T.add("dve", C("tensor_tensor", out=lamt[:, 0:64], in0=lamt[:, 0:64], in1=lamt[:, 64:128], op=ALU.mult),
              reads=["lamt"], writes=["lamt"])
        T.add("dve", C("tensor_tensor", out=lamt[:, 128:192], in0=lamt[:, 128:192], in1=lamt[:, 192:256], op=ALU.mult),
              reads=["lamt"], writes=["lamt"])
        T.add("dve", C("reduce_sum", out=lams[:, 0:1], in_=lamt[:, 0:64], axis=mybir.AxisListType.X),
              reads=["lamt"], writes=["lams"])
        T.add("dve", C("reduce_sum", out=lams[:, 1:2], in_=lamt[:, 128:192], axis=mybir.AxisListType.X),
              reads=["lamt"], writes=["lams"])
        T.add("act", C("activation", out=lams[:, 2:4], in_=lams[:, 0:2], func=AF.Exp),
              reads=["lams"], writes=["lams"])
        T.add("dve", C("tensor_tensor", out=lams[:, 4:5], in0=lams[:, 3:4], in1=lams[:, 2:3], op=ALU.subtract),
              reads=["lams"], writes=["lams"])
        T.add("dve", C("tensor_scalar", out=lams[:, 4:5], in0=lams[:, 4:5], scalar1=-lam_init, scalar2=None,
                                               op0=ALU.add), reads=["lams"], writes=["lams"])
        T.add("dve", C("tensor_scalar", out=gsub[:], in0=gsub[:], scalar1=(1.0 - lam_init), scalar2=None,
                                               op0=ALU.mult), reads=["gsub"], writes=["gsub"])

    def layer_norm(l, idx, tt, R, dst_kind, dst_ap):
        xk = ("x_tm", tt)
        src = x_tm[0:R, tt, :]
        T.add("dve", C("bn_stats", stat[0:R, 0:6], src[:, 0:512]), reads=[xk], writes=["stat"])
        T.add("dve", C("bn_stats", stat[0:R, 6:12], src[:, 512:1024]), reads=[xk], writes=["stat"])
        T.add("dve", C("bn_aggr", stat[0:R, 12:14], stat[0:R, 0:12]), reads=["stat"], writes=["stat"])
        T.add("dve", C("tensor_scalar", out=stat[0:R, 14:15], in0=stat[0:R, 13:14], scalar1=LN_EPS, scalar2=None,
                                               op0=ALU.add), reads=["stat"], writes=["stat"])
        T.add("act", C("activation", out=stat[0:R, 14:15], in_=stat[0:R, 14:15], func=AF.Sqrt),
              reads=["stat"], writes=["stat"])
        T.add("dve", C("reciprocal", out=stat[0:R, 15:16], in_=stat[0:R, 14:15]), reads=["stat"], writes=["stat"])
        T.add("dve", C("tensor_scalar", out=xn[0:R, :], in0=src, scalar1=stat[0:R, 12:13], scalar2=stat[0:R, 15:16],
                                               op0=ALU.subtract, op1=ALU.mult), reads=[xk, "stat"], writes=["xn"])
        T.add("pool", C("tensor_tensor", out=xn[0:R, :], in0=xn[0:R, :], in1=lng[0:R, :], op=ALU.mult),
              reads=["xn", "lng"], writes=["xn"])
        T.add("pool", C("tensor_tensor", out=xn[0:R, :], in0=xn[0:R, :], in1=lnb[0:R, :], op=ALU.add),
              reads=["xn", "lnb"], writes=["xn"])
        T.add("act", C("mul", x_tm[0:R, tt, :], xn[0:R, :], ALPHA), reads=["xn"], writes=[xk])
        T.add("dve", C("tensor_copy", xbf[0:R, :], xn[0:R, :]), reads=["xn"], writes=["xbf"])
        if dst_kind == "dram":
            T.add("sp", C("dma_start", out=dst_ap[0], in_=xn[0:R, :]), reads=["xn"], writes=[dst_ap[1]], dma=True)
        b = transposes(xbf, "xbf", R, [(k * 128, 128) for k in range(8)], None, None)
        evac("act", xT[:, :, tt * R:(tt + 1) * R],
             pt[b][:, :].rearrange("p (k c) -> p k c", c=128)[:, :, 0:R],
             reads=[PTK(b)], writes=[("xT", tt)])

    def load_ln(l, idx):
        T.add("sp", C("dma_start", out=lng[:], in_=I["ln_g"][l, idx].partition_broadcast(128)),
              writes=["lng"], dma=True)
        T.add("sp", C("dma_start", out=lnb[:], in_=I["ln_b"][l, idx].partition_broadcast(128)),
              writes=["lnb"], dma=True)

    def ffn(l, which, nt, R, ln_idx, final_dst):
        N = nt * R
        wgu = I["ffn%d_w_gu" % which][l]
        wd = I["ffn%d_w_d" % which][l]
        load_ln(l, ln_idx)
        for j in range(NFC):
            wt, wk = wload([
                (lambda w: w3(w, 8, 256)[:, :, 0:128], wgu[:, j * 128:(j + 1) * 128].rearrange("(k p) c -> p k c", p=128)),
                (lambda w: w3(w, 8, 256)[:, :, 128:256],
                 wgu[:, DFF + j * 128:DFF + (j + 1) * 128].rearrange("(k p) c -> p k c", p=128)),
            ])
            wv = w3(wt, 8, 256)
            st_ = j % 2
            gb, ub = 2 * st_, 2 * st_ + 1
            for k in range(8):
                T.add("pe", C("matmul", ps[gb][:, 0:N], wv[:, k, 0:128], xT[:, k, 0:N], start=(k == 0), stop=(k == 7)),
                      reads=[wk] + [("xT", t) for t in range(nt)], writes=[PS(gb)])
            for k in range(8):
                T.add("pe", C("matmul", ps[ub][:, 0:N], wv[:, k, 128:256], xT[:, k, 0:N], start=(k == 0), stop=(k == 7)),
                      reads=[wk] + [("xT", t) for t in range(nt)], writes=[PS(ub)])
            T.add("act", C("activation", out=sgt[st_][:, 0:N], in_=ps[gb][:, 0:N], func=AF.Silu),
                  reads=[PS(gb)], writes=[("sgt", st_)])
            T.add("dve", C("tensor_tensor", out=hT[:, j, 0:N], in0=sgt[st_][:, 0:N], in1=ps[ub][:, 0:N], op=ALU.mult),
                  reads=[("sgt", st_), PS(ub)], writes=[("hT", j)])
        groups = [(0, 8), (8, 8), (16, 6)]
        for half in range(2):
            for (j0, nj) in groups:
                wt, wk = wload([(lambda w, nj=nj: w3(w, nj, 512),
                                 wd[j0 * 128:(j0 + nj) * 128, half * 512:(half + 1) * 512].rearrange("(j p) c -> p j c", p=128))])
                wv = w3(wt, nj, 512)
                for jj in range(nj):
                    j = j0 + jj
                    for tt in range(nt):
                        bk = 2 + tt
                        T.add("pe", C("matmul",
                            ps[bk][0:R, :], hT[:, j, tt * R:(tt + 1) * R], wv[:, jj, :], start=(j == 0), stop=(j == NFC - 1)),
                            reads=[wk, ("hT", j)], writes=[PS(bk)])
            for tt in range(nt):
                bk = 2 + tt
                T.add("dve", C("scalar_tensor_tensor",
                    out=x_tm[0:R, tt, half * 512:(half + 1) * 512], in0=ps[bk][0:R, :], scalar=0.5,
                    in1=x_tm[0:R, tt, half * 512:(half + 1) * 512], op0=ALU.mult, op1=ALU.add),
                    reads=[PS(bk), ("x_tm", tt)], writes=[("x_tm", tt)])
        for tt in range(nt):
            if final_dst is not None:
                layer_norm(l, ln_idx, tt, R, "dram", final_dst(tt))
            else:
                layer_norm(l, ln_idx, tt, R, None, None)

    def rope(stage, skey, R, c0, nh, gt):
        import os
        if os.environ.get("KDBG_NOROPE"):
            return
        v = stage[0:R, c0:c0 + nh * 64].rearrange("p (h d) -> p h d", d=64)
        x1 = v[:, :, 0:8]
        x2 = v[:, :, 8:16]
        cos = ropet[0:R, gt, 0:8].unsqueeze(1).to_broadcast([R, nh, 8])
        sin = ropet[0:R, gt, 8:16].unsqueeze(1).to_broadcast([R, nh, 8])
        t = [rtmp[0:R, i, 0:nh * 8].rearrange("p (h d) -> p h d", d=8) for i in range(4)]
        T.add("dve", C("tensor_tensor", out=t[0], in0=x1, in1=cos, op=ALU.mult), reads=[skey, "ropet"], writes=["rtmp"])
        T.add("dve", C("tensor_tensor", out=t[1], in0=x2, in1=sin, op=ALU.mult), reads=[skey, "ropet"], writes=["rtmp"])
        T.add("dve", C("tensor_tensor", out=t[2], in0=x2, in1=cos, op=ALU.mult), reads=[skey, "ropet"], writes=["rtmp"])
        T.add("dve", C("tensor_tensor", out=t[3], in0=x1, in1=sin, op=ALU.mult), reads=[skey, "ropet"], writes=["rtmp"])
        T.add("dve", C("tensor_tensor", out=x1, in0=t[0], in1=t[1], op=ALU.subtract), reads=["rtmp"], writes=[skey])
        T.add("dve", C("tensor_tensor", out=x2, in0=t[2], in1=t[3], op=ALU.add), reads=["rtmp"], writes=[skey])

    def win_phase(l, kind, b, tb, nt, R):
        win = I["w_in"][l]
        sfx = "_p" if kind == "p" else "_s"
        zr = [0]
        import os
        for bi, (c0, ncol) in enumerate(ZBLK):
            if bi > int(os.environ.get("KDBG_WIN", "99")):
                break
            wt, wk = wload([(lambda w, ncol=ncol: w3(w, 8, ncol),
                             win[:, c0:c0 + ncol].rearrange("(k p) c -> p k c", p=128))])
            wv = w3(wt, 8, ncol)
            wmode = os.environ.get("KDBG_WINMODE", "all")
            if wmode == "load":
                T.add("dve", C("tensor_copy", small[:, 0:8], wv[:, 0, 0:8]), reads=[wk], writes=["small"])
                continue
            for tt in range(nt):
                gt = tb * 4 + tt if kind == "p" else 16
                at = gt if kind == "p" else 4
                tok0 = gt * 128 if kind == "p" else 0
                pb = nextps(0, 2)
                for k in range(8):
                    T.add("pe", C("matmul", ps[pb][0:R, 0:ncol], xT[:, k, tt * R:(tt + 1) * R], wv[:, k, 0:ncol],
                                                                    start=(k == 0), stop=(k == 7)),
                          reads=[wk, ("xT", tt)], writes=[PS(pb)])
                si = zr[0] % 2
                zr[0] += 1
                st_, sk = zst[si], ("zst", si)
                z6, zk = z16[si], ("z16", si)
                T.add("act", C("copy", st_[0:R, 0:ncol], ps[pb][0:R, 0:ncol]),
                      reads=[PS(pb)], writes=[sk])
                if wmode == "mm":
                    continue

                def outd(name, col, n, rows=None, st_=st_, sk=sk):
                    dst = O[name + sfx][l, b, tok0:tok0 + R, :] if rows is None else rows
                    T.add("sp", C("dma_start", out=dst, in_=st_[0:R, col:col + n]), reads=[sk], dma=True)

                def cast(dc, sc, n, st_=st_, sk=sk, z6=z6, zk=zk, eng="dve"):
                    T.add(eng, C("tensor_copy", z6[0:R, dc:dc + n], st_[0:R, sc:sc + n]), reads=[sk], writes=[zk])

                ksl = slice(gt * 128, gt * 128 + R)
                asl = slice(at * 128, at * 128 + R)
                qsl = slice(tt * R, (tt + 1) * R)
                if bi == 0:
                    if kind == "s":
                        outd("a_k", 256, 256)
                    elif gt >= 12:
                        outd("a_k", 256, 256, rows=O["a_k_p"][l, b, (gt - 12) * 128:(gt - 11) * 128, :])
                    cast(0, 0, 512)
                    pb2 = transposes(z6, zk, R, [(0, 128), (128, 128), (256, 128), (384, 128)], None, None)
                    pv = pt[pb2][:, :].rearrange("p (k c) -> p k c", c=128)
                    evac("act", aqT2[:, :, qsl], pv[:, 0:2, 0:R], [PTK(pb2)], [("aqT", tt)])
                    evac("act", akT2[:, :, asl], pv[:, 2:4, 0:R], [PTK(pb2)], [("akT", at)])
                elif bi == 1:
                    if kind == "s":
                        outd("a_v", 0, 256)
                    elif gt >= 12:
                        outd("a_v", 0, 256, rows=O["a_v_p"][l, b, (gt - 12) * 128:(gt - 11) * 128, :])
                    T.add("dve", C("tensor_copy",
                        av_aug[0:R, at, :].rearrange("p (h d) -> p h d", d=65)[:, :, 0:64],
                        st_[0:R, 0:256].rearrange("p (h d) -> p h d", d=64)), reads=[sk], writes=[("av", at)])
                    rope(st_, sk, R, 256, 4, gt)
                    cast(0, 256, 256)
                    pb2 = transposes(z6, zk, R, [(0, 128), (128, 128)], None, None)
                    pv = pt[pb2][:, :].rearrange("p (k c) -> p k c", c=128)
                    evac("act", bqT2[:, :, qsl], pv[:, 0:2, 0:R], [PTK(pb2)], [("bqT", tt)])
                elif bi == 2:
                    rope(st_, sk, R, 0, 1, gt)
                    rope(st_, sk, R, 128, 6, gt)
                    outd("b_k", 0, 64)
                    outd("b_v", 64, 64)
                    T.add("dve", C("tensor_copy", bv_aug[0:R, gt, 0:64], st_[0:R, 64:128]),
                          reads=[sk], writes=[("bv", gt)])
                    cast(0, 0, 64)
                    cast(64, 0, 64)
                    cast(128, 128, 384)
                    pb2 = transposes(z6, zk, R, [(0, 128), (128, 128), (256, 128), (384, 128)], None, None)
                    pv = pt[pb2][:, :].rearrange("p (k c) -> p k c", c=128)
                    evac("act", bkT2[:, ksl], pt[pb2][:, 0:R], [PTK(pb2)], [("bkT", gt)])
                    evac("act", iqT2[:, 0:3, qsl], pv[:, 1:4, 0:R], [PTK(pb2)], [("iqT", tt)])
                elif bi == 3:
                    rope(st_, sk, R, 0, 3, gt)
                    outd("b_i", 128, 64)
                    T.add("act", C("mul", iw_sb[0:R, tt, :], st_[0:R, 192:200], 8.0 ** -0.5),
                          reads=[sk], writes=[("iw", tt)])
                    cast(0, 0, 192)
                    cast(192, 128, 64)
                    pb2 = transposes(z6, zk, R, [(0, 128), (128, 128)], None, None)
                    evac("act", iqT2[:, 3, qsl], pt[pb2][:, 0:R], [PTK(pb2)], [("iqT", tt)])
                    evac("act", ikT2[:, ksl], pt[pb2][:, 128:128 + R], [PTK(pb2)], [("ikT", gt)])
                elif bi == 4:
                    rope(st_, sk, R, 0, 8, gt)
                    cast(0, 0, 512)
                    pb2 = transposes(z6, zk, R, [(0, 128), (128, 128), (256, 128), (384, 128)], None, None)
                    pv = pt[pb2][:, :].rearrange("p (k c) -> p k c", c=128)
                    evac("act", cqT2[:, :, qsl], pv[:, 0:4, 0:R], [PTK(pb2)], [("cqT", tt)])
                elif bi == 5:
                    rope(st_, sk, R, 0, 8, gt)
                    outd("c_k", 0, 512)
                    cast(0, 0, 512)
                    pb2 = transposes(z6, zk, R, [(0, 128), (128, 128), (256, 128), (384, 128)], None, None)
                    pv = pt[pb2][:, :].rearrange("p (k c) -> p k c", c=128)
                    evac("act", ckT2[:, :, ksl], pv[:, 0:4, 0:R], [PTK(pb2)], [("ckT", gt)])
                else:
                    outd("c_v", 0, 512)
                    T.add("dve", C("tensor_copy",
                        cv_aug[0:R, gt, :].rearrange("p (h d) -> p h d", d=129)[:, :, 0:128],
                        st_[0:R, 0:512].rearrange("p (h d) -> p h d", d=128)), reads=[sk], writes=[("cv", gt)])

    def ingest_cache(l, b):
        for kt in range(16):
            rows = slice(kt * 128, (kt + 1) * 128)
            ksl = rows
            zi = kt % 2
            z6, zk = z16[zi], ("z16", zi)
            T.add("pool", C("dma_start", out=z6[:, 0:512], in_=I["cache_c_k"][l, b, rows, :]), writes=[zk], dma=True)
            pb2 = transposes(z6, zk, 128, [(0, 128), (128, 128), (256, 128), (384, 128)], None, None)
            pv = pt[pb2][:, :].rearrange("p (k c) -> p k c", c=128)
            evac("act", ckT2[:, :, ksl], pv[:, 0:4, :], [PTK(pb2)], [("ckT", kt)])
            T.add("pool", C("dma_start",
                out=cv_aug[:, kt, :].rearrange("p (h d) -> p h d", d=129)[:, :, 0:128],
                in_=I["cache_c_v"][l, b, rows, :].rearrange("p (h d) -> p h d", d=128)), writes=[("cv", kt)], dma=True)
            T.add("pool", C("dma_start", out=bv_aug[:, kt, 0:64], in_=I["cache_b_v"][l, b, rows, :]),
                  writes=[("bv", kt)], dma=True)
            zi2 = (kt + 1) % 2
            z6b, zkb = z16[zi2], ("z16", zi2)
            for c in range(2):
                T.add("pool", C("dma_start", out=z6b[:, c * 64:(c + 1) * 64], in_=I["cache_b_k"][l, b, rows, :]),
                      writes=[zkb], dma=True)
                T.add("pool", C("dma_start", out=z6b[:, 128 + c * 64:128 + (c + 1) * 64],
                                                                 in_=I["cache_b_idx"][l, b, rows, :]), writes=[zkb], dma=True)
            pb3 = transposes(z6b, zkb, 128, [(0, 128), (128, 128)], None, None)
            evac("act", bkT2[:, ksl], pt[pb3][:, 0:128], [PTK(pb3)], [("bkT", kt)])
            evac("act", ikT2[:, ksl], pt[pb3][:, 128:256], [PTK(pb3)], [("ikT", kt)])
        for kt in range(4):
            rows = slice(kt * 128, (kt + 1) * 128)
            zi = kt % 2
            z6, zk = z16[zi], ("z16", zi)
            T.add("pool", C("dma_start", out=z6[:, 0:256], in_=I["cache_a_k"][l, b, rows, :]), writes=[zk], dma=True)
            pb2 = transposes(z6, zk, 128, [(0, 128), (128, 128)], None, None)
            pv = pt[pb2][:, :].rearrange("p (k c) -> p k c", c=128)
            evac("act", akT2[:, :, rows], pv[:, 0:2, :], [PTK(pb2)], [("akT", kt)])
            T.add("pool", C("dma_start",
                out=av_aug[:, kt, :].rearrange("p (h d) -> p h d", d=65)[:, :, 0:64],
                in_=I["cache_a_v"][l, b, rows, :].rearrange("p (h d) -> p h d", d=64)), writes=[("av", kt)], dma=True)

    def mem_prep(l, kind, b):
        if kind == "s":
            for kt in range(2):
                rows = slice(kt * 128, (kt + 1) * 128)
                T.add("pool", C("dma_start", out=om16[:, :], in_=I["cache_mem_k"][l, b, rows, :]), writes=["om16"], dma=True)
                pb2 = transposes(om16, "om16", 128, [(k * 128, 128) for k in range(8)], None, None)
                evac("act", mkT[:, :, rows], pt[pb2][:, :].rearrange("p (k c) -> p k c", c=128), [PTK(pb2)], [("mkT", kt)])
                T.add("pool", C("dma_start",
                    out=mv_aug[:, kt, :].rearrange("p (h d) -> p h d", d=257)[:, :, 0:256],
                    in_=I["cache_mem_v"][l, b, rows, :].rearrange("p (h d) -> p h d", d=256)), writes=[("mv", kt)], dma=True)
            return
        for kt in range(2):
            rows = slice(kt * 128, (kt + 1) * 128)
            T.add("pool", C("dma_start", out=om16[:, :], in_=I["mem_prompt"][b, rows, :]), writes=["om16"], dma=True)
            pb2 = transposes(om16, "om16", 128, [(k * 128, 128) for k in range(8)], None, None)
            evac("act", qmT[:, :, rows], pt[pb2][:, :].rearrange("p (k c) -> p k c", c=128), [PTK(pb2)], [("qmT", kt)])
        for wi, wname in enumerate(("w_mem_k", "w_mem_v")):
            for half in range(2):
                wt, wk = wload([(lambda w: w3(w, 8, 512),
                                 I[wname][l][:, half * 512:(half + 1) * 512].rearrange("(k p) c -> p k c", p=128))])
                wv = w3(wt, 8, 512)
                for kt in range(2):
                    rows = slice(kt * 128, (kt + 1) * 128)
                    pb = nextps(0, 2)
                    for k in range(8):
                        T.add("pe", C("matmul", ps[pb][:, :], qmT[:, k, rows], wv[:, k, :], start=(k == 0), stop=(k == 7)),
                              reads=[wk, ("qmT", kt)], writes=[PS(pb)])
                    T.add("act", C("copy", mt1[:, :], ps[pb][:, :]), reads=[PS(pb)], writes=["mt1"])
                    dst = O["m_k_p" if wi == 0 else "m_v_p"][l, b, rows, half * 512:(half + 1) * 512]
                    T.add("sp", C("dma_start", out=dst, in_=mt1[:, :]), reads=["mt1"], dma=True)
                    if wi == 0:
                        T.add("dve", C("tensor_copy", y16[:, :], mt1[:, :]), reads=["mt1"], writes=["y16"])
                        pb2 = transposes(y16, "y16", 128, [(k * 128, 128) for k in range(4)], None, None)
                        evac("act", mkT[:, half * 4:half * 4 + 4, rows],
                             pt[pb2][:, 0:512].rearrange("p (k c) -> p k c", c=128), [PTK(pb2)], [("mkT", kt)])
                    else:
                        T.add("dve", C("tensor_copy",
                            mv_aug[:, kt, :].rearrange("p (h d) -> p h d", d=257)[:, 2 * half:2 * half + 2, 0:256],
                            mt1[:, :].rearrange("p (h d) -> p h d", d=256)), reads=["mt1"], writes=[("mv", kt)])

    def norm_heads(acc, acckey, R, nh, dh, out16, okey, cols):
        av = acc[0:R, 0:nh * (dh + 1)].rearrange("p (h d) -> p h d", d=dh + 1)
        T.add("dve", C("reciprocal", out=small[0:R, 0:nh], in_=av[:, :, dh]), reads=[acckey], writes=["small"])
        for s in range(nh):
            T.add("dve", C("tensor_scalar", out=out16[0:R, cols[s]:cols[s] + dh], in0=av[:, s, 0:dh],
                                                        scalar1=small[0:R, s:s + 1], scalar2=None, op0=ALU.mult),
                  reads=[acckey, "small"], writes=[okey])

    def attn_phase(l, kind, tb, nt, R):
        N = nt * R
        prompt = kind == "p"
        nkt_blk = (tb * 4 + nt) if prompt else 17

        def c_head(h):
            started = set()
            for kt in range(nkt_blk):
                ksz = 128 if (prompt or kt < 16) else 64
                tt0 = max(0, kt - tb * 4) if prompt else 0
                cs0 = tt0 * R
                for i in range(2):
                    sbk = nextps(0, 2)
                    pti = sbk
                    T.add("pe", C("matmul",
                        ps[sbk][0:ksz, cs0:N], ckT2[i * 64:(i + 1) * 64, h, kt * 128:kt * 128 + ksz],
                        cqT2[i * 64:(i + 1) * 64, h, cs0:N], start=True, stop=True),
                        reads=[("ckT", kt)] + [("cqT", t) for t in range(tt0, nt)], writes=[PS(sbk)])
                    T.add("act", C("activation", out=PT[pti][0:ksz, cs0:N], in_=ps[sbk][0:ksz, cs0:N],
                                                                          func=AF.Exp, scale=0.125),
                          reads=[PS(sbk)], writes=[("PT", pti)])
                    if prompt and kt >= tb * 4:
                        T.add("pool", C("memset", PT[pti][64:128, cs0:cs0 + 64], 0.0),
                              reads=[], writes=[("PT", pti)])
                    for tt in range(tt0, nt):
                        a = tt * 2 + i
                        bk, col = 2 + a // 3, (a % 3) * 129
                        last = (tb * 4 + tt) if prompt else 16
                        first = bk not in started
                        started.add(bk)
                        T.add("pe", C("matmul",
                            ps[bk][0:R, col:col + 129], PT[pti][0:ksz, tt * R:(tt + 1) * R],
                            cv_aug[0:ksz, kt, h * 129:(h + 1) * 129], start=first, stop=first, skip_group_check=(not first)),
                            reads=[("PT", pti), ("cv", kt)], writes=[PS(bk)])
            for tt in range(nt):
                a0, a1 = tt * 2, tt * 2 + 1
                b0, c0 = 2 + a0 // 3, (a0 % 3) * 129
                b1, c1 = 2 + a1 // 3, (a1 % 3) * 129
                T.add("dve", C("reciprocal", out=small[0:R, 0:1], in_=ps[b0][0:R, c0 + 128:c0 + 129]),
                      reads=[PS(b0)], writes=["small"])
                T.add("dve", C("reciprocal", out=small[0:R, 1:2], in_=ps[b1][0:R, c1 + 128:c1 + 129]),
                      reads=[PS(b1)], writes=["small"])
                T.add("dve", C("tensor_tensor", out=small[0:R, 1:2], in0=small[0:R, 1:2], in1=lams[0:R, 4:5], op=ALU.mult),
                      reads=["small", "lams"], writes=["small"])
                T.add("dve", C("tensor_scalar", out=yct[0:R, :], in0=ps[b0][0:R, c0:c0 + 128], scalar1=small[0:R, 0:1],
                                                       scalar2=None, op0=ALU.mult), reads=[PS(b0), "small"], writes=["yct"])
                T.add("dve", C("scalar_tensor_tensor", out=ycf[0:R, 0:128], in0=ps[b1][0:R, c1:c1 + 128],
                                                                     scalar=small[0:R, 1:2], in1=yct[0:R, :],
                                                                     op0=ALU.mult, op1=ALU.add),
                      reads=[PS(b1), "small", "yct"], writes=["ycf"])
                T.add("act", C("activation", out=yct[0:R, :], in_=ycf[0:R, 0:128], func=AF.Square,
                                                    accum_out=small[0:R, 2:3]), reads=["ycf"], writes=["yct", "small"])
                T.add("dve", C("tensor_scalar", out=small[0:R, 2:3], in0=small[0:R, 2:3], scalar1=1.0 / 128.0,
                                                       scalar2=1e-5, op0=ALU.mult, op1=ALU.add), reads=["small"], writes=["small"])
                T.add("act", C("activation", out=small[0:R, 2:3], in_=small[0:R, 2:3], func=AF.Sqrt),
                      reads=["small"], writes=["small"])
                T.add("dve", C("reciprocal", out=small[0:R, 3:4], in_=small[0:R, 2:3]), reads=["small"], writes=["small"])
                T.add("dve", C("scalar_tensor_tensor", out=y16[0:R, 0:128], in0=ycf[0:R, 0:128],
                                                                     scalar=small[0:R, 3:4], in1=gsub[0:R, :],
                                                                     op0=ALU.mult, op1=ALU.mult),
                      reads=["ycf", "small", "gsub"], writes=["y16"])
                pb2 = transposes(y16, "y16", R, [(0, 128)], None, None)
                evac("act", yT[:, 4 + h, tt * R:(tt + 1) * R], pt[pb2][:, 0:R], [PTK(pb2)], [("yT", tt)])

        import os
        def b_scores(tt):
            gt = tb * 4 + tt if prompt else 16
            nkt = gt + 1 if prompt else 17
            Wk = (gt + 1) * 128 if prompt else PAST + TS
            qsl = slice(tt * R, (tt + 1) * R)
            nkb = (Wk + 511) // 512
            for kb in range(nkb):
                w = min(512, Wk - kb * 512)
                ksl = slice(kb * 512, kb * 512 + w)
                for ih in range(8):
                    hf, c = ih % 2, ih // 2
                    pb = nextps(0, 2)
                    T.add("pe", C("matmul", ps[pb][0:R, 0:w], iqT2[hf * 64:(hf + 1) * 64, c, qsl],
                                                                      ikT2[hf * 64:(hf + 1) * 64, ksl], start=True, stop=True),
                          reads=[("iqT", tt)] + [("ikT", t) for t in range(kb * 4, min(nkt, kb * 4 + 4))], writes=[PS(pb)])
                    ri = pb
                    T.add("act", C("activation", out=relu_t[ri][0:R, 0:w], in_=ps[pb][0:R, 0:w], func=AF.Relu),
                          reads=[PS(pb)], writes=[("relu", ri)])
                    if ih == 0:
                        T.add("dve", C("tensor_scalar", out=score[0:R, ksl], in0=relu_t[ri][0:R, 0:w],
                                                                      scalar1=iw_sb[0:R, tt, 0:1], scalar2=None, op0=ALU.mult),
                              reads=[("relu", ri), ("iw", tt)], writes=["score"])
                    else:
                        T.add("dve", C("scalar_tensor_tensor",
                            out=score[0:R, ksl], in0=relu_t[ri][0:R, 0:w], scalar=iw_sb[0:R, tt, ih:ih + 1],
                            in1=score[0:R, ksl], op0=ALU.mult, op1=ALU.add),
                            reads=[("relu", ri), ("iw", tt), "score"], writes=["score"])
            if prompt:
                T.add("dve", C("memset", score[0:64, Wk - 64:Wk], -1e30), reads=[], writes=["score"])

        def b_topk(tt):
            gt = tb * 4 + tt if prompt else 16
            nkt = gt + 1 if prompt else 17
            Wk = (gt + 1) * 128 if prompt else PAST + TS
            qsl = slice(tt * R, (tt + 1) * R)
            if Wk > 256:
                for r in range(32):
                    T.add("dve", C("max", out=small[0:R, 8:16], in_=score[0:R, 0:Wk]), reads=["score"], writes=["small8"])
                    T.add("dve", C("match_replace", out=score[0:R, 0:Wk], in_to_replace=small[0:R, 8:16],
                                                           in_values=score[0:R, 0:Wk], imm_value=-1e30),
                          reads=["score", "small8"], writes=["score"])
                T.add("dve", C("tensor_single_scalar", out=sel[0:R, 0:Wk], in_=score[0:R, 0:Wk], scalar=-1e29, op=ALU.is_lt),
                      reads=["score"], writes=["sel"])
                if prompt:
                    T.add("dve", C("memset", sel[0:64, Wk - 64:Wk], 0.0), reads=[], writes=["sel"])
            else:
                T.add("dve", C("tensor_single_scalar", out=sel[0:R, 0:Wk], in_=score[0:R, 0:Wk], scalar=-1e29, op=ALU.is_gt),
                      reads=["score"], writes=["sel"])

        def b_attn(tt):
            gt = tb * 4 + tt if prompt else 16
            nkt = gt + 1 if prompt else 17
            Wk = (gt + 1) * 128 if prompt else PAST + TS
            qsl = slice(tt * R, (tt + 1) * R)
            accB = 5
            for kt in range(nkt):
                ksz = 128 if (prompt or kt < 16) else 64
                kcs = slice(kt * 128, kt * 128 + ksz)
                pbm = transposes(sel, "sel", R, [(kt * 128, ksz)], None, None)
                evac("act", selT[pbm][0:ksz, 0:R], pt[pbm][0:ksz, 0:R], [PTK(pbm)], [("selT", pbm)])
                bmode = int(os.environ.get("KDBG_B", "9"))
                if bmode <= 1:
                    continue
                for hf in range(2):
                    for c in range(2):
                        T.add("pe", C("matmul", ps[hf][0:ksz, c * R:(c + 1) * R],
                                      bkT2[hf * 64:(hf + 1) * 64, kcs],
                                      bqT2[hf * 64:(hf + 1) * 64, c, qsl], start=True, stop=True),
                              reads=[("bkT", kt), ("bqT", tt)], writes=[PS(hf)])
                pti = kt % 2
                for hf in range(2):
                    T.add("act", C("activation", out=PT[pti][0:ksz, hf * 2 * R:(hf + 1) * 2 * R], in_=ps[hf][0:ksz, 0:2 * R],
                                   func=AF.Exp, scale=0.125), reads=[PS(hf)], writes=[("PT", pti)])
                if bmode <= 2:
                    continue
                T.add("dve", C("tensor_tensor",
                    out=PT[pti][0:ksz, 0:4 * R].rearrange("p (h q) -> p h q", q=R),
                    in0=PT[pti][0:ksz, 0:4 * R].rearrange("p (h q) -> p h q", q=R),
                    in1=selT[pbm][0:ksz, 0:R].unsqueeze(1).to_broadcast([ksz, 4, R]), op=ALU.mult),
                    reads=[("PT", pti), ("selT", pbm)], writes=[("PT", pti)])
                if bmode <= 3:
                    continue
                for s in range(4):
                    T.add("pe", C("matmul", ps[accB][0:R, s * 65:(s + 1) * 65], PT[pti][0:ksz, s * R:(s + 1) * R],
                                                                       bv_aug[0:ksz, kt, :], start=(kt == 0 and s == 0), stop=(kt == 0 and s == 0),
                                                                       skip_group_check=not (kt == 0 and s == 0)),
                          reads=[("PT", pti), ("bv", kt)], writes=[PS(accB)])
            norm_heads(ps[accB], PS(accB), R, 4, 64, y16, "y16", [0, 128, 64, 192])
            pb2 = transposes(y16, "y16", R, [(0, 128), (128, 128)], None, None)
            evac("act", yT[:, 2:4, qsl], pt[pb2][:, 0:256].rearrange("p (k c) -> p k c", c=128)[:, :, 0:R], [PTK(pb2)], [("yT", tt)])

        def a_attn(tt):
            gt = tb * 4 + tt if prompt else 16
            nkt = gt + 1 if prompt else 17
            Wk = (gt + 1) * 128 if prompt else PAST + TS
            qsl = slice(tt * R, (tt + 1) * R)
            if prompt:
                akts = [(kt, kt - (gt - 4), 128) for kt in range(max(0, gt - 4), gt + 1)]
            else:
                akts = [(kt, kt, 128 if kt < 4 else 64) for kt in range(5)]
            accA = 5
            for n_, (kt, ti, ksz) in enumerate(akts):
                kcs = slice(kt * 128, kt * 128 + ksz)
                for hf in range(2):
                    for c in range(2):
                        T.add("pe", C("matmul", ps[hf][0:ksz, c * R:(c + 1) * R],
                                      akT2[hf * 64:(hf + 1) * 64, c, kcs],
                                      aqT2[hf * 64:(hf + 1) * 64, c, qsl], start=True, stop=True),
                              reads=[("akT", kt), ("aqT", tt)], writes=[PS(hf)])
                bfull = biasA[0:ksz, ti * 512:(ti + 1) * 512].rearrange("p (c f q) -> p c f q", f=2, q=128)
                for hf in range(2):
                    T.add("dve", C("scalar_tensor_tensor",
                                   out=tmpA[0:ksz, hf * 2 * R:(hf + 1) * 2 * R].rearrange("p (c q) -> p c q", q=R),
                                   in0=ps[hf][0:ksz, 0:2 * R].rearrange("p (c q) -> p c q", q=R), scalar=0.125,
                                   in1=bfull[:, :, hf, 0:R], op0=ALU.mult, op1=ALU.add),
                          reads=[PS(hf), "biasA"], writes=["tmpA"])
                pti = n_ % 2
                T.add("act", C("activation", out=PT[pti][0:ksz, 0:4 * R], in_=tmpA[0:ksz, 0:4 * R], func=AF.Exp),
                      reads=["tmpA"], writes=[("PT", pti)])
                for s_ in range(4):
                    h = [0, 2, 1, 3][s_]
                    T.add("pe", C("matmul", ps[accA][0:R, s_ * 65:(s_ + 1) * 65], PT[pti][0:ksz, s_ * R:(s_ + 1) * R],
                                  av_aug[0:ksz, kt, h * 65:(h + 1) * 65], start=(n_ == 0 and s_ == 0), stop=(n_ == 0 and s_ == 0),
                                  skip_group_check=not (n_ == 0 and s_ == 0)),
                          reads=[("PT", pti), ("av", kt)], writes=[PS(accA)])
            norm_heads(ps[accA], PS(accA), R, 4, 64, y16, "y16", [0, 128, 64, 192])
            pb2 = transposes(y16, "y16", R, [(0, 128), (128, 128)], None, None)
            evac("act", yT[:, 0:2, qsl], pt[pb2][:, 0:256].rearrange("p (k c) -> p k c", c=128)[:, :, 0:R], [PTK(pb2)], [("yT", tt)])


        for tt in range(nt):
            b_scores(tt)
            a_attn(tt)
            b_topk(tt)
            if nt == 4:
                c_head(tt)
            else:
                for h in range(4):
                    c_head(h)
            b_attn(tt)

    def tok_proj(l, wname, srcT, srckeys, nt, R, scale):
        W = I[wname][l]
        for half in range(2):
            wt, wk = wload([(lambda w: w3(w, 8, 512), W[:, half * 512:(half + 1) * 512].rearrange("(k p) c -> p k c", p=128))])
            wv = w3(wt, 8, 512)
            for tt in range(nt):
                pb = 2 + (tt % 4)
                for k in range(8):
                    T.add("pe", C("matmul", ps[pb][0:R, :], srcT[:, k, tt * R:(tt + 1) * R], wv[:, k, :],
                                                                    start=(k == 0), stop=(k == 7)),
                          reads=[wk] + srckeys(tt), writes=[PS(pb)])
                T.add("dve", C("scalar_tensor_tensor",
                    out=x_tm[0:R, tt, half * 512:(half + 1) * 512], in0=ps[pb][0:R, :], scalar=scale,
                    in1=x_tm[0:R, tt, half * 512:(half + 1) * 512], op0=ALU.mult, op1=ALU.add),
                    reads=[PS(pb), ("x_tm", tt)], writes=[("x_tm", tt)])

    def merge_phase(l, nt, R):
        N = nt * R
        win = I["w_in"][l]
        load_ln(l, 1)
        for c in range(8):
            dc = slice(c * 128, (c + 1) * 128)
            parts = [(lambda w, g=g: w3(w, 32, 128)[:, g * 8:(g + 1) * 8, :],
                      win[:, cG + g * 1024 + c * 128:cG + g * 1024 + (c + 1) * 128].rearrange("(k p) c -> p k c", p=128))
                     for g in range(3)]
            parts.append((lambda w: w3(w, 32, 128)[:, 24:26, :], I["w_branch_a"][l][:, dc].rearrange("(k p) c -> p k c", p=128)))
            parts.append((lambda w: w3(w, 32, 128)[:, 26:28, :], I["w_branch_b"][l][:, dc].rearrange("(k p) c -> p k c", p=128)))
            parts.append((lambda w: w3(w, 32, 128)[:, 28:32, :], I["w_branch_c"][l][:, dc].rearrange("(k p) c -> p k c", p=128)))
            wgt, wgk = wload(parts)
            wgv = w3(wgt, 32, 128)
            wbk = wgk
            for g in range(3):
                gb = g
                for k in range(8):
                    T.add("pe", C("matmul", ps[gb][:, 0:N], wgv[:, g * 8 + k, :], xT[:, k, 0:N], start=(k == 0), stop=(k == 7)),
                          reads=[wgk] + [("xT", t) for t in range(nt)], writes=[PS(gb)])
                T.add("act", C("activation", out=sgm[g][:, 0:N], in_=ps[gb][:, 0:N], func=AF.Sigmoid),
                      reads=[PS(gb)], writes=[("sgm", g)])
            ks = [(0, 2), (2, 4), (4, 8)]
            for g in range(3):
                bb = 3 + g
                k0, k1 = ks[g]
                for k in range(k0, k1):
                    T.add("pe", C("matmul", ps[bb][:, 0:N], wgv[:, 24 + k, :], yT[:, k, 0:N],
                                                              start=(k == k0), stop=(k == k1 - 1)),
                          reads=[wbk] + [("yT", t) for t in range(nt)], writes=[PS(bb)])
            T.add("dve", C("tensor_tensor", out=mt1[:, 0:N], in0=sgm[0][:, 0:N], in1=ps[3][:, 0:N], op=ALU.mult),
                  reads=[("sgm", 0), PS(3)], writes=["mt1"])
            T.add("dve", C("tensor_tensor", out=mt2[:, 0:N], in0=sgm[1][:, 0:N], in1=ps[4][:, 0:N], op=ALU.mult),
                  reads=[("sgm", 1), PS(4)], writes=["mt2"])
            T.add("pool", C("tensor_tensor", out=mt1[:, 0:N], in0=mt1[:, 0:N], in1=mt2[:, 0:N], op=ALU.add),
                  reads=["mt1", "mt2"], writes=["mt1"])
            T.add("dve", C("tensor_tensor", out=mt2[:, 0:N], in0=sgm[2][:, 0:N], in1=ps[5][:, 0:N], op=ALU.mult),
                  reads=[("sgm", 2), PS(5)], writes=["mt2"])
            T.add("pool", C("tensor_tensor", out=mT[:, c, 0:N], in0=mt1[:, 0:N], in1=mt2[:, 0:N], op=ALU.add),
                  reads=["mt1", "mt2"], writes=[("mT", c)])
        tok_proj(l, "w_out", mT, lambda tt: [("mT", c) for c in range(8)], nt, R, 1.0)
        for tt in range(nt):
            layer_norm(l, 1, tt, R, None, None)

    def mem_phase(l, nt, R):
        N = nt * R
        load_ln(l, 2)
        wq = I["w_mem_q"][l]
        for half in range(2):
            wt, wk = wload([(lambda w: w3(w, 8, 512), wq[:, half * 512:(half + 1) * 512].rearrange("(k p) c -> p k c", p=128))])
            wv = w3(wt, 8, 512)
            for cc in range(4):
                c = half * 4 + cc
                pb = nextps(0, 2)
                for k in range(8):
                    T.add("pe", C("matmul", ps[pb][:, 0:N], wv[:, k, cc * 128:(cc + 1) * 128], xT[:, k, 0:N],
                                                                    start=(k == 0), stop=(k == 7)),
                          reads=[wk] + [("xT", t) for t in range(nt)], writes=[PS(pb)])
                T.add("act", C("copy", qmT[:, c, 0:N], ps[pb][:, 0:N]), reads=[PS(pb)], writes=[("qmT", c)])
        for h in range(4):
            for kt in range(2):
                sbk = nextps(0, 2)
                for c2 in range(2):
                    T.add("pe", C("matmul", ps[sbk][:, 0:N], mkT[:, h * 2 + c2, kt * 128:(kt + 1) * 128],
                                                                   qmT[:, h * 2 + c2, 0:N], start=(c2 == 0), stop=(c2 == 1)),
                          reads=[("mkT", kt), ("qmT", h * 2), ("qmT", h * 2 + 1)], writes=[PS(sbk)])
                pti = sbk
                T.add("act", C("activation", out=PT[pti][:, 0:N], in_=ps[sbk][:, 0:N], func=AF.Exp, scale=1.0 / 16.0),
                      reads=[PS(sbk)], writes=[("PT", pti)])
                for tt in range(nt):
                    bk = 2 + tt
                    T.add("pe", C("matmul", ps[bk][0:R, 0:257], PT[pti][:, tt * R:(tt + 1) * R],
                                                                         mv_aug[:, kt, h * 257:(h + 1) * 257], start=(kt == 0), stop=(kt == 1)),
                          reads=[("PT", pti), ("mv", kt)], writes=[PS(bk)])
            for tt in range(nt):
                bk = 2 + tt
                T.add("dve", C("reciprocal", out=small[0:R, 0:1], in_=ps[bk][0:R, 256:257]), reads=[PS(bk)], writes=["small"])
                T.add("dve", C("tensor_scalar", out=y16[0:R, 0:256], in0=ps[bk][0:R, 0:256], scalar1=small[0:R, 0:1],
                                                              scalar2=None, op0=ALU.mult), reads=[PS(bk), "small"], writes=["y16"])
                pb2 = transposes(y16, "y16", R, [(0, 128), (128, 128)], None, None)
                evac("act", yT[:, 2 * h:2 * h + 2, tt * R:(tt + 1) * R],
                     pt[pb2][:, 0:256].rearrange("p (k c) -> p k c", c=128)[:, :, 0:R], [PTK(pb2)], [("yT", tt)])
        tok_proj(l, "w_mem_o", yT, lambda tt: [("yT", tt)], nt, R, 1.0)
        for tt in range(nt):
            layer_norm(l, 2, tt, R, None, None)

    def load_block(src_rows, nt, R):
        for tt in range(nt):
            sap, skey = src_rows(tt)
            T.add("sp", C("dma_start", out=xn[0:R, :], in_=sap), reads=[skey], writes=["xn"], dma=True)
            T.add("act", C("mul", x_tm[0:R, tt, :], xn[0:R, :], ALPHA), reads=["xn"], writes=[("x_tm", tt)])
            T.add("dve", C("tensor_copy", xbf[0:R, :], xn[0:R, :]), reads=["xn"], writes=["xbf"])
            b = transposes(xbf, "xbf", R, [(k * 128, 128) for k in range(8)], None, None)
            evac("act", xT[:, :, tt * R:(tt + 1) * R], pt[b][:, :].rearrange("p (k c) -> p k c", c=128)[:, :, 0:R],
                 reads=[PTK(b)], writes=[("xT", tt)])

    class _Stop(Exception):
        pass

    phc = {}

    def ph(name):
        phc[name] = phc.get(name, 0) + 1
        if stop is not None:
            sn, _, cnt = stop.partition("@")
            if name == sn and phc[name] >= int(cnt or 1):
                raise _Stop()

    try:
        for (kind, b) in units:
            prompt = kind == "p"
            nblk, nt, R = (4, 4, 128) if prompt else (1, 1, 64)
            for l in range(n_layers):
                layer_consts(l)
                ph("consts")
                T.barrier()
                mem_prep(l, kind, b)
                ph("memprep")
                if not prompt:
                    T.barrier()
                    ingest_cache(l, b)
                    ph("ingest")
                for tb in range(nblk):
                    if l == 0:
                        if prompt:
                            src = lambda tt, tb=tb: (I["x_prompt"][b, (tb * 4 + tt) * 128:(tb * 4 + tt + 1) * 128, :], "xin")
                        else:
                            src = lambda tt: (I["x_sample"][b, :, :], "xin")
                    else:
                        src = lambda tt, tb=tb, R=R: (xmid[(tb * 4 + tt) * 128:(tb * 4 + tt) * 128 + R, :], ("xmid", tb * 4 + tt))
                    load_block(src, nt, R)
                    ph("load")
                    T.barrier()
                    ffn(l, 1, nt, R, 0, None)
                    ph("ffn1")
                    T.barrier()
                    win_phase(l, kind, b, tb, nt, R)
                    ph("win")
                    attn_phase(l, kind, tb, nt, R)
                    ph("attn")
                    T.barrier()
                    merge_phase(l, nt, R)
                    ph("merge")
                    mem_phase(l, nt, R)
                    ph("mem")
                    T.barrier()
                    if l == n_layers - 1:
                        if prompt:
                            fd = lambda tt, tb=tb: (O["y_p"][b, (tb * 4 + tt) * 128:(tb * 4 + tt + 1) * 128, :], "yout")
                        else:
                            fd = lambda tt: (O["y_s"][b, :, :], "yout")
                    else:
                        fd = lambda tt, tb=tb, R=R: (xmid[(tb * 4 + tt) * 128:(tb * 4 + tt) * 128 + R, :], ("xmid", tb * 4 + tt))
                    ffn(l, 2, nt, R, 3, fd)
                    ph("ffn2")
    except _Stop:
        pass
    T.emit(nc)
    es.close()
    return nc


def _bias_table(a_rel_bias):
    L = a_rel_bias.shape[0]
    kl = np.arange(640)[:, None]
    ql = np.arange(128)[None, :]
    idx = np.clip(ql - kl + 512, -128, 128) + 128
    masked = ((ql < 64) & (kl >= 576)) | ((ql >= 64) & (kl < 64))
    tab = a_rel_bias[:, :, idx]
    tab = np.where(masked[None, None], np.float32(NEG), tab).astype(np.float32)
    tab = tab.reshape(L, 4, 5, 128, 128).transpose(0, 3, 2, 1, 4)
    return np.ascontiguousarray(tab.reshape(L, 128, 5 * 4 * 128))


def _rope_table():
    pos = np.arange(17 * 128, dtype=np.float32)
    inv = (500000.0 ** (-np.arange(8, dtype=np.float32) / 8)).astype(np.float32)
    ang = pos[:, None] * inv[None, :]
    return np.concatenate([np.cos(ang), np.sin(ang)], axis=1).astype(np.float32)


_OUT_ORDER = ["y_p", "y_s", "a_k_p", "a_v_p", "b_k_p", "b_v_p", "b_i_p", "c_k_p", "c_v_p", "m_k_p", "m_v_p",
              "a_k_s", "a_v_s", "b_k_s", "b_v_s", "b_i_s", "c_k_s", "c_v_s"]


def make_in_maps(inp):
    f = lambda a: np.ascontiguousarray(np.asarray(a, dtype=np.float32))
    shared = {k: f(inp[k]) for k in ("ln_g", "ln_b", "ffn1_w_gu", "ffn1_w_d", "ffn2_w_gu", "ffn2_w_d", "w_in",
                                     "w_branch_a", "w_branch_b", "w_branch_c", "w_out", "w_mem_q", "w_mem_k",
                                     "w_mem_v", "w_mem_o", "c_subln_g")}
    shared["c_lambda"] = f(inp["c_lambda"]).reshape(DEPTH, 256)
    shared["biasA"] = _bias_table(f(inp["a_rel_bias"]))
    shared["rope_cs"] = _rope_table()
    maps = []
    for c in range(NCORES):
        s = slice(4 * c, 4 * c + 4)
        m = dict(shared)
        m["x_prompt"] = f(inp["x_prompt"][s])
        m["x_sample"] = f(inp["x_sample"][s])
        m["mem_prompt"] = f(inp["mem_prompt"][s])
        m["cache_a_k"] = f(inp["cache_a_k"][:, s]).reshape(DEPTH, 4, 512, 256)
        m["cache_a_v"] = f(inp["cache_a_v"][:, s]).reshape(DEPTH, 4, 512, 256)
        m["cache_b_k"] = f(inp["cache_b_k"][:, s])
        m["cache_b_v"] = f(inp["cache_b_v"][:, s])
        m["cache_b_idx"] = f(inp["cache_b_idx"][:, s])
        m["cache_c_k"] = f(inp["cache_c_k"][:, s]).reshape(DEPTH, 4, PAST, 512)
        m["cache_c_v"] = f(inp["cache_c_v"][:, s]).reshape(DEPTH, 4, PAST, 512)
        m["cache_mem_k"] = f(inp["cache_mem_k"][:, s]).reshape(DEPTH, 4, 256, 1024)
        m["cache_mem_v"] = f(inp["cache_mem_v"][:, s]).reshape(DEPTH, 4, 256, 1024)
        maps.append(m)
    return maps


def assemble(results):
    def cat(name, axis):
        return np.concatenate([r[name] for r in results], axis=axis)
    B = 4 * len(results)
    out = {
        "y_p": cat("y_p", 0), "y_s": cat("y_s", 0),
        "a_k_p": cat("a_k_p", 1).reshape(DEPTH, B, 512, 4, 64), "a_v_p": cat("a_v_p", 1).reshape(DEPTH, B, 512, 4, 64),
        "b_k_p": cat("b_k_p", 1), "b_v_p": cat("b_v_p", 1), "b_i_p": cat("b_i_p", 1),
        "c_k_p": cat("c_k_p", 1).reshape(DEPTH, B, SEQ, 4, 128), "c_v_p": cat("c_v_p", 1).reshape(DEPTH, B, SEQ, 4, 128),
        "m_k_p": cat("m_k_p", 1).reshape(DEPTH, B, 256, 4, 256), "m_v_p": cat("m_v_p", 1).reshape(DEPTH, B, 256, 4, 256),
        "a_k_s": cat("a_k_s", 1).reshape(DEPTH, B, TS, 4, 64), "a_v_s": cat("a_v_s", 1).reshape(DEPTH, B, TS, 4, 64),
        "b_k_s": cat("b_k_s", 1), "b_v_s": cat("b_v_s", 1), "b_i_s": cat("b_i_s", 1),
        "c_k_s": cat("c_k_s", 1).reshape(DEPTH, B, TS, 4, 128), "c_v_s": cat("c_v_s", 1).reshape(DEPTH, B, TS, 4, 128),
    }
    return tuple(np.ascontiguousarray(out[k], dtype=np.float32) for k in _OUT_ORDER)


def kernel(**inputs):
    units = [("p", b) for b in range(4)] + [("s", b) for b in range(4)]
    nc = build_program(units)
    in_maps = make_in_maps(inputs)
    res = run_bass_kernel_spmd(nc, in_maps, core_ids=list(range(NCORES)))
    return assemble(res.results)
```

```python
import contextlib
import math
import numpy as np
import concourse.bass as bass
import concourse.mybir as mybir
from concourse.bass_utils import run_bass_kernel_spmd

F32 = mybir.dt.float32
BF16 = mybir.dt.bfloat16
AF = mybir.ActivationFunctionType
ALU = mybir.AluOpType

NCORES = 8
D = 1024
SEQ = 2048
TS = 64
PAST = 2048
DEPTH = 2
DFF = 2816
NFC = DFF // 128
INC = 6344
ALPHA = (2.0 * DEPTH) ** 0.25
LN_EPS = 1e-5
NEG = -30000.0
cAq, cAk, cAv, cBq, cBk, cBv, cIq, cIk, cIw, cCq, cCk, cCv, cG = (
    0, 256, 512, 768, 1024, 1088, 1152, 1664, 1728, 1736, 2248, 2760, 3272)
ZBLK = [(0, 512), (512, 512), (1024, 512), (1536, 200), (1736, 512), (2248, 512), (2760, 512)]

ENGS = ("pe", "act", "dve", "pool", "sp")
NDMASEM = 8


def C(method, *args, **kw):
    return lambda e: getattr(e, method)(*args, **kw)


class Ins:
    __slots__ = ("eng", "idx", "fn", "dma", "deps", "sig", "slot", "val", "waits")

    def __init__(self, eng, idx, fn, dma):
        self.eng, self.idx, self.fn, self.dma = eng, idx, fn, dma
        self.deps = ()
        self.sig = False
        self.slot = 0
        self.val = 0
        self.waits = ()


class Trk:
    def __init__(self):
        self.streams = {e: [] for e in ENGS}
        self.wr = {}
        self.rd = {}
        self.ndma = {e: 0 for e in ENGS}
        self.pending = {e: [] for e in ENGS}
        self.lastc = {e: None for e in ENGS}
        self.lastd = {e: {} for e in ENGS}

    def barrier(self):
        deps = []
        for e in ENGS:
            deps += list(self.lastd[e].values())
            if self.lastc[e] is not None:
                deps.append(self.lastc[e])
        for e in ENGS:
            self.pending[e] = list(deps)

    def add(self, eng, fn, reads=(), writes=(), dma=False, nobarrier=False):
        st = self.streams[eng]
        ins = Ins(eng, len(st), fn, dma)
        deps = {}
        if not nobarrier:
            for d in self.pending[eng]:
                deps[id(d)] = d
            self.pending[eng] = []
        for k in reads:
            w = self.wr.get(k)
            if w is not None:
                deps[id(w)] = w
        for k in writes:
            w = self.wr.get(k)
            if w is not None:
                deps[id(w)] = w
            for r in self.rd.get(k, ()):
                deps[id(r)] = r
        for k in reads:
            self.rd.setdefault(k, []).append(ins)
        for k in writes:
            self.wr[k] = ins
            self.rd[k] = []
        if dma:
            j = self.ndma[eng]
            self.ndma[eng] = j + 1
            ins.slot = j % NDMASEM
            ins.val = 16 * (j // NDMASEM + 1)
            self.lastd[eng][ins.slot] = ins
        else:
            self.lastc[eng] = ins
        ds = []
        for d in deps.values():
            if d is ins:
                continue
            if (not dma) and (not d.dma) and d.eng == "pe" and eng == "pe":
                continue
            ds.append(d)
        ins.deps = ds
        st.append(ins)
        return ins

    def finalize(self):
        for e in ENGS:
            seen = {}
            seen_dma = set()
            prev = {}
            for ins in self.streams[e]:
                waits = []
                if ins.dma:
                    p = prev.get(ins.slot)
                    if p is not None and id(p) not in seen_dma:
                        waits.append(p)
                        seen_dma.add(id(p))
                    prev[ins.slot] = ins
                for d in ins.deps:
                    if d.dma:
                        if id(d) in seen_dma:
                            continue
                        seen_dma.add(id(d))
                        waits.append(d)
                    else:
                        if seen.get(d.eng, -1) >= d.idx:
                            continue
                        seen[d.eng] = d.idx
                        d.sig = True
                        waits.append(d)
                ins.waits = waits
        for e in ENGS:
            c = 0
            for ins in self.streams[e]:
                if not ins.dma and ins.sig:
                    c += 1
                    ins.val = c

    def emit(self, nc):
        self.finalize()
        final_waits = []
        for e in ENGS:
            last = {}
            for ins in self.streams[e]:
                if ins.dma:
                    last[ins.slot] = ins
            final_waits += list(last.values())
        with contextlib.ExitStack() as es:
            csem = {e: es.enter_context(nc.semaphore("c_" + e)) for e in ENGS}
            dsem = {e: [es.enter_context(nc.semaphore("d_%s%d" % (e, i))) for i in range(NDMASEM)]
                    for e in ENGS if self.ndma[e] > 0}
            block = es.enter_context(nc.Block())

            def run(e, eng):
                for ins in self.streams[e]:
                    for d in ins.waits:
                        if d.dma:
                            eng.wait_ge(dsem[d.eng][d.slot], d.val)
                        else:
                            eng.wait_ge(csem[d.eng], d.val)
                    r = ins.fn(eng)
                    if ins.dma:
                        r.then_inc(dsem[e][ins.slot], 16)
                    elif ins.sig:
                        r.then_inc(csem[e], 1)
                if e == "sp":
                    for d in final_waits:
                        eng.wait_ge(dsem[d.eng][d.slot], d.val)

            block.tensor(lambda eng: run("pe", eng))
            block.scalar(lambda eng: run("act", eng))
            block.vector(lambda eng: run("dve", eng))
            block.gpsimd(lambda eng: run("pool", eng))
            block.sync(lambda eng: run("sp", eng))


def build_program(units, n_layers=DEPTH, stop=None):
    nc = bass.Bass("TRN2", target_bir_lowering=False)
    T = Trk()
    es = contextlib.ExitStack()

    def din(name, shape):
        return nc.dram_tensor(name, list(shape), F32, kind="ExternalInput").ap()

    def dout(name, shape):
        return nc.dram_tensor(name, list(shape), F32, kind="ExternalOutput").ap()

    NB = 4
    I = {}
    I["x_prompt"] = din("x_prompt", (NB, SEQ, D))
    I["x_sample"] = din("x_sample", (NB, TS, D))
    I["cache_a_k"] = din("cache_a_k", (DEPTH, NB, 512, 256))
    I["cache_a_v"] = din("cache_a_v", (DEPTH, NB, 512, 256))
    I["cache_b_k"] = din("cache_b_k", (DEPTH, NB, PAST, 64))
    I["cache_b_v"] = din("cache_b_v", (DEPTH, NB, PAST, 64))
    I["cache_b_idx"] = din("cache_b_idx", (DEPTH, NB, PAST, 64))
    I["cache_c_k"] = din("cache_c_k", (DEPTH, NB, PAST, 512))
    I["cache_c_v"] = din("cache_c_v", (DEPTH, NB, PAST, 512))
    I["cache_mem_k"] = din("cache_mem_k", (DEPTH, NB, 256, 1024))
    I["cache_mem_v"] = din("cache_mem_v", (DEPTH, NB, 256, 1024))
    I["mem_prompt"] = din("mem_prompt", (NB, 256, D))
    I["ln_g"] = din("ln_g", (DEPTH, 4, D))
    I["ln_b"] = din("ln_b", (DEPTH, 4, D))
    I["ffn1_w_gu"] = din("ffn1_w_gu", (DEPTH, D, 2 * DFF))
    I["ffn1_w_d"] = din("ffn1_w_d", (DEPTH, DFF, D))
    I["ffn2_w_gu"] = din("ffn2_w_gu", (DEPTH, D, 2 * DFF))
    I["ffn2_w_d"] = din("ffn2_w_d", (DEPTH, DFF, D))
    I["w_in"] = din("w_in", (DEPTH, D, INC))
    I["biasA"] = din("biasA", (DEPTH, 128, 5 * 4 * 128))
    I["c_lambda"] = din("c_lambda", (DEPTH, 256))
    I["c_subln_g"] = din("c_subln_g", (DEPTH, 128))
    I["w_branch_a"] = din("w_branch_a", (DEPTH, 256, D))
    I["w_branch_b"] = din("w_branch_b", (DEPTH, 256, D))
    I["w_branch_c"] = din("w_branch_c", (DEPTH, 512, D))
    I["w_out"] = din("w_out", (DEPTH, D, D))
    I["w_mem_q"] = din("w_mem_q", (DEPTH, D, D))
    I["w_mem_k"] = din("w_mem_k", (DEPTH, D, D))
    I["w_mem_v"] = din("w_mem_v", (DEPTH, D, D))
    I["w_mem_o"] = din("w_mem_o", (DEPTH, D, D))
    I["rope_cs"] = din("rope_cs", (17 * 128, 16))

    O = {}
    O["y_p"] = dout("y_p", (NB, SEQ, D))
    O["y_s"] = dout("y_s", (NB, TS, D))
    O["a_k_p"] = dout("a_k_p", (DEPTH, NB, 512, 256))
    O["a_v_p"] = dout("a_v_p", (DEPTH, NB, 512, 256))
    O["b_k_p"] = dout("b_k_p", (DEPTH, NB, SEQ, 64))
    O["b_v_p"] = dout("b_v_p", (DEPTH, NB, SEQ, 64))
    O["b_i_p"] = dout("b_i_p", (DEPTH, NB, SEQ, 64))
    O["c_k_p"] = dout("c_k_p", (DEPTH, NB, SEQ, 512))
    O["c_v_p"] = dout("c_v_p", (DEPTH, NB, SEQ, 512))
    O["m_k_p"] = dout("m_k_p", (DEPTH, NB, 256, 1024))
    O["m_v_p"] = dout("m_v_p", (DEPTH, NB, 256, 1024))
    O["a_k_s"] = dout("a_k_s", (DEPTH, NB, TS, 256))
    O["a_v_s"] = dout("a_v_s", (DEPTH, NB, TS, 256))
    O["b_k_s"] = dout("b_k_s", (DEPTH, NB, TS, 64))
    O["b_v_s"] = dout("b_v_s", (DEPTH, NB, TS, 64))
    O["b_i_s"] = dout("b_i_s", (DEPTH, NB, TS, 64))
    O["c_k_s"] = dout("c_k_s", (DEPTH, NB, TS, 512))
    O["c_v_s"] = dout("c_v_s", (DEPTH, NB, TS, 512))
    xmid = nc.dram_tensor("xmid", [SEQ, D], F32, kind="Internal").ap()

    def sb(name, shape, dt=F32):
        return es.enter_context(nc.sbuf_tensor(name, list(shape), dt))

    NKT = 17
    akT2 = sb("akT2", [128, 2, NKT * 128], BF16)
    av_aug = sb("av_aug", [128, NKT, 4 * 65], BF16)
    bkT2 = sb("bkT2", [128, NKT * 128], BF16)
    bv_aug = sb("bv_aug", [128, NKT, 65], BF16)
    ikT2 = sb("ikT2", [128, NKT * 128], BF16)
    ckT2 = sb("ckT2", [128, 4, NKT * 128], BF16)
    cv_aug = sb("cv_aug", [128, NKT, 4 * 129], BF16)
    mkT = sb("mkT", [128, 8, 256], BF16)
    mv_aug = sb("mv_aug", [128, 2, 4 * 257], BF16)
    x_tm = sb("x_tm", [128, 4, D], F32)
    xT = sb("xT", [128, 8, 512], BF16)
    yT = sb("yT", [128, 8, 512], BF16)
    NWS = 4
    wslot = [sb("wslot%d" % i, [128, 4096], BF16) for i in range(NWS)]
    lng = sb("lng", [128, D], F32)
    lnb = sb("lnb", [128, D], F32)
    biasA = sb("biasA_sb", [128, 5 * 4 * 128], BF16)
    ropet = sb("ropet", [128, 17, 16], F32)
    ident = sb("ident", [128, 128], BF16)
    idf = sb("idf", [128, 128], F32)
    gsub = sb("gsub", [128, 128], F32)
    lamt = sb("lamt", [128, 256], F32)
    lams = sb("lams", [128, 8], F32)
    arena = sb("arena", [128, 52224], mybir.dt.uint8)
    small = sb("small", [128, 64], F32)
    stat = sb("stat", [128, 16], F32)

    def carve(off, shape, dt):
        n = int(np.prod(shape[1:])) * (2 if dt == BF16 else 4)
        v = arena[:, off:off + n].bitcast(dt)
        if len(shape) == 3:
            v = v.rearrange("p (a b) -> p a b", b=shape[2])
        return v, off + n

    xn, o = carve(0, [128, D], F32)
    xbf, o = carve(o, [128, D], BF16)
    base = o
    hT, o = carve(base, [128, NFC, 512], BF16)
    sgt = []
    for i in range(2):
        v, o = carve(o, [128, 512], F32)
        sgt.append(v)
    ffn_end = o
    o2 = base
    zst = []
    for i in range(2):
        v, o2 = carve(o2, [128, 512], F32)
        zst.append(v)
    z16 = []
    for i in range(2):
        v, o2 = carve(o2, [128, 512], BF16)
        z16.append(v)
    rtmp, o2 = carve(o2, [128, 4, 128], F32)
    aqT2, o2 = carve(o2, [128, 2, 512], BF16)
    bqT2, o2 = carve(o2, [128, 2, 512], BF16)
    iqT2, o2 = carve(o2, [128, 4, 512], BF16)
    cqT2, o2 = carve(o2, [128, 4, 512], BF16)
    score, o2 = carve(o2, [128, NKT * 128], F32)
    sel, o2 = carve(o2, [128, NKT * 128], BF16)
    relu_t = []
    for i in range(2):
        v, o2 = carve(o2, [128, 512], F32)
        relu_t.append(v)
    PT = []
    for i in range(2):
        v, o2 = carve(o2, [128, 512], BF16)
        PT.append(v)
    tmpA, o2 = carve(o2, [128, 512], F32)
    ycf, o2 = carve(o2, [128, 512], F32)
    yct, o2 = carve(o2, [128, 128], F32)
    y16, o2 = carve(o2, [128, 512], BF16)
    iw_sb, o2 = carve(o2, [128, 4, 8], F32)
    selT = []
    for i in range(2):
        v, o2 = carve(o2, [128, 128], BF16)
        selT.append(v)
    attn_end = o2
    o3 = base
    mT, o3 = carve(o3, [128, 8, 512], BF16)
    sgm = []
    for i in range(3):
        v, o3 = carve(o3, [128, 512], BF16)
        sgm.append(v)
    mt1, o3 = carve(o3, [128, 512], F32)
    mt2, o3 = carve(o3, [128, 512], F32)
    qmT, o3 = carve(o3, [128, 8, 512], BF16)
    om16, o3 = carve(o3, [128, D], BF16)
    assert o3 <= 37632 + base, o3
    assert max(ffn_end, attn_end, o3) <= 52224, (ffn_end, attn_end, o3)

    ps = [es.enter_context(nc.psum_tensor("ps%d" % i, [128, 512], F32)) for i in range(6)]
    pt = [es.enter_context(nc.psum_tensor("pt%d" % i, [128, 1024], BF16)) for i in range(2)]

    def PS(i):
        return ("ps", i)

    def PTK(i):
        return ("pt", i)

    wctr = [0]

    def wload(parts):
        s = wctr[0] % NWS
        wctr[0] += 1
        wt = wslot[s]
        for dstf, src in parts:
            T.add("pool", C("dma_start", out=dstf(wt), in_=src), writes=[("w", s)], dma=True, nobarrier=True)
        return wt, ("w", s)

    def w3(wt, k, c):
        return wt[:, 0:k * c].rearrange("p (k c) -> p k c", c=c)

    psrot = [0]

    def nextps(lo, hi):
        i = lo + psrot[0] % (hi - lo)
        psrot[0] += 1
        return i

    ptrot = [0]

    def nextpt():
        i = ptrot[0] % 2
        ptrot[0] += 1
        return i

    def transposes(src16, src_key, R, blocks, dst_fn, dst_keys, eng="act"):
        b = nextpt()
        for i, (c0, ncol) in enumerate(blocks):
            T.add("pe", C("transpose",
                pt[b][0:ncol, i * 128:i * 128 + R], src16[0:R, c0:c0 + ncol], ident[0:R, 0:R]),
                reads=[src_key, "ident"], writes=[PTK(b)])
        return b

    def evac(eng, out_ap, in_ap, reads, writes):
        if eng == "act":
            T.add("act", C("copy", out_ap, in_ap), reads=reads, writes=writes)
        else:
            T.add(eng, C("tensor_copy", out_ap, in_ap), reads=reads, writes=writes)

    T.add("pool", C("memset", idf[:], 0.0), writes=["idf"])
    T.add("pool", C("affine_select", out=idf[:], in_=idf[:], compare_op=ALU.not_equal, fill=1.0,
                                            base=0, pattern=[[-1, 128]], channel_multiplier=1),
          reads=["idf"], writes=["idf"])
    T.add("dve", C("tensor_copy", ident[:], idf[:]), reads=["idf"], writes=["ident"])
    T.add("sp", C("dma_start", out=ropet[:], in_=I["rope_cs"].rearrange("(t p) c -> p t c", p=128)),
          writes=["ropet"], dma=True)
    T.add("pool", C("memset", av_aug[:], 1.0), writes=[("av", k) for k in range(NKT)])
    T.add("pool", C("memset", bv_aug[:], 1.0), writes=[("bv", k) for k in range(NKT)])
    T.add("pool", C("memset", cv_aug[:], 1.0), writes=[("cv", k) for k in range(NKT)])
    T.add("pool", C("memset", mv_aug[:], 1.0), writes=[("mv", k) for k in range(2)])

    def layer_consts(l):
        lam_init = 0.8 - 0.6 * math.exp(-0.3 * l)
        T.add("pool", C("dma_start", out=biasA[:], in_=I["biasA"][l]), writes=["biasA"], dma=True)
        T.add("sp", C("dma_start", out=lamt[:], in_=I["c_lambda"][l].partition_broadcast(128)),
              writes=["lamt"], dma=True)
        T.add("sp", C("dma_start", out=gsub[:], in_=I["c_subln_g"][l].partition_broadcast(128)),
              writes=["gsub"], dma=True)
        T.add("dve", C("tensor_tensor", out=lamt[:, 0:64], in0=lamt[:, 0:64], in1=lamt[:, 64:128], op=ALU.mult),
              reads=["lamt"], writes=["lamt"])
        T.add("dve", C("tensor_tensor", out=lamt[:, 128:192], in0=lamt[:, 128:192], in1=lamt[:, 192:256], op=ALU.mult),
              reads=["lamt"], writes=["lamt"])
        T.add("dve", C("reduce_sum", out=lams[:, 0:1], in_=lamt[:, 0:64], axis=mybir.AxisListType.X),
              reads=["lamt"], writes=["lams"])
        T.add("dve", C("reduce_sum", out=lams[:, 1:2], in_=lamt[:, 128:192], axis=mybir.AxisListType.X),
              reads=["lamt"], writes=["lams"])
        T.add("act", C("activation", out=lams[:, 2:4], in_=lams[:, 0:2], func=AF.Exp),
              reads=["lams"], writes=["lams"])
        T.add("dve", C("tensor_tensor", out=lams[:, 4:5], in0=lams[:, 3:4], in1=lams[:, 2:3], op=ALU.subtract),
              reads=["lams"], writes=["lams"])
        T.add("dve", C("tensor_scalar", out=lams[:, 4:5], in0=lams[:, 4:5], scalar1=-lam_init, scalar2=None,
                                               op0=ALU.add), reads=["lams"], writes=["lams"])
        T.add("dve", C("tensor_scalar", out=gsub[:], in0=gsub[:], scalar1=(1.0 - lam_init), scalar2=None,
                                               op0=ALU.mult), reads=["gsub"], writes=["gsub"])

    def layer_norm(l, idx, tt, R, dst_kind, dst_ap):
        xk = ("x_tm", tt)
        src = x_tm[0:R, tt, :]
        T.add("dve", C("bn_stats", stat[0:R, 0:6], src[:, 0:512]), reads=[xk], writes=["stat"])
        T.add("dve", C("bn_stats", stat[0:R, 6:12], src[:, 512:1024]), reads=[xk], writes=["stat"])
        T.add("dve", C("bn_aggr", stat[0:R, 12:14], stat[0:R, 0:12]), reads=["stat"], writes=["stat"])
        T.add("dve", C("tensor_scalar", out=stat[0:R, 14:15], in0=stat[0:R, 13:14], scalar1=LN_EPS, scalar2=None,
                                               op0=ALU.add), reads=["stat"], writes=["stat"])
        T.add("act", C("activation", out=stat[0:R, 14:15], in_=stat[0:R, 14:15], func=AF.Sqrt),
              reads=["stat"], writes=["stat"])
        T.add("dve", C("reciprocal", out=stat[0:R, 15:16], in_=stat[0:R, 14:15]), reads=["stat"], writes=["stat"])
        T.add("dve", C("tensor_scalar", out=xn[0:R, :], in0=src, scalar1=stat[0:R, 12:13], scalar2=stat[0:R, 15:16],
                                               op0=ALU.subtract, op1=ALU.mult), reads=[xk, "stat"], writes=["xn"])
        T.add("pool", C("tensor_tensor", out=xn[0:R, :], in0=xn[0:R, :], in1=lng[0:R, :], op=ALU.mult),
              reads=["xn", "lng"], writes=["xn"])
        T.add("pool", C("tensor_tensor", out=xn[0:R, :], in0=xn[0:R, :], in1=lnb[0:R, :], op=ALU.add),
              reads=["xn", "lnb"], writes=["xn"])
        T.add("act", C("mul", x_tm[0:R, tt, :], xn[0:R, :], ALPHA), reads=["xn"], writes=[xk])
        T.add("dve", C("tensor_copy", xbf[0:R, :], xn[0:R, :]), reads=["xn"], writes=["xbf"])
        if dst_kind == "dram":
            T.add("sp", C("dma_start", out=dst_ap[0], in_=xn[0:R, :]), reads=["xn"], writes=[dst_ap[1]], dma=True)
        b = transposes(xbf, "xbf", R, [(k * 128, 128) for k in range(8)], None, None)
        evac("act", xT[:, :, tt * R:(tt + 1) * R],
             pt[b][:, :].rearrange("p (k c) -> p k c", c=128)[:, :, 0:R],
             reads=[PTK(b)], writes=[("xT", tt)])

    def load_ln(l, idx):
        T.add("sp", C("dma_start", out=lng[:], in_=I["ln_g"][l, idx].partition_broadcast(128)),
              writes=["lng"], dma=True)
        T.add("sp", C("dma_start", out=lnb[:], in_=I["ln_b"][l, idx].partition_broadcast(128)),
              writes=["lnb"], dma=True)

    def ffn(l, which, nt, R, ln_idx, final_dst):
        N = nt * R
        wgu = I["ffn%d_w_gu" % which][l]
        wd = I["ffn%d_w_d" % which][l]
        load_ln(l, ln_idx)
        for j in range(NFC):
            wt, wk = wload([
                (lambda w: w3(w, 8, 256)[:, :, 0:128], wgu[:, j * 128:(j + 1) * 128].rearrange("(k p) c -> p k c", p=128)),
                (lambda w: w3(w, 8, 256)[:, :, 128:256],
                 wgu[:, DFF + j * 128:DFF + (j + 1) * 128].rearrange("(k p) c -> p k c", p=128)),
            ])
            wv = w3(wt, 8, 256)
            st_ = j % 2
            gb, ub = 2 * st_, 2 * st_ + 1
            for k in range(8):
                T.add("pe", C("matmul", ps[gb][:, 0:N], wv[:, k, 0:128], xT[:, k, 0:N], start=(k == 0), stop=(k == 7)),
                      reads=[wk] + [("xT", t) for t in range(nt)], writes=[PS(gb)])
            for k in range(8):
                T.add("pe", C("matmul", ps[ub][:, 0:N], wv[:, k, 128:256], xT[:, k, 0:N], start=(k == 0), stop=(k == 7)),
                      reads=[wk] + [("xT", t) for t in range(nt)], writes=[PS(ub)])
            T.add("act", C("activation", out=sgt[st_][:, 0:N], in_=ps[gb][:, 0:N], func=AF.Silu),
                  reads=[PS(gb)], writes=[("sgt", st_)])
            T.add("dve", C("tensor_tensor", out=hT[:, j, 0:N], in0=sgt[st_][:, 0:N], in1=ps[ub][:, 0:N], op=ALU.mult),
                  reads=[("sgt", st_), PS(ub)], writes=[("hT", j)])
        groups = [(0, 8), (8, 8), (16, 6)]
        for half in range(2):
            for (j0, nj) in groups:
                wt, wk = wload([(lambda w, nj=nj: w3(w, nj, 512),
                                 wd[j0 * 128:(j0 + nj) * 128, half * 512:(half + 1) * 512].rearrange("(j p) c -> p j c", p=128))])
                wv = w3(wt, nj, 512)
                for jj in range(nj):
                    j = j0 + jj
                    for tt in range(nt):
                        bk = 2 + tt
                        T.add("pe", C("matmul",
                            ps[bk][0:R, :], hT[:, j, tt * R:(tt + 1) * R], wv[:, jj, :], start=(j == 0), stop=(j == NFC - 1)),
                            reads=[wk, ("hT", j)], writes=[PS(bk)])
            for tt in range(nt):
                bk = 2 + tt
                T.add("dve", C("scalar_tensor_tensor",
                    out=x_tm[0:R, tt, half * 512:(half + 1) * 512], in0=ps[bk][0:R, :], scalar=0.5,
                    in1=x_tm[0:R, tt, half * 512:(half + 1) * 512], op0=ALU.mult, op1=ALU.add),
                    reads=[PS(bk), ("x_tm", tt)], writes=[("x_tm", tt)])
        for tt in range(nt):
            if final_dst is not None:
                layer_norm(l, ln_idx, tt, R, "dram", final_dst(tt))
            else:
                layer_norm(l, ln_idx, tt, R, None, None)

    def rope(stage, skey, R, c0, nh, gt):
        import os
        if os.environ.get("KDBG_NOROPE"):
            return
        v = stage[0:R, c0:c0 + nh * 64].rearrange("p (h d) -> p h d", d=64)
        x1 = v[:, :, 0:8]
        x2 = v[:, :, 8:16]
        cos = ropet[0:R, gt, 0:8].unsqueeze(1).to_broadcast([R, nh, 8])
        sin = ropet[0:R, gt, 8:16].unsqueeze(1).to_broadcast([R, nh, 8])
        t = [rtmp[0:R, i, 0:nh * 8].rearrange("p (h d) -> p h d", d=8) for i in range(4)]
        T.add("dve", C("tensor_tensor", out=t[0], in0=x1, in1=cos, op=ALU.mult), reads=[skey, "ropet"], writes=["rtmp"])
        T.add("dve", C("tensor_tensor", out=t[1], in0=x2, in1=sin, op=ALU.mult), reads=[skey, "ropet"], writes=["rtmp"])
        T.add("dve", C("tensor_tensor", out=t[2], in0=x2, in1=cos, op=ALU.mult), reads=[skey, "ropet"], writes=["rtmp"])
        T.add("dve", C("tensor_tensor", out=t[3], in0=x1, in1=sin, op=ALU.mult), reads=[skey, "ropet"], writes=["rtmp"])
        T.add("dve", C("tensor_tensor", out=x1, in0=t[0], in1=t[1], op=ALU.subtract), reads=["rtmp"], writes=[skey])
        T.add("dve", C("tensor_tensor", out=x2, in0=t[2], in1=t[3], op=ALU.add), reads=["rtmp"], writes=[skey])

    def win_phase(l, kind, b, tb, nt, R):
        win = I["w_in"][l]
        sfx = "_p" if kind == "p" else "_s"
        zr = [0]
        import os
        for bi, (c0, ncol) in enumerate(ZBLK):
            if bi > int(os.environ.get("KDBG_WIN", "99")):
                break
            wt, wk = wload([(lambda w, ncol=ncol: w3(w, 8, ncol),
                             win[:, c0:c0 + ncol].rearrange("(k p) c -> p k c", p=128))])
            wv = w3(wt, 8, ncol)
            wmode = os.environ.get("KDBG_WINMODE", "all")
            if wmode == "load":
                T.add("dve", C("tensor_copy", small[:, 0:8], wv[:, 0, 0:8]), reads=[wk], writes=["small"])
                continue
            for tt in range(nt):
                gt = tb * 4 + tt if kind == "p" else 16
                at = gt if kind == "p" else 4
                tok0 = gt * 128 if kind == "p" else 0
                pb = nextps(0, 2)
                for k in range(8):
                    T.add("pe", C("matmul", ps[pb][0:R, 0:ncol], xT[:, k, tt * R:(tt + 1) * R], wv[:, k, 0:ncol],
                                                                    start=(k == 0), stop=(k == 7)),
                          reads=[wk, ("xT", tt)], writes=[PS(pb)])
                si = zr[0] % 2
                zr[0] += 1
                st_, sk = zst[si], ("zst", si)
                z6, zk = z16[si], ("z16", si)
                T.add("act", C("copy", st_[0:R, 0:ncol], ps[pb][0:R, 0:ncol]),
                      reads=[PS(pb)], writes=[sk])
                if wmode == "mm":
                    continue

                def outd(name, col, n, rows=None, st_=st_, sk=sk):
                    dst = O[name + sfx][l, b, tok0:tok0 + R, :] if rows is None else rows
                    T.add("sp", C("dma_start", out=dst, in_=st_[0:R, col:col + n]), reads=[sk], dma=True)

                def cast(dc, sc, n, st_=st_, sk=sk, z6=z6, zk=zk, eng="dve"):
                    T.add(eng, C("tensor_copy", z6[0:R, dc:dc + n], st_[0:R, sc:sc + n]), reads=[sk], writes=[zk])

                ksl = slice(gt * 128, gt * 128 + R)
                asl = slice(at * 128, at * 128 + R)
                qsl = slice(tt * R, (tt + 1) * R)
                if bi == 0:
                    if kind == "s":
                        outd("a_k", 256, 256)
                    elif gt >= 12:
                        outd("a_k", 256, 256, rows=O["a_k_p"][l, b, (gt - 12) * 128:(gt - 11) * 128, :])
                    cast(0, 0, 512)
                    pb2 = transposes(z6, zk, R, [(0, 128), (128, 128), (256, 128), (384, 128)], None, None)
                    pv = pt[pb2][:, :].rearrange("p (k c) -> p k c", c=128)
                    evac("act", aqT2[:, :, qsl], pv[:, 0:2, 0:R], [PTK(pb2)], [("aqT", tt)])
                    evac("act", akT2[:, :, asl], pv[:, 2:4, 0:R], [PTK(pb2)], [("akT", at)])
                elif bi == 1:
                    if kind == "s":
                        outd("a_v", 0, 256)
                    elif gt >= 12:
                        outd("a_v", 0, 256, rows=O["a_v_p"][l, b, (gt - 12) * 128:(gt - 11) * 128, :])
                    T.add("dve", C("tensor_copy",
                        av_aug[0:R, at, :].rearrange("p (h d) -> p h d", d=65)[:, :, 0:64],
                        st_[0:R, 0:256].rearrange("p (h d) -> p h d", d=64)), reads=[sk], writes=[("av", at)])
                    rope(st_, sk, R, 256, 4, gt)
                    cast(0, 256, 256)
                    pb2 = transposes(z6, zk, R, [(0, 128), (128, 128)], None, None)
                    pv = pt[pb2][:, :].rearrange("p (k c) -> p k c", c=128)
                    evac("act", bqT2[:, :, qsl], pv[:, 0:2, 0:R], [PTK(pb2)], [("bqT", tt)])
                elif bi == 2:
                    rope(st_, sk, R, 0, 1, gt)
                    rope(st_, sk, R, 128, 6, gt)
                    outd("b_k", 0, 64)
                    outd("b_v", 64, 64)
                    T.add("dve", C("tensor_copy", bv_aug[0:R, gt, 0:64], st_[0:R, 64:128]),
                          reads=[sk], writes=[("bv", gt)])
                    cast(0, 0, 64)
                    cast(64, 0, 64)
                    cast(128, 128, 384)
                    pb2 = transposes(z6, zk, R, [(0, 128), (128, 128), (256, 128), (384, 128)], None, None)
                    pv = pt[pb2][:, :].rearrange("p (k c) -> p k c", c=128)
                    evac("act", bkT2[:, ksl], pt[pb2][:, 0:R], [PTK(pb2)], [("bkT", gt)])
                    evac("act", iqT2[:, 0:3, qsl], pv[:, 1:4, 0:R], [PTK(pb2)], [("iqT", tt)])
                elif bi == 3:
                    rope(st_, sk, R, 0, 3, gt)
                    outd("b_i", 128, 64)
                    T.add("act", C("mul", iw_sb[0:R, tt, :], st_[0:R, 192:200], 8.0 ** -0.5),
                          reads=[sk], writes=[("iw", tt)])
                    cast(0, 0, 192)
                    cast(192, 128, 64)
                    pb2 = transposes(z6, zk, R, [(0, 128), (128, 128)], None, None)
                    evac("act", iqT2[:, 3, qsl], pt[pb2][:, 0:R], [PTK(pb2)], [("iqT", tt)])
                    evac("act", ikT2[:, ksl], pt[pb2][:, 128:128 + R], [PTK(pb2)], [("ikT", gt)])
                elif bi == 4:
                    rope(st_, sk, R, 0, 8, gt)
                    cast(0, 0, 512)
                    pb2 = transposes(z6, zk, R, [(0, 128), (128, 128), (256, 128), (384, 128)], None, None)
                    pv = pt[pb2][:, :].rearrange("p (k c) -> p k c", c=128)
                    evac("act", cqT2[:, :, qsl], pv[:, 0:4, 0:R], [PTK(pb2)], [("cqT", tt)])
                elif bi == 5:
                    rope(st_, sk, R, 0, 8, gt)
                    outd("c_k", 0, 512)
                    cast(0, 0, 512)
                    pb2 = transposes(z6, zk, R, [(0, 128), (128, 128), (256, 128), (384, 128)], None, None)
                    pv = pt[pb2][:, :].rearrange("p (k c) -> p k c", c=128)
                    evac("act", ckT2[:, :, ksl], pv[:, 0:4, 0:R], [PTK(pb2)], [("ckT", gt)])
                else:
                    outd("c_v", 0, 512)
                    T.add("dve", C("tensor_copy",
                        cv_aug[0:R, gt, :].rearrange("p (h d) -> p h d", d=129)[:, :, 0:128],
                        st_[0:R, 0:512].rearrange("p (h d) -> p h d", d=128)), reads=[sk], writes=[("cv", gt)])

    def ingest_cache(l, b):
        for kt in range(16):
            rows = slice(kt * 128, (kt + 1) * 128)
            ksl = rows
            zi = kt % 2
            z6, zk = z16[zi], ("z16", zi)
            T.add("pool", C("dma_start", out=z6[:, 0:512], in_=I["cache_c_k"][l, b, rows, :]), writes=[zk], dma=True)
            pb2 = transposes(z6, zk, 128, [(0, 128), (128, 128), (256, 128), (384, 128)], None, None)
            pv = pt[pb2][:, :].rearrange("p (k c) -> p k c", c=128)
            evac("act", ckT2[:, :, ksl], pv[:, 0:4, :], [PTK(pb2)], [("ckT", kt)])
            T.add("pool", C("dma_start",
                out=cv_aug[:, kt, :].rearrange("p (h d) -> p h d", d=129)[:, :, 0:128],
                in_=I["cache_c_v"][l, b, rows, :].rearrange("p (h d) -> p h d", d=128)), writes=[("cv", kt)], dma=True)
            T.add("pool", C("dma_start", out=bv_aug[:, kt, 0:64], in_=I["cache_b_v"][l, b, rows, :]),
                  writes=[("bv", kt)], dma=True)
            zi2 = (kt + 1) % 2
            z6b, zkb = z16[zi2], ("z16", zi2)
            for c in range(2):
                T.add("pool", C("dma_start", out=z6b[:, c * 64:(c + 1) * 64], in_=I["cache_b_k"][l, b, rows, :]),
                      writes=[zkb], dma=True)
                T.add("pool", C("dma_start", out=z6b[:, 128 + c * 64:128 + (c + 1) * 64],
                                                                 in_=I["cache_b_idx"][l, b, rows, :]), writes=[zkb], dma=True)
            pb3 = transposes(z6b, zkb, 128, [(0, 128), (128, 128)], None, None)
            evac("act", bkT2[:, ksl], pt[pb3][:, 0:128], [PTK(pb3)], [("bkT", kt)])
            evac("act", ikT2[:, ksl], pt[pb3][:, 128:256], [PTK(pb3)], [("ikT", kt)])
        for kt in range(4):
            rows = slice(kt * 128, (kt + 1) * 128)
            zi = kt % 2
            z6, zk = z16[zi], ("z16", zi)
            T.add("pool", C("dma_start", out=z6[:, 0:256], in_=I["cache_a_k"][l, b, rows, :]), writes=[zk], dma=True)
            pb2 = transposes(z6, zk, 128, [(0, 128), (128, 128)], None, None)
            pv = pt[pb2][:, :].rearrange("p (k c) -> p k c", c=128)
            evac("act", akT2[:, :, rows], pv[:, 0:2, :], [PTK(pb2)], [("akT", kt)])
            T.add("pool", C("dma_start",
                out=av_aug[:, kt, :].rearrange("p (h d) -> p h d", d=65)[:, :, 0:64],
                in_=I["cache_a_v"][l, b, rows, :].rearrange("p (h d) -> p h d", d=64)), writes=[("av", kt)], dma=True)

    def mem_prep(l, kind, b):
        if kind == "s":
            for kt in range(2):
                rows = slice(kt * 128, (kt + 1) * 128)
                T.add("pool", C("dma_start", out=om16[:, :], in_=I["cache_mem_k"][l, b, rows, :]), writes=["om16"], dma=True)
                pb2 = transposes(om16, "om16", 128, [(k * 128, 128) for k in range(8)], None, None)
                evac("act", mkT[:, :, rows], pt[pb2][:, :].rearrange("p (k c) -> p k c", c=128), [PTK(pb2)], [("mkT", kt)])
                T.add("pool", C("dma_start",
                    out=mv_aug[:, kt, :].rearrange("p (h d) -> p h d", d=257)[:, :, 0:256],
                    in_=I["cache_mem_v"][l, b, rows, :].rearrange("p (h d) -> p h d", d=256)), writes=[("mv", kt)], dma=True)
            return
        for kt in range(2):
            rows = slice(kt * 128, (kt + 1) * 128)
            T.add("pool", C("dma_start", out=om16[:, :], in_=I["mem_prompt"][b, rows, :]), writes=["om16"], dma=True)
            pb2 = transposes(om16, "om16", 128, [(k * 128, 128) for k in range(8)], None, None)
            evac("act", qmT[:, :, rows], pt[pb2][:, :].rearrange("p (k c) -> p k c", c=128), [PTK(pb2)], [("qmT", kt)])
        for wi, wname in enumerate(("w_mem_k", "w_mem_v")):
            for half in range(2):
                wt, wk = wload([(lambda w: w3(w, 8, 512),
                                 I[wname][l][:, half * 512:(half + 1) * 512].rearrange("(k p) c -> p k c", p=128))])
                wv = w3(wt, 8, 512)
                for kt in range(2):
                    rows = slice(kt * 128, (kt + 1) * 128)
                    pb = nextps(0, 2)
                    for k in range(8):
                        T.add("pe", C("matmul", ps[pb][:, :], qmT[:, k, rows], wv[:, k, :], start=(k == 0), stop=(k == 7)),
                              reads=[wk, ("qmT", kt)], writes=[PS(pb)])
                    T.add("act", C("copy", mt1[:, :], ps[pb][:, :]), reads=[PS(pb)], writes=["mt1"])
                    dst = O["m_k_p" if wi == 0 else "m_v_p"][l, b, rows, half * 512:(half + 1) * 512]
                    T.add("sp", C("dma_start", out=dst, in_=mt1[:, :]), reads=["mt1"], dma=True)
                    if wi == 0:
                        T.add("dve", C("tensor_copy", y16[:, :], mt1[:, :]), reads=["mt1"], writes=["y16"])
                        pb2 = transposes(y16, "y16", 128, [(k * 128, 128) for k in range(4)], None, None)
                        evac("act", mkT[:, half * 4:half * 4 + 4, rows],
                             pt[pb2][:, 0:512].rearrange("p (k c) -> p k c", c=128), [PTK(pb2)], [("mkT", kt)])
                    else:
                        T.add("dve", C("tensor_copy",
                            mv_aug[:, kt, :].rearrange("p (h d) -> p h d", d=257)[:, 2 * half:2 * half + 2, 0:256],
                            mt1[:, :].rearrange("p (h d) -> p h d", d=256)), reads=["mt1"], writes=[("mv", kt)])

    def norm_heads(acc, acckey, R, nh, dh, out16, okey, cols):
        av = acc[0:R, 0:nh * (dh + 1)].rearrange("p (h d) -> p h d", d=dh + 1)
        T.add("dve", C("reciprocal", out=small[0:R, 0:nh], in_=av[:, :, dh]), reads=[acckey], writes=["small"])
        for s in range(nh):
            T.add("dve", C("tensor_scalar", out=out16[0:R, cols[s]:cols[s] + dh], in0=av[:, s, 0:dh],
                                                        scalar1=small[0:R, s:s + 1], scalar2=None, op0=ALU.mult),
                  reads=[acckey, "small"], writes=[okey])

    def attn_phase(l, kind, tb, nt, R):
        N = nt * R
        prompt = kind == "p"
        nkt_blk = (tb * 4 + nt) if prompt else 17

        def c_head(h):
            started = set()
            for kt in range(nkt_blk):
                ksz = 128 if (prompt or kt < 16) else 64
                tt0 = max(0, kt - tb * 4) if prompt else 0
                cs0 = tt0 * R
                for i in range(2):
                    sbk = nextps(0, 2)
                    pti = sbk
                    T.add("pe", C("matmul",
                        ps[sbk][0:ksz, cs0:N], ckT2[i * 64:(i + 1) * 64, h, kt * 128:kt * 128 + ksz],
                        cqT2[i * 64:(i + 1) * 64, h, cs0:N], start=True, stop=True),
                        reads=[("ckT", kt)] + [("cqT", t) for t in range(tt0, nt)], writes=[PS(sbk)])
                    T.add("act", C("activation", out=PT[pti][0:ksz, cs0:N], in_=ps[sbk][0:ksz, cs0:N],
                                                                          func=AF.Exp, scale=0.125),
                          reads=[PS(sbk)], writes=[("PT", pti)])
                    if prompt and kt >= tb * 4:
                        T.add("pool", C("memset", PT[pti][64:128, cs0:cs0 + 64], 0.0),
                              reads=[], writes=[("PT", pti)])
                    for tt in range(tt0, nt):
                        a = tt * 2 + i
                        bk, col = 2 + a // 3, (a % 3) * 129
                        last = (tb * 4 + tt) if prompt else 16
                        first = bk not in started
                        started.add(bk)
                        T.add("pe", C("matmul",
                            ps[bk][0:R, col:col + 129], PT[pti][0:ksz, tt * R:(tt + 1) * R],
                            cv_aug[0:ksz, kt, h * 129:(h + 1) * 129], start=first, stop=first, skip_group_check=(not first)),
                            reads=[("PT", pti), ("cv", kt)], writes=[PS(bk)])
            for tt in range(nt):
                a0, a1 = tt * 2, tt * 2 + 1
                b0, c0 = 2 + a0 // 3, (a0 % 3) * 129
                b1, c1 = 2 + a1 // 3, (a1 % 3) * 129
                T.add("dve", C("reciprocal", out=small[0:R, 0:1], in_=ps[b0][0:R, c0 + 128:c0 + 129]),
                      reads=[PS(b0)], writes=["small"])
                T.add("dve", C("reciprocal", out=small[0:R, 1:2], in_=ps[b1][0:R, c1 + 128:c1 + 129]),
                      reads=[PS(b1)], writes=["small"])
                T.add("dve", C("tensor_tensor", out=small[0:R, 1:2], in0=small[0:R, 1:2], in1=lams[0:R, 4:5], op=ALU.mult),
                      reads=["small", "lams"], writes=["small"])
                T.add("dve", C("tensor_scalar", out=yct[0:R, :], in0=ps[b0][0:R, c0:c0 + 128], scalar1=small[0:R, 0:1],
                                                       scalar2=None, op0=ALU.mult), reads=[PS(b0), "small"], writes=["yct"])
                T.add("dve", C("scalar_tensor_tensor", out=ycf[0:R, 0:128], in0=ps[b1][0:R, c1:c1 + 128],
                                                                     scalar=small[0:R, 1:2], in1=yct[0:R, :],
                                                                     op0=ALU.mult, op1=ALU.add),
                      reads=[PS(b1), "small", "yct"], writes=["ycf"])
                T.add("act", C("activation", out=yct[0:R, :], in_=ycf[0:R, 0:128], func=AF.Square,
                                                    accum_out=small[0:R, 2:3]), reads=["ycf"], writes=["yct", "small"])
                T.add("dve", C("tensor_scalar", out=small[0:R, 2:3], in0=small[0:R, 2:3], scalar1=1.0 / 128.0,
                                                       scalar2=1e-5, op0=ALU.mult, op1=ALU.add), reads=["small"], writes=["small"])
                T.add("act", C("activation", out=small[0:R, 2:3], in_=small[0:R, 2:3], func=AF.Sqrt),
                      reads=["small"], writes=["small"])
                T.add("dve", C("reciprocal", out=small[0:R, 3:4], in_=small[0:R, 2:3]), reads=["small"], writes=["small"])
                T.add("dve", C("scalar_tensor_tensor", out=y16[0:R, 0:128], in0=ycf[0:R, 0:128],
                                                                     scalar=small[0:R, 3:4], in1=gsub[0:R, :],
                                                                     op0=ALU.mult, op1=ALU.mult),
                      reads=["ycf", "small", "gsub"], writes=["y16"])
                pb2 = transposes(y16, "y16", R, [(0, 128)], None, None)
                evac("act", yT[:, 4 + h, tt * R:(tt + 1) * R], pt[pb2][:, 0:R], [PTK(pb2)], [("yT", tt)])

        import os
        def b_scores(tt):
            gt = tb * 4 + tt if prompt else 16
            nkt = gt + 1 if prompt else 17
            Wk = (gt + 1) * 128 if prompt else PAST + TS
            qsl = slice(tt * R, (tt + 1) * R)
            nkb = (Wk + 511) // 512
            for kb in range(nkb):
                w = min(512, Wk - kb * 512)
                ksl = slice(kb * 512, kb * 512 + w)
                for ih in range(8):
                    hf, c = ih % 2, ih // 2
                    pb = nextps(0, 2)
                    T.add("pe", C("matmul", ps[pb][0:R, 0:w], iqT2[hf * 64:(hf + 1) * 64, c, qsl],
                                                                      ikT2[hf * 64:(hf + 1) * 64, ksl], start=True, stop=True),
                          reads=[("iqT", tt)] + [("ikT", t) for t in range(kb * 4, min(nkt, kb * 4 + 4))], writes=[PS(pb)])
                    ri = pb
                    T.add("act", C("activation", out=relu_t[ri][0:R, 0:w], in_=ps[pb][0:R, 0:w], func=AF.Relu),
                          reads=[PS(pb)], writes=[("relu", ri)])
                    if ih == 0:
                        T.add("dve", C("tensor_scalar", out=score[0:R, ksl], in0=relu_t[ri][0:R, 0:w],
                                                                      scalar1=iw_sb[0:R, tt, 0:1], scalar2=None, op0=ALU.mult),
                              reads=[("relu", ri), ("iw", tt)], writes=["score"])
                    else:
                        T.add("dve", C("scalar_tensor_tensor",
                            out=score[0:R, ksl], in0=relu_t[ri][0:R, 0:w], scalar=iw_sb[0:R, tt, ih:ih + 1],
                            in1=score[0:R, ksl], op0=ALU.mult, op1=ALU.add),
                            reads=[("relu", ri), ("iw", tt), "score"], writes=["score"])
            if prompt:
                T.add("dve", C("memset", score[0:64, Wk - 64:Wk], -1e30), reads=[], writes=["score"])

        def b_topk(tt):
            gt = tb * 4 + tt if prompt else 16
            nkt = gt + 1 if prompt else 17
            Wk = (gt + 1) * 128 if prompt else PAST + TS
            qsl = slice(tt * R, (tt + 1) * R)
            if Wk > 256:
                for r in range(32):
                    T.add("dve", C("max", out=small[0:R, 8:16], in_=score[0:R, 0:Wk]), reads=["score"], writes=["small8"])
                    T.add("dve", C("match_replace", out=score[0:R, 0:Wk], in_to_replace=small[0:R, 8:16],
                                                           in_values=score[0:R, 0:Wk], imm_value=-1e30),
                          reads=["score", "small8"], writes=["score"])
                T.add("dve", C("tensor_single_scalar", out=sel[0:R, 0:Wk], in_=score[0:R, 0:Wk], scalar=-1e29, op=ALU.is_lt),
                      reads=["score"], writes=["sel"])
                if prompt:
                    T.add("dve", C("memset", sel[0:64, Wk - 64:Wk], 0.0), reads=[], writes=["sel"])
            else:
                T.add("dve", C("tensor_single_scalar", out=sel[0:R, 0:Wk], in_=score[0:R, 0:Wk], scalar=-1e29, op=ALU.is_gt),
                      reads=["score"], writes=["sel"])

        def b_attn(tt):
            gt = tb * 4 + tt if prompt else 16
            nkt = gt + 1 if prompt else 17
            Wk = (gt + 1) * 128 if prompt else PAST + TS
            qsl = slice(tt * R, (tt + 1) * R)
            accB = 5
            for kt in range(nkt):
                ksz = 128 if (prompt or kt < 16) else 64
                kcs = slice(kt * 128, kt * 128 + ksz)
                pbm = transposes(sel, "sel", R, [(kt * 128, ksz)], None, None)
                evac("act", selT[pbm][0:ksz, 0:R], pt[pbm][0:ksz, 0:R], [PTK(pbm)], [("selT", pbm)])
                bmode = int(os.environ.get("KDBG_B", "9"))
                if bmode <= 1:
                    continue
                for hf in range(2):
                    for c in range(2):
                        T.add("pe", C("matmul", ps[hf][0:ksz, c * R:(c + 1) * R],
                                      bkT2[hf * 64:(hf + 1) * 64, kcs],
                                      bqT2[hf * 64:(hf + 1) * 64, c, qsl], start=True, stop=True),
                              reads=[("bkT", kt), ("bqT", tt)], writes=[PS(hf)])
                pti = kt % 2
                for hf in range(2):
                    T.add("act", C("activation", out=PT[pti][0:ksz, hf * 2 * R:(hf + 1) * 2 * R], in_=ps[hf][0:ksz, 0:2 * R],
                                   func=AF.Exp, scale=0.125), reads=[PS(hf)], writes=[("PT", pti)])
                if bmode <= 2:
                    continue
                T.add("dve", C("tensor_tensor",
                    out=PT[pti][0:ksz, 0:4 * R].rearrange("p (h q) -> p h q", q=R),
                    in0=PT[pti][0:ksz, 0:4 * R].rearrange("p (h q) -> p h q", q=R),
                    in1=selT[pbm][0:ksz, 0:R].unsqueeze(1).to_broadcast([ksz, 4, R]), op=ALU.mult),
                    reads=[("PT", pti), ("selT", pbm)], writes=[("PT", pti)])
                if bmode <= 3:
                    continue
                for s in range(4):
                    T.add("pe", C("matmul", ps[accB][0:R, s * 65:(s + 1) * 65], PT[pti][0:ksz, s * R:(s + 1) * R],
                                                                       bv_aug[0:ksz, kt, :], start=(kt == 0 and s == 0), stop=(kt == 0 and s == 0),
                                                                       skip_group_check=not (kt == 0 and s == 0)),
                          reads=[("PT", pti), ("bv", kt)], writes=[PS(accB)])
            norm_heads(ps[accB], PS(accB), R, 4, 64, y16, "y16", [0, 128, 64, 192])
            pb2 = transposes(y16, "y16", R, [(0, 128), (128, 128)], None, None)
            evac("act", yT[:, 2:4, qsl], pt[pb2][:, 0:256].rearrange("p (k c) -> p k c", c=128)[:, :, 0:R], [PTK(pb2)], [("yT", tt)])

        def a_attn(tt):
            gt = tb * 4 + tt if prompt else 16
            nkt = gt + 1 if prompt else 17
            Wk = (gt + 1) * 128 if prompt else PAST + TS
            qsl = slice(tt * R, (tt + 1) * R)
            if prompt:
                akts = [(kt, kt - (gt - 4), 128) for kt in range(max(0, gt - 4), gt + 1)]
            else:
                akts = [(kt, kt, 128 if kt < 4 else 64) for kt in range(5)]
            accA = 5
            for n_, (kt, ti, ksz) in enumerate(akts):
                kcs = slice(kt * 128, kt * 128 + ksz)
                for hf in range(2):
                    for c in range(2):
                        T.add("pe", C("matmul", ps[hf][0:ksz, c * R:(c + 1) * R],
                                      akT2[hf * 64:(hf + 1) * 64, c, kcs],
                                      aqT2[hf * 64:(hf + 1) * 64, c, qsl], start=True, stop=True),
                              reads=[("akT", kt), ("aqT", tt)], writes=[PS(hf)])
                bfull = biasA[0:ksz, ti * 512:(ti + 1) * 512].rearrange("p (c f q) -> p c f q", f=2, q=128)
                for hf in range(2):
                    T.add("dve", C("scalar_tensor_tensor",
                                   out=tmpA[0:ksz, hf * 2 * R:(hf + 1) * 2 * R].rearrange("p (c q) -> p c q", q=R),
                                   in0=ps[hf][0:ksz, 0:2 * R].rearrange("p (c q) -> p c q", q=R), scalar=0.125,
                                   in1=bfull[:, :, hf, 0:R], op0=ALU.mult, op1=ALU.add),
                          reads=[PS(hf), "biasA"], writes=["tmpA"])
                pti = n_ % 2
                T.add("act", C("activation", out=PT[pti][0:ksz, 0:4 * R], in_=tmpA[0:ksz, 0:4 * R], func=AF.Exp),
                      reads=["tmpA"], writes=[("PT", pti)])
                for s_ in range(4):
                    h = [0, 2, 1, 3][s_]
                    T.add("pe", C("matmul", ps[accA][0:R, s_ * 65:(s_ + 1) * 65], PT[pti][0:ksz, s_ * R:(s_ + 1) * R],
                                  av_aug[0:ksz, kt, h * 65:(h + 1) * 65], start=(n_ == 0 and s_ == 0), stop=(n_ == 0 and s_ == 0),
                                  skip_group_check=not (n_ == 0 and s_ == 0)),
                          reads=[("PT", pti), ("av", kt)], writes=[PS(accA)])
            norm_heads(ps[accA], PS(accA), R, 4, 64, y16, "y16", [0, 128, 64, 192])
            pb2 = transposes(y16, "y16", R, [(0, 128), (128, 128)], None, None)
            evac("act", yT[:, 0:2, qsl], pt[pb2][:, 0:256].rearrange("p (k c) -> p k c", c=128)[:, :, 0:R], [PTK(pb2)], [("yT", tt)])


        for tt in range(nt):
            b_scores(tt)
            a_attn(tt)
            b_topk(tt)
            if nt == 4:
                c_head(tt)
            else:
                for h in range(4):
                    c_head(h)
            b_attn(tt)

    def tok_proj(l, wname, srcT, srckeys, nt, R, scale):
        W = I[wname][l]
        for half in range(2):
            wt, wk = wload([(lambda w: w3(w, 8, 512), W[:, half * 512:(half + 1) * 512].rearrange("(k p) c -> p k c", p=128))])
            wv = w3(wt, 8, 512)
            for tt in range(nt):
                pb = 2 + (tt % 4)
                for k in range(8):
                    T.add("pe", C("matmul", ps[pb][0:R, :], srcT[:, k, tt * R:(tt + 1) * R], wv[:, k, :],
                                                                    start=(k == 0), stop=(k == 7)),
                          reads=[wk] + srckeys(tt), writes=[PS(pb)])
                T.add("dve", C("scalar_tensor_tensor",
                    out=x_tm[0:R, tt, half * 512:(half + 1) * 512], in0=ps[pb][0:R, :], scalar=scale,
                    in1=x_tm[0:R, tt, half * 512:(half + 1) * 512], op0=ALU.mult, op1=ALU.add),
                    reads=[PS(pb), ("x_tm", tt)], writes=[("x_tm", tt)])

    def merge_phase(l, nt, R):
        N = nt * R
        win = I["w_in"][l]
        load_ln(l, 1)
        for c in range(8):
            dc = slice(c * 128, (c + 1) * 128)
            parts = [(lambda w, g=g: w3(w, 32, 128)[:, g * 8:(g + 1) * 8, :],
                      win[:, cG + g * 1024 + c * 128:cG + g * 1024 + (c + 1) * 128].rearrange("(k p) c -> p k c", p=128))
                     for g in range(3)]
            parts.append((lambda w: w3(w, 32, 128)[:, 24:26, :], I["w_branch_a"][l][:, dc].rearrange("(k p) c -> p k c", p=128)))
            parts.append((lambda w: w3(w, 32, 128)[:, 26:28, :], I["w_branch_b"][l][:, dc].rearrange("(k p) c -> p k c", p=128)))
            parts.append((lambda w: w3(w, 32, 128)[:, 28:32, :], I["w_branch_c"][l][:, dc].rearrange("(k p) c -> p k c", p=128)))
            wgt, wgk = wload(parts)
            wgv = w3(wgt, 32, 128)
            wbk = wgk
            for g in range(3):
                gb = g
                for k in range(8):
                    T.add("pe", C("matmul", ps[gb][:, 0:N], wgv[:, g * 8 + k, :], xT[:, k, 0:N], start=(k == 0), stop=(k == 7)),
                          reads=[wgk] + [("xT", t) for t in range(nt)], writes=[PS(gb)])
                T.add("act", C("activation", out=sgm[g][:, 0:N], in_=ps[gb][:, 0:N], func=AF.Sigmoid),
                      reads=[PS(gb)], writes=[("sgm", g)])
            ks = [(0, 2), (2, 4), (4, 8)]
            for g in range(3):
                bb = 3 + g
                k0, k1 = ks[g]
                for k in range(k0, k1):
                    T.add("pe", C("matmul", ps[bb][:, 0:N], wgv[:, 24 + k, :], yT[:, k, 0:N],
                                                              start=(k == k0), stop=(k == k1 - 1)),
                          reads=[wbk] + [("yT", t) for t in range(nt)], writes=[PS(bb)])
            T.add("dve", C("tensor_tensor", out=mt1[:, 0:N], in0=sgm[0][:, 0:N], in1=ps[3][:, 0:N], op=ALU.mult),
                  reads=[("sgm", 0), PS(3)], writes=["mt1"])
            T.add("dve", C("tensor_tensor", out=mt2[:, 0:N], in0=sgm[1][:, 0:N], in1=ps[4][:, 0:N], op=ALU.mult),
                  reads=[("sgm", 1), PS(4)], writes=["mt2"])
            T.add("pool", C("tensor_tensor", out=mt1[:, 0:N], in0=mt1[:, 0:N], in1=mt2[:, 0:N], op=ALU.add),
                  reads=["mt1", "mt2"], writes=["mt1"])
            T.add("dve", C("tensor_tensor", out=mt2[:, 0:N], in0=sgm[2][:, 0:N], in1=ps[5][:, 0:N], op=ALU.mult),
                  reads=[("sgm", 2), PS(5)], writes=["mt2"])
            T.add("pool", C("tensor_tensor", out=mT[:, c, 0:N], in0=mt1[:, 0:N], in1=mt2[:, 0:N], op=ALU.add),
                  reads=["mt1", "mt2"], writes=[("mT", c)])
        tok_proj(l, "w_out", mT, lambda tt: [("mT", c) for c in range(8)], nt, R, 1.0)
        for tt in range(nt):
            layer_norm(l, 1, tt, R, None, None)

    def mem_phase(l, nt, R):
        N = nt * R
        load_ln(l, 2)
        wq = I["w_mem_q"][l]
        for half in range(2):
            wt, wk = wload([(lambda w: w3(w, 8, 512), wq[:, half * 512:(half + 1) * 512].rearrange("(k p) c -> p k c", p=128))])
            wv = w3(wt, 8, 512)
            for cc in range(4):
                c = half * 4 + cc
                pb = nextps(0, 2)
                for k in range(8):
                    T.add("pe", C("matmul", ps[pb][:, 0:N], wv[:, k, cc * 128:(cc + 1) * 128], xT[:, k, 0:N],
                                                                    start=(k == 0), stop=(k == 7)),
                          reads=[wk] + [("xT", t) for t in range(nt)], writes=[PS(pb)])
                T.add("act", C("copy", qmT[:, c, 0:N], ps[pb][:, 0:N]), reads=[PS(pb)], writes=[("qmT", c)])
        for h in range(4):
            for kt in range(2):
                sbk = nextps(0, 2)
                for c2 in range(2):
                    T.add("pe", C("matmul", ps[sbk][:, 0:N], mkT[:, h * 2 + c2, kt * 128:(kt + 1) * 128],
                                                                   qmT[:, h * 2 + c2, 0:N], start=(c2 == 0), stop=(c2 == 1)),
                          reads=[("mkT", kt), ("qmT", h * 2), ("qmT", h * 2 + 1)], writes=[PS(sbk)])
                pti = sbk
                T.add("act", C("activation", out=PT[pti][:, 0:N], in_=ps[sbk][:, 0:N], func=AF.Exp, scale=1.0 / 16.0),
                      reads=[PS(sbk)], writes=[("PT", pti)])
                for tt in range(nt):
                    bk = 2 + tt
                    T.add("pe", C("matmul", ps[bk][0:R, 0:257], PT[pti][:, tt * R:(tt + 1) * R],
                                                                         mv_aug[:, kt, h * 257:(h + 1) * 257], start=(kt == 0), stop=(kt == 1)),
                          reads=[("PT", pti), ("mv", kt)], writes=[PS(bk)])
            for tt in range(nt):
                bk = 2 + tt
                T.add("dve", C("reciprocal", out=small[0:R, 0:1], in_=ps[bk][0:R, 256:257]), reads=[PS(bk)], writes=["small"])
                T.add("dve", C("tensor_scalar", out=y16[0:R, 0:256], in0=ps[bk][0:R, 0:256], scalar1=small[0:R, 0:1],
                                                              scalar2=None, op0=ALU.mult), reads=[PS(bk), "small"], writes=["y16"])
                pb2 = transposes(y16, "y16", R, [(0, 128), (128, 128)], None, None)
                evac("act", yT[:, 2 * h:2 * h + 2, tt * R:(tt + 1) * R],
                     pt[pb2][:, 0:256].rearrange("p (k c) -> p k c", c=128)[:, :, 0:R], [PTK(pb2)], [("yT", tt)])
        tok_proj(l, "w_mem_o", yT, lambda tt: [("yT", tt)], nt, R, 1.0)
        for tt in range(nt):
            layer_norm(l, 2, tt, R, None, None)

    def load_block(src_rows, nt, R):
        for tt in range(nt):
            sap, skey = src_rows(tt)
            T.add("sp", C("dma_start", out=xn[0:R, :], in_=sap), reads=[skey], writes=["xn"], dma=True)
            T.add("act", C("mul", x_tm[0:R, tt, :], xn[0:R, :], ALPHA), reads=["xn"], writes=[("x_tm", tt)])
            T.add("dve", C("tensor_copy", xbf[0:R, :], xn[0:R, :]), reads=["xn"], writes=["xbf"])
            b = transposes(xbf, "xbf", R, [(k * 128, 128) for k in range(8)], None, None)
            evac("act", xT[:, :, tt * R:(tt + 1) * R], pt[b][:, :].rearrange("p (k c) -> p k c", c=128)[:, :, 0:R],
                 reads=[PTK(b)], writes=[("xT", tt)])

    class _Stop(Exception):
        pass

    phc = {}

    def ph(name):
        phc[name] = phc.get(name, 0) + 1
        if stop is not None:
            sn, _, cnt = stop.partition("@")
            if name == sn and phc[name] >= int(cnt or 1):
                raise _Stop()

    try:
        for (kind, b) in units:
            prompt = kind == "p"
            nblk, nt, R = (4, 4, 128) if prompt else (1, 1, 64)
            for l in range(n_layers):
                layer_consts(l)
                ph("consts")
                T.barrier()
                mem_prep(l, kind, b)
                ph("memprep")
                if not prompt:
                    T.barrier()
                    ingest_cache(l, b)
                    ph("ingest")
                for tb in range(nblk):
                    if l == 0:
                        if prompt:
                            src = lambda tt, tb=tb: (I["x_prompt"][b, (tb * 4 + tt) * 128:(tb * 4 + tt + 1) * 128, :], "xin")
                        else:
                            src = lambda tt: (I["x_sample"][b, :, :], "xin")
                    else:
                        src = lambda tt, tb=tb, R=R: (xmid[(tb * 4 + tt) * 128:(tb * 4 + tt) * 128 + R, :], ("xmid", tb * 4 + tt))
                    load_block(src, nt, R)
                    ph("load")
                    if tb == 0:
                        T.barrier()
                    ffn(l, 1, nt, R, 0, None)
                    ph("ffn1")
                    T.barrier()
                    win_phase(l, kind, b, tb, nt, R)
                    ph("win")
                    attn_phase(l, kind, tb, nt, R)
                    ph("attn")
                    T.barrier()
                    merge_phase(l, nt, R)
                    ph("merge")
                    mem_phase(l, nt, R)
                    ph("mem")
                    T.barrier()
                    if l == n_layers - 1:
                        if prompt:
                            fd = lambda tt, tb=tb: (O["y_p"][b, (tb * 4 + tt) * 128:(tb * 4 + tt + 1) * 128, :], "yout")
                        else:
                            fd = lambda tt: (O["y_s"][b, :, :], "yout")
                    else:
                        fd = lambda tt, tb=tb, R=R: (xmid[(tb * 4 + tt) * 128:(tb * 4 + tt) * 128 + R, :], ("xmid", tb * 4 + tt))
                    ffn(l, 2, nt, R, 3, fd)
                    ph("ffn2")
    except _Stop:
        pass
    T.emit(nc)
    es.close()
    return nc


def _bias_table(a_rel_bias):
    L = a_rel_bias.shape[0]
    kl = np.arange(640)[:, None]
    ql = np.arange(128)[None, :]
    idx = np.clip(ql - kl + 512, -128, 128) + 128
    masked = ((ql < 64) & (kl >= 576)) | ((ql >= 64) & (kl < 64))
    tab = a_rel_bias[:, :, idx]
    tab = np.where(masked[None, None], np.float32(NEG), tab).astype(np.float32)
    tab = tab.reshape(L, 4, 5, 128, 128).transpose(0, 3, 2, 1, 4)
    return np.ascontiguousarray(tab.reshape(L, 128, 5 * 4 * 128))


def _rope_table():
    pos = np.arange(17 * 128, dtype=np.float32)
    inv = (500000.0 ** (-np.arange(8, dtype=np.float32) / 8)).astype(np.float32)
    ang = pos[:, None] * inv[None, :]
    return np.concatenate([np.cos(ang), np.sin(ang)], axis=1).astype(np.float32)


_OUT_ORDER = ["y_p", "y_s", "a_k_p", "a_v_p", "b_k_p", "b_v_p", "b_i_p", "c_k_p", "c_v_p", "m_k_p", "m_v_p",
              "a_k_s", "a_v_s", "b_k_s", "b_v_s", "b_i_s", "c_k_s", "c_v_s"]


def make_in_maps(inp):
    f = lambda a: np.ascontiguousarray(np.asarray(a, dtype=np.float32))
    shared = {k: f(inp[k]) for k in ("ln_g", "ln_b", "ffn1_w_gu", "ffn1_w_d", "ffn2_w_gu", "ffn2_w_d", "w_in",
                                     "w_branch_a", "w_branch_b", "w_branch_c", "w_out", "w_mem_q", "w_mem_k",
                                     "w_mem_v", "w_mem_o", "c_subln_g")}
    shared["c_lambda"] = f(inp["c_lambda"]).reshape(DEPTH, 256)
    shared["biasA"] = _bias_table(f(inp["a_rel_bias"]))
    shared["rope_cs"] = _rope_table()
    maps = []
    for c in range(NCORES):
        s = slice(4 * c, 4 * c + 4)
        m = dict(shared)
        m["x_prompt"] = f(inp["x_prompt"][s])
        m["x_sample"] = f(inp["x_sample"][s])
        m["mem_prompt"] = f(inp["mem_prompt"][s])
        m["cache_a_k"] = f(inp["cache_a_k"][:, s]).reshape(DEPTH, 4, 512, 256)
        m["cache_a_v"] = f(inp["cache_a_v"][:, s]).reshape(DEPTH, 4, 512, 256)
        m["cache_b_k"] = f(inp["cache_b_k"][:, s])
        m["cache_b_v"] = f(inp["cache_b_v"][:, s])
        m["cache_b_idx"] = f(inp["cache_b_idx"][:, s])
        m["cache_c_k"] = f(inp["cache_c_k"][:, s]).reshape(DEPTH, 4, PAST, 512)
        m["cache_c_v"] = f(inp["cache_c_v"][:, s]).reshape(DEPTH, 4, PAST, 512)
        m["cache_mem_k"] = f(inp["cache_mem_k"][:, s]).reshape(DEPTH, 4, 256, 1024)
        m["cache_mem_v"] = f(inp["cache_mem_v"][:, s]).reshape(DEPTH, 4, 256, 1024)
        maps.append(m)
    return maps


def assemble(results):
    def cat(name, axis):
        return np.concatenate([r[name] for r in results], axis=axis)
    B = 4 * len(results)
    out = {
        "y_p": cat("y_p", 0), "y_s": cat("y_s", 0),
        "a_k_p": cat("a_k_p", 1).reshape(DEPTH, B, 512, 4, 64), "a_v_p": cat("a_v_p", 1).reshape(DEPTH, B, 512, 4, 64),
        "b_k_p": cat("b_k_p", 1), "b_v_p": cat("b_v_p", 1), "b_i_p": cat("b_i_p", 1),
        "c_k_p": cat("c_k_p", 1).reshape(DEPTH, B, SEQ, 4, 128), "c_v_p": cat("c_v_p", 1).reshape(DEPTH, B, SEQ, 4, 128),
        "m_k_p": cat("m_k_p", 1).reshape(DEPTH, B, 256, 4, 256), "m_v_p": cat("m_v_p", 1).reshape(DEPTH, B, 256, 4, 256),
        "a_k_s": cat("a_k_s", 1).reshape(DEPTH, B, TS, 4, 64), "a_v_s": cat("a_v_s", 1).reshape(DEPTH, B, TS, 4, 64),
        "b_k_s": cat("b_k_s", 1), "b_v_s": cat("b_v_s", 1), "b_i_s": cat("b_i_s", 1),
        "c_k_s": cat("c_k_s", 1).reshape(DEPTH, B, TS, 4, 128), "c_v_s": cat("c_v_s", 1).reshape(DEPTH, B, TS, 4, 128),
    }
    return tuple(np.ascontiguousarray(out[k], dtype=np.float32) for k in _OUT_ORDER)


def kernel(**inputs):
    units = [("p", b) for b in range(4)] + [("s", b) for b in range(4)]
    nc = build_program(units)
    in_maps = make_in_maps(inputs)
    res = run_bass_kernel_spmd(nc, in_maps, core_ids=list(range(NCORES)))
    return assemble(res.results)
```

```python
import contextlib
import math
import numpy as np
import concourse.bass as bass
import concourse.mybir as mybir
from concourse.bass_utils import run_bass_kernel_spmd

F32 = mybir.dt.float32
BF16 = mybir.dt.bfloat16
AF = mybir.ActivationFunctionType
ALU = mybir.AluOpType

NCORES = 8
D = 1024
SEQ = 2048
TS = 64
PAST = 2048
DEPTH = 2
DFF = 2816
NFC = DFF // 128
INC = 6344
ALPHA = (2.0 * DEPTH) ** 0.25
LN_EPS = 1e-5
NEG = -30000.0
cAq, cAk, cAv, cBq, cBk, cBv, cIq, cIk, cIw, cCq, cCk, cCv, cG = (
    0, 256, 512, 768, 1024, 1088, 1152, 1664, 1728, 1736, 2248, 2760, 3272)
ZBLK = [(0, 512), (512, 512), (1024, 512), (1536, 200), (1736, 512), (2248, 512), (2760, 512)]

ENGS = ("pe", "act", "dve", "pool", "sp")
NDMASEM = 8


def C(method, *args, **kw):
    return lambda e: getattr(e, method)(*args, **kw)


class Ins:
    __slots__ = ("eng", "idx", "fn", "dma", "deps", "sig", "slot", "val", "waits")

    def __init__(self, eng, idx, fn, dma):
        self.eng, self.idx, self.fn, self.dma = eng, idx, fn, dma
        self.deps = ()
        self.sig = False
        self.slot = 0
        self.val = 0
        self.waits = ()


class Trk:
    def __init__(self):
        self.streams = {e: [] for e in ENGS}
        self.wr = {}
        self.rd = {}
        self.ndma = {e: 0 for e in ENGS}
        self.pending = {e: [] for e in ENGS}
        self.lastc = {e: None for e in ENGS}
        self.lastd = {e: {} for e in ENGS}

    def barrier(self):
        deps = []
        for e in ENGS:
            deps += list(self.lastd[e].values())
            if self.lastc[e] is not None:
                deps.append(self.lastc[e])
        for e in ENGS:
            self.pending[e] = list(deps)

    def add(self, eng, fn, reads=(), writes=(), dma=False, nobarrier=False):
        st = self.streams[eng]
        ins = Ins(eng, len(st), fn, dma)
        deps = {}
        if not nobarrier:
            for d in self.pending[eng]:
                deps[id(d)] = d
            self.pending[eng] = []
        for k in reads:
            w = self.wr.get(k)
            if w is not None:
                deps[id(w)] = w
        for k in writes:
            w = self.wr.get(k)
            if w is not None:
                deps[id(w)] = w
            for r in self.rd.get(k, ()):
                deps[id(r)] = r
        for k in reads:
            self.rd.setdefault(k, []).append(ins)
        for k in writes:
            self.wr[k] = ins
            self.rd[k] = []
        if dma:
            j = self.ndma[eng]
            self.ndma[eng] = j + 1
            ins.slot = j % NDMASEM
            ins.val = 16 * (j // NDMASEM + 1)
            self.lastd[eng][ins.slot] = ins
        else:
            self.lastc[eng] = ins
        ds = []
        for d in deps.values():
            if d is ins:
                continue
            if (not dma) and (not d.dma) and d.eng == "pe" and eng == "pe":
                continue
            ds.append(d)
        ins.deps = ds
        st.append(ins)
        return ins

    def finalize(self):
        for e in ENGS:
            seen = {}
            seen_dma = set()
            prev = {}
            for ins in self.streams[e]:
                waits = []
                if ins.dma:
                    p = prev.get(ins.slot)
                    if p is not None and id(p) not in seen_dma:
                        waits.append(p)
                        seen_dma.add(id(p))
                    prev[ins.slot] = ins
                for d in ins.deps:
                    if d.dma:
                        if id(d) in seen_dma:
                            continue
                        seen_dma.add(id(d))
                        waits.append(d)
                    else:
                        if seen.get(d.eng, -1) >= d.idx:
                            continue
                        seen[d.eng] = d.idx
                        d.sig = True
                        waits.append(d)
                ins.waits = waits
        for e in ENGS:
            c = 0
            for ins in self.streams[e]:
                if not ins.dma and ins.sig:
                    c += 1
                    ins.val = c

    def emit(self, nc):
        self.finalize()
        final_waits = []
        for e in ENGS:
            last = {}
            for ins in self.streams[e]:
                if ins.dma:
                    last[ins.slot] = ins
            final_waits += list(last.values())
        with contextlib.ExitStack() as es:
            csem = {e: es.enter_context(nc.semaphore("c_" + e)) for e in ENGS}
            dsem = {e: [es.enter_context(nc.semaphore("d_%s%d" % (e, i))) for i in range(NDMASEM)]
                    for e in ENGS if self.ndma[e] > 0}
            block = es.enter_context(nc.Block())

            def run(e, eng):
                for ins in self.streams[e]:
                    for d in ins.waits:
                        if d.dma:
                            eng.wait_ge(dsem[d.eng][d.slot], d.val)
                        else:
                            eng.wait_ge(csem[d.eng], d.val)
                    r = ins.fn(eng)
                    if ins.dma:
                        r.then_inc(dsem[e][ins.slot], 16)
                    elif ins.sig:
                        r.then_inc(csem[e], 1)
                if e == "sp":
                    for d in final_waits:
                        eng.wait_ge(dsem[d.eng][d.slot], d.val)

            block.tensor(lambda eng: run("pe", eng))
            block.scalar(lambda eng: run("act", eng))
            block.vector(lambda eng: run("dve", eng))
            block.gpsimd(lambda eng: run("pool", eng))
            block.sync(lambda eng: run("sp", eng))


def build_program(units, n_layers=DEPTH, stop=None):
    nc = bass.Bass("TRN2", target_bir_lowering=False)
    T = Trk()
    es = contextlib.ExitStack()

    def din(name, shape):
        return nc.dram_tensor(name, list(shape), F32, kind="ExternalInput").ap()

    def dout(name, shape):
        return nc.dram_tensor(name, list(shape), F32, kind="ExternalOutput").ap()

    NB = 4
    I = {}
    I["x_prompt"] = din("x_prompt", (NB, SEQ, D))
    I["x_sample"] = din("x_sample", (NB, TS, D))
    I["cache_a_k"] = din("cache_a_k", (DEPTH, NB, 512, 256))
    I["cache_a_v"] = din("cache_a_v", (DEPTH, NB, 512, 256))
    I["cache_b_k"] = din("cache_b_k", (DEPTH, NB, PAST, 64))
    I["cache_b_v"] = din("cache_b_v", (DEPTH, NB, PAST, 64))
    I["cache_b_idx"] = din("cache_b_idx", (DEPTH, NB, PAST, 64))
    I["cache_c_k"] = din("cache_c_k", (DEPTH, NB, PAST, 512))
    I["cache_c_v"] = din("cache_c_v", (DEPTH, NB, PAST, 512))
    I["cache_mem_k"] = din("cache_mem_k", (DEPTH, NB, 256, 1024))
    I["cache_mem_v"] = din("cache_mem_v", (DEPTH, NB, 256, 1024))
    I["mem_prompt"] = din("mem_prompt", (NB, 256, D))
    I["ln_g"] = din("ln_g", (DEPTH, 4, D))
    I["ln_b"] = din("ln_b", (DEPTH, 4, D))
    I["ffn1_w_gu"] = din("ffn1_w_gu", (DEPTH, D, 2 * DFF))
    I["ffn1_w_d"] = din("ffn1_w_d", (DEPTH, DFF, D))
    I["ffn2_w_gu"] = din("ffn2_w_gu", (DEPTH, D, 2 * DFF))
    I["ffn2_w_d"] = din("ffn2_w_d", (DEPTH, DFF, D))
    I["w_in"] = din("w_in", (DEPTH, D, INC))
    I["biasA"] = din("biasA", (DEPTH, 128, 5 * 4 * 128))
    I["c_lambda"] = din("c_lambda", (DEPTH, 256))
    I["c_subln_g"] = din("c_subln_g", (DEPTH, 128))
    I["w_branch_a"] = din("w_branch_a", (DEPTH, 256, D))
    I["w_branch_b"] = din("w_branch_b", (DEPTH, 256, D))
    I["w_branch_c"] = din("w_branch_c", (DEPTH, 512, D))
    I["w_out"] = din("w_out", (DEPTH, D, D))
    I["w_mem_q"] = din("w_mem_q", (DEPTH, D, D))
    I["w_mem_k"] = din("w_mem_k", (DEPTH, D, D))
    I["w_mem_v"] = din("w_mem_v", (DEPTH, D, D))
    I["w_mem_o"] = din("w_mem_o", (DEPTH, D, D))
    I["rope_cs"] = din("rope_cs", (17 * 128, 16))

    O = {}
    O["y_p"] = dout("y_p", (NB, SEQ, D))
    O["y_s"] = dout("y_s", (NB, TS, D))
    O["a_k_p"] = dout("a_k_p", (DEPTH, NB, 512, 256))
    O["a_v_p"] = dout("a_v_p", (DEPTH, NB, 512, 256))
    O["b_k_p"] = dout("b_k_p", (DEPTH, NB, SEQ, 64))
    O["b_v_p"] = dout("b_v_p", (DEPTH, NB, SEQ, 64))
    O["b_i_p"] = dout("b_i_p", (DEPTH, NB, SEQ, 64))
    O["c_k_p"] = dout("c_k_p", (DEPTH, NB, SEQ, 512))
    O["c_v_p"] = dout("c_v_p", (DEPTH, NB, SEQ, 512))
    O["m_k_p"] = dout("m_k_p", (DEPTH, NB, 256, 1024))
    O["m_v_p"] = dout("m_v_p", (DEPTH, NB, 256, 1024))
    O["a_k_s"] = dout("a_k_s", (DEPTH, NB, TS, 256))
    O["a_v_s"] = dout("a_v_s", (DEPTH, NB, TS, 256))
    O["b_k_s"] = dout("b_k_s", (DEPTH, NB, TS, 64))
    O["b_v_s"] = dout("b_v_s", (DEPTH, NB, TS, 64))
    O["b_i_s"] = dout("b_i_s", (DEPTH, NB, TS, 64))
    O["c_k_s"] = dout("c_k_s", (DEPTH, NB, TS, 512))
    O["c_v_s"] = dout("c_v_s", (DEPTH, NB, TS, 512))
    xmid = nc.dram_tensor("xmid", [SEQ, D], F32, kind="Internal").ap()

    def sb(name, shape, dt=F32):
        return es.enter_context(nc.sbuf_tensor(name, list(shape), dt))

    NKT = 17
    akT2 = sb("akT2", [128, 2, NKT * 128], BF16)
    av_aug = sb("av_aug", [128, NKT, 4 * 65], BF16)
    bkT2 = sb("bkT2", [128, NKT * 128], BF16)
    bv_aug = sb("bv_aug", [128, NKT, 65], BF16)
    ikT2 = sb("ikT2", [128, NKT * 128], BF16)
    ckT2 = sb("ckT2", [128, 4, NKT * 128], BF16)
    cv_aug = sb("cv_aug", [128, NKT, 4 * 129], BF16)
    mkT = sb("mkT", [128, 8, 256], BF16)
    mv_aug = sb("mv_aug", [128, 2, 4 * 257], BF16)
    x_tm = sb("x_tm", [128, 4, D], F32)
    xT = sb("xT", [128, 8, 512], BF16)
    yT = sb("yT", [128, 8, 512], BF16)
    NWS = 4
    wslot = [sb("wslot%d" % i, [128, 4096], BF16) for i in range(NWS)]
    lng = sb("lng", [128, D], F32)
    lnb = sb("lnb", [128, D], F32)
    biasA = sb("biasA_sb", [128, 5 * 4 * 128], BF16)
    ropet = sb("ropet", [128, 17, 16], F32)
    ident = sb("ident", [128, 128], BF16)
    idf = sb("idf", [128, 128], F32)
    gsub = sb("gsub", [128, 128], F32)
    lamt = sb("lamt", [128, 256], F32)
    lams = sb("lams", [128, 8], F32)
    arena = sb("arena", [128, 52224], mybir.dt.uint8)
    small = sb("small", [128, 64], F32)
    stat = sb("stat", [128, 16], F32)

    def carve(off, shape, dt):
        n = int(np.prod(shape[1:])) * (2 if dt == BF16 else 4)
        v = arena[:, off:off + n].bitcast(dt)
        if len(shape) == 3:
            v = v.rearrange("p (a b) -> p a b", b=shape[2])
        return v, off + n

    xn, o = carve(0, [128, D], F32)
    xbf, o = carve(o, [128, D], BF16)
    base = o
    hT, o = carve(base, [128, NFC, 512], BF16)
    sgt = []
    for i in range(2):
        v, o = carve(o, [128, 512], F32)
        sgt.append(v)
    ffn_end = o
    o2 = base
    zst = []
    for i in range(2):
        v, o2 = carve(o2, [128, 512], F32)
        zst.append(v)
    z16 = []
    for i in range(2):
        v, o2 = carve(o2, [128, 512], BF16)
        z16.append(v)
    rtmp, o2 = carve(o2, [128, 4, 128], F32)
    aqT2, o2 = carve(o2, [128, 2, 512], BF16)
    bqT2, o2 = carve(o2, [128, 2, 512], BF16)
    iqT2, o2 = carve(o2, [128, 4, 512], BF16)
    cqT2, o2 = carve(o2, [128, 4, 512], BF16)
    score, o2 = carve(o2, [128, NKT * 128], F32)
    sel, o2 = carve(o2, [128, NKT * 128], BF16)
    relu_t = []
    for i in range(2):
        v, o2 = carve(o2, [128, 512], F32)
        relu_t.append(v)
    PT = []
    for i in range(2):
        v, o2 = carve(o2, [128, 512], BF16)
        PT.append(v)
    tmpA, o2 = carve(o2, [128, 512], F32)
    ycf, o2 = carve(o2, [128, 512], F32)
    yct, o2 = carve(o2, [128, 128], F32)
    y16, o2 = carve(o2, [128, 512], BF16)
    iw_sb, o2 = carve(o2, [128, 4, 8], F32)
    selT = []
    for i in range(2):
        v, o2 = carve(o2, [128, 128], BF16)
        selT.append(v)
    attn_end = o2
    o3 = base
    mT, o3 = carve(o3, [128, 8, 512], BF16)
    sgm = []
    for i in range(3):
        v, o3 = carve(o3, [128, 512], BF16)
        sgm.append(v)
    mt1, o3 = carve(o3, [128, 512], F32)
    mt2, o3 = carve(o3, [128, 512], F32)
    qmT, o3 = carve(o3, [128, 8, 512], BF16)
    om16, o3 = carve(o3, [128, D], BF16)
    assert o3 <= 37632 + base, o3
    assert max(ffn_end, attn_end, o3) <= 52224, (ffn_end, attn_end, o3)

    ps = [es.enter_context(nc.psum_tensor("ps%d" % i, [128, 512], F32)) for i in range(6)]
    pt = [es.enter_context(nc.psum_tensor("pt%d" % i, [128, 1024], BF16)) for i in range(2)]

    def PS(i):
        return ("ps", i)

    def PTK(i):
        return ("pt", i)

    wctr = [0]

    def wload(parts):
        s = wctr[0] % NWS
        wctr[0] += 1
        wt = wslot[s]
        for dstf, src in parts:
            T.add("pool", C("dma_start", out=dstf(wt), in_=src), writes=[("w", s)], dma=True, nobarrier=True)
        return wt, ("w", s)

    def w3(wt, k, c):
        return wt[:, 0:k * c].rearrange("p (k c) -> p k c", c=c)

    psrot = [0]

    def nextps(lo, hi):
        i = lo + psrot[0] % (hi - lo)
        psrot[0] += 1
        return i

    ptrot = [0]

    def nextpt():
        i = ptrot[0] % 2
        ptrot[0] += 1
        return i

    def transposes(src16, src_key, R, blocks, dst_fn, dst_keys, eng="act"):
        b = nextpt()
        for i, (c0, ncol) in enumerate(blocks):
            T.add("pe", C("transpose",
                pt[b][0:ncol, i * 128:i * 128 + R], src16[0:R, c0:c0 + ncol], ident[0:R, 0:R]),
                reads=[src_key, "ident"], writes=[PTK(b)])
        return b

    def evac(eng, out_ap, in_ap, reads, writes):
        if eng == "act":
            T.add("act", C("copy", out_ap, in_ap), reads=reads, writes=writes)
        else:
            T.add(eng, C("tensor_copy", out_ap, in_ap), reads=reads, writes=writes)

    T.add("pool", C("memset", idf[:], 0.0), writes=["idf"])
    T.add("pool", C("affine_select", out=idf[:], in_=idf[:], compare_op=ALU.not_equal, fill=1.0,
                                            base=0, pattern=[[-1, 128]], channel_multiplier=1),
          reads=["idf"], writes=["idf"])
    T.add("dve", C("tensor_copy", ident[:], idf[:]), reads=["idf"], writes=["ident"])
    T.add("sp", C("dma_start", out=ropet[:], in_=I["rope_cs"].rearrange("(t p) c -> p t c", p=128)),
          writes=["ropet"], dma=True)
    T.add("pool", C("memset", av_aug[:], 1.0), writes=[("av", k) for k in range(NKT)])
    T.add("pool", C("memset", bv_aug[:], 1.0), writes=[("bv", k) for k in range(NKT)])
    T.add("pool", C("memset", cv_aug[:], 1.0), writes=[("cv", k) for k in range(NKT)])
    T.add("pool", C("memset", mv_aug[:], 1.0), writes=[("mv", k) for k in range(2)])

    def layer_consts(l):
        lam_init = 0.8 - 0.6 * math.exp(-0.3 * l)
        T.add("pool", C("dma_start", out=biasA[:], in_=I["biasA"][l]), writes=["biasA"], dma=True)
        T.add("sp", C("dma_start", out=lamt[:], in_=I["c_lambda"][l].partition_broadcast(128)),
              writes=["lamt"], dma=True)
        T.add("sp", C("dma_start", out=gsub[:], in_=I["c_subln_g"][l].partition_broadcast(128)),
              writes=["gsub"], dma=True)
        T.add("dve", C("tensor_tensor", out=lamt[:, 0:64], in0=lamt[:, 0:64], in1=lamt[:, 64:128], op=ALU.mult),
              reads=["lamt"], writes=["lamt"])
        T.add("dve", C("tensor_tensor", out=lamt[:, 128:192], in0=lamt[:, 128:192], in1=lamt[:, 192:256], op=ALU.mult),
              reads=["lamt"], writes=["lamt"])
        T.add("dve", C("reduce_sum", out=lams[:, 0:1], in_=lamt[:, 0:64], axis=mybir.AxisListType.X),
              reads=["lamt"], writes=["lams"])
        T.add("dve", C("reduce_sum", out=lams[:, 1:2], in_=lamt[:, 128:192], axis=mybir.AxisListType.X),
              reads=["lamt"], writes=["lams"])
        T.add("act", C("activation", out=lams[:, 2:4], in_=lams[:, 0:2], func=AF.Exp),
              reads=["lams"], writes=["lams"])
        T.add("dve", C("tensor_tensor", out=lams[:, 4:5], in0=lams[:, 3:4], in1=lams[:, 2:3], op=ALU.subtract),
              reads=["lams"], writes=["lams"])
        T.add("dve", C("tensor_scalar", out=lams[:, 4:5], in0=lams[:, 4:5], scalar1=-lam_init, scalar2=None,
                                               op0=ALU.add), reads=["lams"], writes=["lams"])
        T.add("dve", C("tensor_scalar", out=gsub[:], in0=gsub[:], scalar1=(1.0 - lam_init), scalar2=None,
                                               op0=ALU.mult), reads=["gsub"], writes=["gsub"])

    def layer_norm(l, idx, tt, R, dst_kind, dst_ap):
        xk = ("x_tm", tt)
        src = x_tm[0:R, tt, :]
        T.add("dve", C("bn_stats", stat[0:R, 0:6], src[:, 0:512]), reads=[xk], writes=["stat"])
        T.add("dve", C("bn_stats", stat[0:R, 6:12], src[:, 512:1024]), reads=[xk], writes=["stat"])
        T.add("dve", C("bn_aggr", stat[0:R, 12:14], stat[0:R, 0:12]), reads=["stat"], writes=["stat"])
        T.add("dve", C("tensor_scalar", out=stat[0:R, 14:15], in0=stat[0:R, 13:14], scalar1=LN_EPS, scalar2=None,
                                               op0=ALU.add), reads=["stat"], writes=["stat"])
        T.add("act", C("activation", out=stat[0:R, 14:15], in_=stat[0:R, 14:15], func=AF.Sqrt),
              reads=["stat"], writes=["stat"])
        T.add("dve", C("reciprocal", out=stat[0:R, 15:16], in_=stat[0:R, 14:15]), reads=["stat"], writes=["stat"])
        T.add("dve", C("scalar_tensor_tensor", out=xn[0:R, :], in0=src, scalar=stat[0:R, 12:13], in1=lng[0:R, :],
                       op0=ALU.subtract, op1=ALU.mult), reads=[xk, "stat", "lng"], writes=["xn"])
        T.add("dve", C("scalar_tensor_tensor", out=xn[0:R, :], in0=xn[0:R, :], scalar=stat[0:R, 15:16], in1=lnb[0:R, :],
                       op0=ALU.mult, op1=ALU.add), reads=["xn", "stat", "lnb"], writes=["xn"])
        T.add("act", C("mul", x_tm[0:R, tt, :], xn[0:R, :], ALPHA), reads=["xn"], writes=[xk])
        T.add("act", C("copy", xbf[0:R, :], xn[0:R, :]), reads=["xn"], writes=["xbf"])
        if dst_kind == "dram":
            T.add("sp", C("dma_start", out=dst_ap[0], in_=xn[0:R, :]), reads=["xn"], writes=[dst_ap[1]], dma=True)
        b = transposes(xbf, "xbf", R, [(k * 128, 128) for k in range(8)], None, None)
        evac("act", xT[:, :, tt * R:(tt + 1) * R],
             pt[b][:, :].rearrange("p (k c) -> p k c", c=128)[:, :, 0:R],
             reads=[PTK(b)], writes=[("xT", tt)])

    def load_ln(l, idx):
        T.add("sp", C("dma_start", out=lng[:], in_=I["ln_g"][l, idx].partition_broadcast(128)),
              writes=["lng"], dma=True)
        T.add("sp", C("dma_start", out=lnb[:], in_=I["ln_b"][l, idx].partition_broadcast(128)),
              writes=["lnb"], dma=True)

    def ffn(l, which, nt, R, ln_idx, final_dst):
        N = nt * R
        wgu = I["ffn%d_w_gu" % which][l]
        wd = I["ffn%d_w_d" % which][l]
        load_ln(l, ln_idx)
        for j in range(NFC):
            wt, wk = wload([
                (lambda w: w3(w, 8, 256)[:, :, 0:128], wgu[:, j * 128:(j + 1) * 128].rearrange("(k p) c -> p k c", p=128)),
                (lambda w: w3(w, 8, 256)[:, :, 128:256],
                 wgu[:, DFF + j * 128:DFF + (j + 1) * 128].rearrange("(k p) c -> p k c", p=128)),
            ])
            wv = w3(wt, 8, 256)
            st_ = j % 2
            gb, ub = 2 * st_, 2 * st_ + 1
            for k in range(8):
                T.add("pe", C("matmul", ps[gb][:, 0:N], wv[:, k, 0:128], xT[:, k, 0:N], start=(k == 0), stop=(k == 7)),
                      reads=[wk] + [("xT", t) for t in range(nt)], writes=[PS(gb)])
            for k in range(8):
                T.add("pe", C("matmul", ps[ub][:, 0:N], wv[:, k, 128:256], xT[:, k, 0:N], start=(k == 0), stop=(k == 7)),
                      reads=[wk] + [("xT", t) for t in range(nt)], writes=[PS(ub)])
            T.add("act", C("activation", out=sgt[st_][:, 0:N], in_=ps[gb][:, 0:N], func=AF.Silu),
                  reads=[PS(gb)], writes=[("sgt", st_)])
            T.add("dve", C("tensor_tensor", out=hT[:, j, 0:N], in0=sgt[st_][:, 0:N], in1=ps[ub][:, 0:N], op=ALU.mult),
                  reads=[("sgt", st_), PS(ub)], writes=[("hT", j)])
        groups = [(0, 8), (8, 8), (16, 6)]
        for half in range(2):
            for (j0, nj) in groups:
                wt, wk = wload([(lambda w, nj=nj: w3(w, nj, 512),
                                 wd[j0 * 128:(j0 + nj) * 128, half * 512:(half + 1) * 512].rearrange("(j p) c -> p j c", p=128))])
                wv = w3(wt, nj, 512)
                for jj in range(nj):
                    j = j0 + jj
                    for tt in range(nt):
                        bk = 2 + tt
                        T.add("pe", C("matmul",
                            ps[bk][0:R, :], hT[:, j, tt * R:(tt + 1) * R], wv[:, jj, :], start=(j == 0), stop=(j == NFC - 1)),
                            reads=[wk, ("hT", j)], writes=[PS(bk)])
            for tt in range(nt):
                bk = 2 + tt
                T.add("dve", C("scalar_tensor_tensor",
                    out=x_tm[0:R, tt, half * 512:(half + 1) * 512], in0=ps[bk][0:R, :], scalar=0.5,
                    in1=x_tm[0:R, tt, half * 512:(half + 1) * 512], op0=ALU.mult, op1=ALU.add),
                    reads=[PS(bk), ("x_tm", tt)], writes=[("x_tm", tt)])
        for tt in range(nt):
            if final_dst is not None:
                layer_norm(l, ln_idx, tt, R, "dram", final_dst(tt))
            else:
                layer_norm(l, ln_idx, tt, R, None, None)

    def rope(stage, skey, R, c0, nh, gt):
        import os
        if os.environ.get("KDBG_NOROPE"):
            return
        v = stage[0:R, c0:c0 + nh * 64].rearrange("p (h d) -> p h d", d=64)
        x1 = v[:, :, 0:8]
        x2 = v[:, :, 8:16]
        cos = ropet[0:R, gt, 0:8].unsqueeze(1).to_broadcast([R, nh, 8])
        sin = ropet[0:R, gt, 8:16].unsqueeze(1).to_broadcast([R, nh, 8])
        t = [rtmp[0:R, i, 0:nh * 8].rearrange("p (h d) -> p h d", d=8) for i in range(4)]
        T.add("dve", C("tensor_tensor", out=t[0], in0=x1, in1=cos, op=ALU.mult), reads=[skey, "ropet"], writes=["rtmp"])
        T.add("dve", C("tensor_tensor", out=t[1], in0=x2, in1=sin, op=ALU.mult), reads=[skey, "ropet"], writes=["rtmp"])
        T.add("dve", C("tensor_tensor", out=t[2], in0=x2, in1=cos, op=ALU.mult), reads=[skey, "ropet"], writes=["rtmp"])
        T.add("dve", C("tensor_tensor", out=t[3], in0=x1, in1=sin, op=ALU.mult), reads=[skey, "ropet"], writes=["rtmp"])
        T.add("dve", C("tensor_tensor", out=x1, in0=t[0], in1=t[1], op=ALU.subtract), reads=["rtmp"], writes=[skey])
        T.add("dve", C("tensor_tensor", out=x2, in0=t[2], in1=t[3], op=ALU.add), reads=["rtmp"], writes=[skey])

    def win_phase(l, kind, b, tb, nt, R):
        win = I["w_in"][l]
        sfx = "_p" if kind == "p" else "_s"
        zr = [0]
        import os
        for bi, (c0, ncol) in enumerate(ZBLK):
            if bi > int(os.environ.get("KDBG_WIN", "99")):
                break
            wt, wk = wload([(lambda w, ncol=ncol: w3(w, 8, ncol),
                             win[:, c0:c0 + ncol].rearrange("(k p) c -> p k c", p=128))])
            wv = w3(wt, 8, ncol)
            wmode = os.environ.get("KDBG_WINMODE", "all")
            if wmode == "load":
                T.add("dve", C("tensor_copy", small[:, 0:8], wv[:, 0, 0:8]), reads=[wk], writes=["small"])
                continue
            for tt in range(nt):
                gt = tb * 4 + tt if kind == "p" else 16
                at = gt if kind == "p" else 4
                tok0 = gt * 128 if kind == "p" else 0
                pb = nextps(0, 2)
                for k in range(8):
                    T.add("pe", C("matmul", ps[pb][0:R, 0:ncol], xT[:, k, tt * R:(tt + 1) * R], wv[:, k, 0:ncol],
                                                                    start=(k == 0), stop=(k == 7)),
                          reads=[wk, ("xT", tt)], writes=[PS(pb)])
                si = zr[0] % 2
                zr[0] += 1
                st_, sk = zst[si], ("zst", si)
                z6, zk = z16[si], ("z16", si)
                T.add("act", C("copy", st_[0:R, 0:ncol], ps[pb][0:R, 0:ncol]),
                      reads=[PS(pb)], writes=[sk])
                if wmode == "mm":
                    continue

                def outd(name, col, n, rows=None, st_=st_, sk=sk):
                    dst = O[name + sfx][l, b, tok0:tok0 + R, :] if rows is None else rows
                    T.add("sp", C("dma_start", out=dst, in_=st_[0:R, col:col + n]), reads=[sk], dma=True)

                def cast(dc, sc, n, st_=st_, sk=sk, z6=z6, zk=zk, eng="dve"):
                    T.add(eng, C("tensor_copy", z6[0:R, dc:dc + n], st_[0:R, sc:sc + n]), reads=[sk], writes=[zk])

                ksl = slice(gt * 128, gt * 128 + R)
                asl = slice(at * 128, at * 128 + R)
                qsl = slice(tt * R, (tt + 1) * R)
                if bi == 0:
                    if kind == "s":
                        outd("a_k", 256, 256)
                    elif gt >= 12:
                        outd("a_k", 256, 256, rows=O["a_k_p"][l, b, (gt - 12) * 128:(gt - 11) * 128, :])
                    cast(0, 0, 512)
                    pb2 = transposes(z6, zk, R, [(0, 128), (128, 128), (256, 128), (384, 128)], None, None)
                    pv = pt[pb2][:, :].rearrange("p (k c) -> p k c", c=128)
                    evac("act", aqT2[:, :, qsl], pv[:, 0:2, 0:R], [PTK(pb2)], [("aqT", tt)])
                    evac("act", akT2[:, :, asl], pv[:, 2:4, 0:R], [PTK(pb2)], [("akT", at)])
                elif bi == 1:
                    if kind == "s":
                        outd("a_v", 0, 256)
                    elif gt >= 12:
                        outd("a_v", 0, 256, rows=O["a_v_p"][l, b, (gt - 12) * 128:(gt - 11) * 128, :])
                    T.add("dve", C("tensor_copy",
                        av_aug[0:R, at, :].rearrange("p (h d) -> p h d", d=65)[:, :, 0:64],
                        st_[0:R, 0:256].rearrange("p (h d) -> p h d", d=64)), reads=[sk], writes=[("av", at)])
                    rope(st_, sk, R, 256, 4, gt)
                    cast(0, 256, 256)
                    pb2 = transposes(z6, zk, R, [(0, 128), (128, 128)], None, None)
                    pv = pt[pb2][:, :].rearrange("p (k c) -> p k c", c=128)
                    evac("act", bqT2[:, :, qsl], pv[:, 0:2, 0:R], [PTK(pb2)], [("bqT", tt)])
                elif bi == 2:
                    rope(st_, sk, R, 0, 1, gt)
                    rope(st_, sk, R, 128, 6, gt)
                    outd("b_k", 0, 64)
                    outd("b_v", 64, 64)
                    T.add("dve", C("tensor_copy", bv_aug[0:R, gt, 0:64], st_[0:R, 64:128]),
                          reads=[sk], writes=[("bv", gt)])
                    cast(0, 0, 64)
                    cast(64, 0, 64)
                    cast(128, 128, 384)
                    pb2 = transposes(z6, zk, R, [(0, 128), (128, 128), (256, 128), (384, 128)], None, None)
                    pv = pt[pb2][:, :].rearrange("p (k c) -> p k c", c=128)
                    evac("act", bkT2[:, ksl], pt[pb2][:, 0:R], [PTK(pb2)], [("bkT", gt)])
                    evac("act", iqT2[:, 0:3, qsl], pv[:, 1:4, 0:R], [PTK(pb2)], [("iqT", tt)])
                elif bi == 3:
                    rope(st_, sk, R, 0, 3, gt)
                    outd("b_i", 128, 64)
                    T.add("act", C("mul", iw_sb[0:R, tt, :], st_[0:R, 192:200], 8.0 ** -0.5),
                          reads=[sk], writes=[("iw", tt)])
                    cast(0, 0, 192)
                    cast(192, 128, 64)
                    pb2 = transposes(z6, zk, R, [(0, 128), (128, 128)], None, None)
                    evac("act", iqT2[:, 3, qsl], pt[pb2][:, 0:R], [PTK(pb2)], [("iqT", tt)])
                    evac("act", ikT2[:, ksl], pt[pb2][:, 128:128 + R], [PTK(pb2)], [("ikT", gt)])
                elif bi == 4:
                    rope(st_, sk, R, 0, 8, gt)
                    cast(0, 0, 512)
                    pb2 = transposes(z6, zk, R, [(0, 128), (128, 128), (256, 128), (384, 128)], None, None)
                    pv = pt[pb2][:, :].rearrange("p (k c) -> p k c", c=128)
                    evac("act", cqT2[:, :, qsl], pv[:, 0:4, 0:R], [PTK(pb2)], [("cqT", tt)])
                elif bi == 5:
                    rope(st_, sk, R, 0, 8, gt)
                    outd("c_k", 0, 512)
                    cast(0, 0, 512)
                    pb2 = transposes(z6, zk, R, [(0, 128), (128, 128), (256, 128), (384, 128)], None, None)
                    pv = pt[pb2][:, :].rearrange("p (k c) -> p k c", c=128)
                    evac("act", ckT2[:, :, ksl], pv[:, 0:4, 0:R], [PTK(pb2)], [("ckT", gt)])
                else:
                    outd("c_v", 0, 512)
                    T.add("dve", C("tensor_copy",
                        cv_aug[0:R, gt, :].rearrange("p (h d) -> p h d", d=129)[:, :, 0:128],
                        st_[0:R, 0:512].rearrange("p (h d) -> p h d", d=128)), reads=[sk], writes=[("cv", gt)])

    def ingest_cache(l, b):
        for kt in range(16):
            rows = slice(kt * 128, (kt + 1) * 128)
            ksl = rows
            zi = kt % 2
            z6, zk = z16[zi], ("z16", zi)
            T.add("pool", C("dma_start", out=z6[:, 0:512], in_=I["cache_c_k"][l, b, rows, :]), writes=[zk], dma=True)
            pb2 = transposes(z6, zk, 128, [(0, 128), (128, 128), (256, 128), (384, 128)], None, None)
            pv = pt[pb2][:, :].rearrange("p (k c) -> p k c", c=128)
            evac("act", ckT2[:, :, ksl], pv[:, 0:4, :], [PTK(pb2)], [("ckT", kt)])
            T.add("pool", C("dma_start",
                out=cv_aug[:, kt, :].rearrange("p (h d) -> p h d", d=129)[:, :, 0:128],
                in_=I["cache_c_v"][l, b, rows, :].rearrange("p (h d) -> p h d", d=128)), writes=[("cv", kt)], dma=True)
            T.add("pool", C("dma_start", out=bv_aug[:, kt, 0:64], in_=I["cache_b_v"][l, b, rows, :]),
                  writes=[("bv", kt)], dma=True)
            zi2 = (kt + 1) % 2
            z6b, zkb = z16[zi2], ("z16", zi2)
            for c in range(2):
                T.add("pool", C("dma_start", out=z6b[:, c * 64:(c + 1) * 64], in_=I["cache_b_k"][l, b, rows, :]),
                      writes=[zkb], dma=True)
                T.add("pool", C("dma_start", out=z6b[:, 128 + c * 64:128 + (c + 1) * 64],
                                                                 in_=I["cache_b_idx"][l, b, rows, :]), writes=[zkb], dma=True)
            pb3 = transposes(z6b, zkb, 128, [(0, 128), (128, 128)], None, None)
            evac("act", bkT2[:, ksl], pt[pb3][:, 0:128], [PTK(pb3)], [("bkT", kt)])
            evac("act", ikT2[:, ksl], pt[pb3][:, 128:256], [PTK(pb3)], [("ikT", kt)])
        for kt in range(4):
            rows = slice(kt * 128, (kt + 1) * 128)
            zi = kt % 2
            z6, zk = z16[zi], ("z16", zi)
            T.add("pool", C("dma_start", out=z6[:, 0:256], in_=I["cache_a_k"][l, b, rows, :]), writes=[zk], dma=True)
            pb2 = transposes(z6, zk, 128, [(0, 128), (128, 128)], None, None)
            pv = pt[pb2][:, :].rearrange("p (k c) -> p k c", c=128)
            evac("act", akT2[:, :, rows], pv[:, 0:2, :], [PTK(pb2)], [("akT", kt)])
            T.add("pool", C("dma_start",
                out=av_aug[:, kt, :].rearrange("p (h d) -> p h d", d=65)[:, :, 0:64],
                in_=I["cache_a_v"][l, b, rows, :].rearrange("p (h d) -> p h d", d=64)), writes=[("av", kt)], dma=True)

    def mem_prep(l, kind, b):
        if kind == "s":
            for kt in range(2):
                rows = slice(kt * 128, (kt + 1) * 128)
                T.add("pool", C("dma_start", out=om16[:, :], in_=I["cache_mem_k"][l, b, rows, :]), writes=["om16"], dma=True)
                pb2 = transposes(om16, "om16", 128, [(k * 128, 128) for k in range(8)], None, None)
                evac("act", mkT[:, :, rows], pt[pb2][:, :].rearrange("p (k c) -> p k c", c=128), [PTK(pb2)], [("mkT", kt)])
                T.add("pool", C("dma_start",
                    out=mv_aug[:, kt, :].rearrange("p (h d) -> p h d", d=257)[:, :, 0:256],
                    in_=I["cache_mem_v"][l, b, rows, :].rearrange("p (h d) -> p h d", d=256)), writes=[("mv", kt)], dma=True)
            return
        for kt in range(2):
            rows = slice(kt * 128, (kt + 1) * 128)
            T.add("pool", C("dma_start", out=om16[:, :], in_=I["mem_prompt"][b, rows, :]), writes=["om16"], dma=True)
            pb2 = transposes(om16, "om16", 128, [(k * 128, 128) for k in range(8)], None, None)
            evac("act", qmT[:, :, rows], pt[pb2][:, :].rearrange("p (k c) -> p k c", c=128), [PTK(pb2)], [("qmT", kt)])
        for wi, wname in enumerate(("w_mem_k", "w_mem_v")):
            for half in range(2):
                wt, wk = wload([(lambda w: w3(w, 8, 512),
                                 I[wname][l][:, half * 512:(half + 1) * 512].rearrange("(k p) c -> p k c", p=128))])
                wv = w3(wt, 8, 512)
                for kt in range(2):
                    rows = slice(kt * 128, (kt + 1) * 128)
                    pb = nextps(0, 2)
                    for k in range(8):
                        T.add("pe", C("matmul", ps[pb][:, :], qmT[:, k, rows], wv[:, k, :], start=(k == 0), stop=(k == 7)),
                              reads=[wk, ("qmT", kt)], writes=[PS(pb)])
                    T.add("act", C("copy", mt1[:, :], ps[pb][:, :]), reads=[PS(pb)], writes=["mt1"])
                    dst = O["m_k_p" if wi == 0 else "m_v_p"][l, b, rows, half * 512:(half + 1) * 512]
                    T.add("sp", C("dma_start", out=dst, in_=mt1[:, :]), reads=["mt1"], dma=True)
                    if wi == 0:
                        T.add("dve", C("tensor_copy", y16[:, :], mt1[:, :]), reads=["mt1"], writes=["y16"])
                        pb2 = transposes(y16, "y16", 128, [(k * 128, 128) for k in range(4)], None, None)
                        evac("act", mkT[:, half * 4:half * 4 + 4, rows],
                             pt[pb2][:, 0:512].rearrange("p (k c) -> p k c", c=128), [PTK(pb2)], [("mkT", kt)])
                    else:
                        T.add("dve", C("tensor_copy",
                            mv_aug[:, kt, :].rearrange("p (h d) -> p h d", d=257)[:, 2 * half:2 * half + 2, 0:256],
                            mt1[:, :].rearrange("p (h d) -> p h d", d=256)), reads=["mt1"], writes=[("mv", kt)])

    def norm_heads(acc, acckey, R, nh, dh, out16, okey, cols):
        av = acc[0:R, 0:nh * (dh + 1)].rearrange("p (h d) -> p h d", d=dh + 1)
        T.add("dve", C("reciprocal", out=small[0:R, 0:nh], in_=av[:, :, dh]), reads=[acckey], writes=["small"])
        for s in range(nh):
            T.add("dve", C("tensor_scalar", out=out16[0:R, cols[s]:cols[s] + dh], in0=av[:, s, 0:dh],
                                                        scalar1=small[0:R, s:s + 1], scalar2=None, op0=ALU.mult),
                  reads=[acckey, "small"], writes=[okey])

    def attn_phase(l, kind, tb, nt, R):
        N = nt * R
        prompt = kind == "p"
        nkt_blk = (tb * 4 + nt) if prompt else 17

        def c_head(h):
            started = set()
            for kt in range(nkt_blk):
                ksz = 128 if (prompt or kt < 16) else 64
                tt0 = max(0, kt - tb * 4) if prompt else 0
                cs0 = tt0 * R
                for i in range(2):
                    sbk = nextps(0, 2)
                    pti = sbk
                    T.add("pe", C("matmul",
                        ps[sbk][0:ksz, cs0:N], ckT2[i * 64:(i + 1) * 64, h, kt * 128:kt * 128 + ksz],
                        cqT2[i * 64:(i + 1) * 64, h, cs0:N], start=True, stop=True),
                        reads=[("ckT", kt)] + [("cqT", t) for t in range(tt0, nt)], writes=[PS(sbk)])
                    T.add("act", C("activation", out=PT[pti][0:ksz, cs0:N], in_=ps[sbk][0:ksz, cs0:N],
                                                                          func=AF.Exp, scale=0.125),
                          reads=[PS(sbk)], writes=[("PT", pti)])
                    if prompt and kt >= tb * 4:
                        T.add("pool", C("memset", PT[pti][64:128, cs0:cs0 + 64], 0.0),
                              reads=[], writes=[("PT", pti)])
                    for tt in range(tt0, nt):
                        a = tt * 2 + i
                        bk, col = 2 + a // 3, (a % 3) * 129
                        last = (tb * 4 + tt) if prompt else 16
                        first = bk not in started
                        started.add(bk)
                        T.add("pe", C("matmul",
                            ps[bk][0:R, col:col + 129], PT[pti][0:ksz, tt * R:(tt + 1) * R],
                            cv_aug[0:ksz, kt, h * 129:(h + 1) * 129], start=first, stop=first, skip_group_check=(not first)),
                            reads=[("PT", pti), ("cv", kt)], writes=[PS(bk)])
            for tt in range(nt):
                a0, a1 = tt * 2, tt * 2 + 1
                b0, c0 = 2 + a0 // 3, (a0 % 3) * 129
                b1, c1 = 2 + a1 // 3, (a1 % 3) * 129
                T.add("dve", C("reciprocal", out=small[0:R, 0:1], in_=ps[b0][0:R, c0 + 128:c0 + 129]),
                      reads=[PS(b0)], writes=["small"])
                T.add("dve", C("reciprocal", out=small[0:R, 1:2], in_=ps[b1][0:R, c1 + 128:c1 + 129]),
                      reads=[PS(b1)], writes=["small"])
                T.add("dve", C("tensor_tensor", out=small[0:R, 1:2], in0=small[0:R, 1:2], in1=lams[0:R, 4:5], op=ALU.mult),
                      reads=["small", "lams"], writes=["small"])
                T.add("dve", C("tensor_scalar", out=yct[0:R, :], in0=ps[b0][0:R, c0:c0 + 128], scalar1=small[0:R, 0:1],
                                                       scalar2=None, op0=ALU.mult), reads=[PS(b0), "small"], writes=["yct"])
                T.add("dve", C("scalar_tensor_tensor", out=ycf[0:R, 0:128], in0=ps[b1][0:R, c1:c1 + 128],
                                                                     scalar=small[0:R, 1:2], in1=yct[0:R, :],
                                                                     op0=ALU.mult, op1=ALU.add),
                      reads=[PS(b1), "small", "yct"], writes=["ycf"])
                T.add("act", C("activation", out=yct[0:R, :], in_=ycf[0:R, 0:128], func=AF.Square,
                                                    accum_out=small[0:R, 2:3]), reads=["ycf"], writes=["yct", "small"])
                T.add("dve", C("tensor_scalar", out=small[0:R, 2:3], in0=small[0:R, 2:3], scalar1=1.0 / 128.0,
                                                       scalar2=1e-5, op0=ALU.mult, op1=ALU.add), reads=["small"], writes=["small"])
                T.add("act", C("activation", out=small[0:R, 2:3], in_=small[0:R, 2:3], func=AF.Sqrt),
                      reads=["small"], writes=["small"])
                T.add("dve", C("reciprocal", out=small[0:R, 3:4], in_=small[0:R, 2:3]), reads=["small"], writes=["small"])
                T.add("dve", C("scalar_tensor_tensor", out=y16[0:R, 0:128], in0=ycf[0:R, 0:128],
                                                                     scalar=small[0:R, 3:4], in1=gsub[0:R, :],
                                                                     op0=ALU.mult, op1=ALU.mult),
                      reads=["ycf", "small", "gsub"], writes=["y16"])
                pb2 = transposes(y16, "y16", R, [(0, 128)], None, None)
                evac("act", yT[:, 4 + h, tt * R:(tt + 1) * R], pt[pb2][:, 0:R], [PTK(pb2)], [("yT", tt)])

        import os
        def b_scores(tt):
            gt = tb * 4 + tt if prompt else 16
            nkt = gt + 1 if prompt else 17
            Wk = (gt + 1) * 128 if prompt else PAST + TS
            qsl = slice(tt * R, (tt + 1) * R)
            nkb = (Wk + 511) // 512
            for kb in range(nkb):
                w = min(512, Wk - kb * 512)
                ksl = slice(kb * 512, kb * 512 + w)
                for ih in range(8):
                    hf, c = ih % 2, ih // 2
                    pb = nextps(0, 2)
                    T.add("pe", C("matmul", ps[pb][0:R, 0:w], iqT2[hf * 64:(hf + 1) * 64, c, qsl],
                                                                      ikT2[hf * 64:(hf + 1) * 64, ksl], start=True, stop=True),
                          reads=[("iqT", tt)] + [("ikT", t) for t in range(kb * 4, min(nkt, kb * 4 + 4))], writes=[PS(pb)])
                    ri = pb
                    T.add("act", C("activation", out=relu_t[ri][0:R, 0:w], in_=ps[pb][0:R, 0:w], func=AF.Relu),
                          reads=[PS(pb)], writes=[("relu", ri)])
                    if ih == 0:
                        T.add("dve", C("tensor_scalar", out=score[0:R, ksl], in0=relu_t[ri][0:R, 0:w],
                                                                      scalar1=iw_sb[0:R, tt, 0:1], scalar2=None, op0=ALU.mult),
                              reads=[("relu", ri), ("iw", tt)], writes=["score"])
                    else:
                        T.add("dve", C("scalar_tensor_tensor",
                            out=score[0:R, ksl], in0=relu_t[ri][0:R, 0:w], scalar=iw_sb[0:R, tt, ih:ih + 1],
                            in1=score[0:R, ksl], op0=ALU.mult, op1=ALU.add),
                            reads=[("relu", ri), ("iw", tt), "score"], writes=["score"])
            if prompt:
                T.add("dve", C("memset", score[0:64, Wk - 64:Wk], -1e30), reads=[], writes=["score"])

        def b_topk(tt):
            gt = tb * 4 + tt if prompt else 16
            nkt = gt + 1 if prompt else 17
            Wk = (gt + 1) * 128 if prompt else PAST + TS
            qsl = slice(tt * R, (tt + 1) * R)
            if Wk > 256:
                for r in range(32):
                    T.add("dve", C("max", out=small[0:R, 8:16], in_=score[0:R, 0:Wk]), reads=["score"], writes=["small8"])
                    T.add("dve", C("match_replace", out=score[0:R, 0:Wk], in_to_replace=small[0:R, 8:16],
                                                           in_values=score[0:R, 0:Wk], imm_value=-1e30),
                          reads=["score", "small8"], writes=["score"])
                T.add("dve", C("tensor_single_scalar", out=sel[0:R, 0:Wk], in_=score[0:R, 0:Wk], scalar=-1e29, op=ALU.is_lt),
                      reads=["score"], writes=["sel"])
                if prompt:
                    T.add("dve", C("memset", sel[0:64, Wk - 64:Wk], 0.0), reads=[], writes=["sel"])
            else:
                T.add("dve", C("tensor_single_scalar", out=sel[0:R, 0:Wk], in_=score[0:R, 0:Wk], scalar=-1e29, op=ALU.is_gt),
                      reads=["score"], writes=["sel"])

        def b_attn(tt):
            gt = tb * 4 + tt if prompt else 16
            nkt = gt + 1 if prompt else 17
            Wk = (gt + 1) * 128 if prompt else PAST + TS
            qsl = slice(tt * R, (tt + 1) * R)
            accB = 5
            for kt in range(nkt):
                ksz = 128 if (prompt or kt < 16) else 64
                kcs = slice(kt * 128, kt * 128 + ksz)
                pbm = transposes(sel, "sel", R, [(kt * 128, ksz)], None, None)
                evac("act", selT[pbm][0:ksz, 0:R], pt[pbm][0:ksz, 0:R], [PTK(pbm)], [("selT", pbm)])
                bmode = int(os.environ.get("KDBG_B", "9"))
                if bmode <= 1:
                    continue
                for hf in range(2):
                    for c in range(2):
                        T.add("pe", C("matmul", ps[hf][0:ksz, c * R:(c + 1) * R],
                                      bkT2[hf * 64:(hf + 1) * 64, kcs],
                                      bqT2[hf * 64:(hf + 1) * 64, c, qsl], start=True, stop=True),
                              reads=[("bkT", kt), ("bqT", tt)], writes=[PS(hf)])
                pti = kt % 2
                for hf in range(2):
                    T.add("act", C("activation", out=PT[pti][0:ksz, hf * 2 * R:(hf + 1) * 2 * R], in_=ps[hf][0:ksz, 0:2 * R],
                                   func=AF.Exp, scale=0.125), reads=[PS(hf)], writes=[("PT", pti)])
                if bmode <= 2:
                    continue
                T.add("dve", C("tensor_tensor",
                    out=PT[pti][0:ksz, 0:4 * R].rearrange("p (h q) -> p h q", q=R),
                    in0=PT[pti][0:ksz, 0:4 * R].rearrange("p (h q) -> p h q", q=R),
                    in1=selT[pbm][0:ksz, 0:R].unsqueeze(1).to_broadcast([ksz, 4, R]), op=ALU.mult),
                    reads=[("PT", pti), ("selT", pbm)], writes=[("PT", pti)])
                if bmode <= 3:
                    continue
                for s in range(4):
                    T.add("pe", C("matmul", ps[accB][0:R, s * 65:(s + 1) * 65], PT[pti][0:ksz, s * R:(s + 1) * R],
                                                                       bv_aug[0:ksz, kt, :], start=(kt == 0 and s == 0), stop=(kt == 0 and s == 0),
                                                                       skip_group_check=not (kt == 0 and s == 0)),
                          reads=[("PT", pti), ("bv", kt)], writes=[PS(accB)])
            norm_heads(ps[accB], PS(accB), R, 4, 64, y16, "y16", [0, 128, 64, 192])
            pb2 = transposes(y16, "y16", R, [(0, 128), (128, 128)], None, None)
            evac("act", yT[:, 2:4, qsl], pt[pb2][:, 0:256].rearrange("p (k c) -> p k c", c=128)[:, :, 0:R], [PTK(pb2)], [("yT", tt)])

        def a_attn(tt):
            gt = tb * 4 + tt if prompt else 16
            nkt = gt + 1 if prompt else 17
            Wk = (gt + 1) * 128 if prompt else PAST + TS
            qsl = slice(tt * R, (tt + 1) * R)
            if prompt:
                akts = [(kt, kt - (gt - 4), 128) for kt in range(max(0, gt - 4), gt + 1)]
            else:
                akts = [(kt, kt, 128 if kt < 4 else 64) for kt in range(5)]
            accA = 5
            for n_, (kt, ti, ksz) in enumerate(akts):
                kcs = slice(kt * 128, kt * 128 + ksz)
                for hf in range(2):
                    for c in range(2):
                        T.add("pe", C("matmul", ps[hf][0:ksz, c * R:(c + 1) * R],
                                      akT2[hf * 64:(hf + 1) * 64, c, kcs],
                                      aqT2[hf * 64:(hf + 1) * 64, c, qsl], start=True, stop=True),
                              reads=[("akT", kt), ("aqT", tt)], writes=[PS(hf)])
                bfull = biasA[0:ksz, ti * 512:(ti + 1) * 512].rearrange("p (c f q) -> p c f q", f=2, q=128)
                for hf in range(2):
                    T.add("dve", C("scalar_tensor_tensor",
                                   out=tmpA[0:ksz, hf * 2 * R:(hf + 1) * 2 * R].rearrange("p (c q) -> p c q", q=R),
                                   in0=ps[hf][0:ksz, 0:2 * R].rearrange("p (c q) -> p c q", q=R), scalar=0.125,
                                   in1=bfull[:, :, hf, 0:R], op0=ALU.mult, op1=ALU.add),
                          reads=[PS(hf), "biasA"], writes=["tmpA"])
                pti = n_ % 2
                T.add("act", C("activation", out=PT[pti][0:ksz, 0:4 * R], in_=tmpA[0:ksz, 0:4 * R], func=AF.Exp),
                      reads=["tmpA"], writes=[("PT", pti)])
                for s_ in range(4):
                    h = [0, 2, 1, 3][s_]
                    T.add("pe", C("matmul", ps[accA][0:R, s_ * 65:(s_ + 1) * 65], PT[pti][0:ksz, s_ * R:(s_ + 1) * R],
                                  av_aug[0:ksz, kt, h * 65:(h + 1) * 65], start=(n_ == 0 and s_ == 0), stop=(n_ == 0 and s_ == 0),
                                  skip_group_check=not (n_ == 0 and s_ == 0)),
                          reads=[("PT", pti), ("av", kt)], writes=[PS(accA)])
            norm_heads(ps[accA], PS(accA), R, 4, 64, y16, "y16", [0, 128, 64, 192])
            pb2 = transposes(y16, "y16", R, [(0, 128), (128, 128)], None, None)
            evac("act", yT[:, 0:2, qsl], pt[pb2][:, 0:256].rearrange("p (k c) -> p k c", c=128)[:, :, 0:R], [PTK(pb2)], [("yT", tt)])


        for tt in range(nt):
            b_scores(tt)
            a_attn(tt)
            b_topk(tt)
            if nt == 4:
                c_head(tt)
            else:
                for h in range(4):
                    c_head(h)
            b_attn(tt)

    def tok_proj(l, wname, srcT, srckeys, nt, R, scale):
        W = I[wname][l]
        for half in range(2):
            wt, wk = wload([(lambda w: w3(w, 8, 512), W[:, half * 512:(half + 1) * 512].rearrange("(k p) c -> p k c", p=128))])
            wv = w3(wt, 8, 512)
            for tt in range(nt):
                pb = 2 + (tt % 4)
                for k in range(8):
                    T.add("pe", C("matmul", ps[pb][0:R, :], srcT[:, k, tt * R:(tt + 1) * R], wv[:, k, :],
                                                                    start=(k == 0), stop=(k == 7)),
                          reads=[wk] + srckeys(tt), writes=[PS(pb)])
                T.add("dve", C("scalar_tensor_tensor",
                    out=x_tm[0:R, tt, half * 512:(half + 1) * 512], in0=ps[pb][0:R, :], scalar=scale,
                    in1=x_tm[0:R, tt, half * 512:(half + 1) * 512], op0=ALU.mult, op1=ALU.add),
                    reads=[PS(pb), ("x_tm", tt)], writes=[("x_tm", tt)])

    def merge_phase(l, nt, R):
        N = nt * R
        win = I["w_in"][l]
        load_ln(l, 1)
        for c in range(8):
            dc = slice(c * 128, (c + 1) * 128)
            parts = [(lambda w, g=g: w3(w, 32, 128)[:, g * 8:(g + 1) * 8, :],
                      win[:, cG + g * 1024 + c * 128:cG + g * 1024 + (c + 1) * 128].rearrange("(k p) c -> p k c", p=128))
                     for g in range(3)]
            parts.append((lambda w: w3(w, 32, 128)[:, 24:26, :], I["w_branch_a"][l][:, dc].rearrange("(k p) c -> p k c", p=128)))
            parts.append((lambda w: w3(w, 32, 128)[:, 26:28, :], I["w_branch_b"][l][:, dc].rearrange("(k p) c -> p k c", p=128)))
            parts.append((lambda w: w3(w, 32, 128)[:, 28:32, :], I["w_branch_c"][l][:, dc].rearrange("(k p) c -> p k c", p=128)))
            wgt, wgk = wload(parts)
            wgv = w3(wgt, 32, 128)
            wbk = wgk
            for g in range(3):
                gb = g
                for k in range(8):
                    T.add("pe", C("matmul", ps[gb][:, 0:N], wgv[:, g * 8 + k, :], xT[:, k, 0:N], start=(k == 0), stop=(k == 7)),
                          reads=[wgk] + [("xT", t) for t in range(nt)], writes=[PS(gb)])
                T.add("act", C("activation", out=sgm[g][:, 0:N], in_=ps[gb][:, 0:N], func=AF.Sigmoid),
                      reads=[PS(gb)], writes=[("sgm", g)])
            ks = [(0, 2), (2, 4), (4, 8)]
            for g in range(3):
                bb = 3 + g
                k0, k1 = ks[g]
                for k in range(k0, k1):
                    T.add("pe", C("matmul", ps[bb][:, 0:N], wgv[:, 24 + k, :], yT[:, k, 0:N],
                                                              start=(k == k0), stop=(k == k1 - 1)),
                          reads=[wbk] + [("yT", t) for t in range(nt)], writes=[PS(bb)])
            T.add("dve", C("tensor_tensor", out=mt1[:, 0:N], in0=sgm[0][:, 0:N], in1=ps[3][:, 0:N], op=ALU.mult),
                  reads=[("sgm", 0), PS(3)], writes=["mt1"])
            T.add("dve", C("tensor_tensor", out=mt2[:, 0:N], in0=sgm[1][:, 0:N], in1=ps[4][:, 0:N], op=ALU.mult),
                  reads=[("sgm", 1), PS(4)], writes=["mt2"])
            T.add("pool", C("tensor_tensor", out=mt1[:, 0:N], in0=mt1[:, 0:N], in1=mt2[:, 0:N], op=ALU.add),
                  reads=["mt1", "mt2"], writes=["mt1"])
            T.add("dve", C("tensor_tensor", out=mt2[:, 0:N], in0=sgm[2][:, 0:N], in1=ps[5][:, 0:N], op=ALU.mult),
                  reads=[("sgm", 2), PS(5)], writes=["mt2"])
            T.add("pool", C("tensor_tensor", out=mT[:, c, 0:N], in0=mt1[:, 0:N], in1=mt2[:, 0:N], op=ALU.add),
                  reads=["mt1", "mt2"], writes=[("mT", c)])
        tok_proj(l, "w_out", mT, lambda tt: [("mT", c) for c in range(8)], nt, R, 1.0)
        for tt in range(nt):
            layer_norm(l, 1, tt, R, None, None)

    def mem_phase(l, nt, R):
        N = nt * R
        load_ln(l, 2)
        wq = I["w_mem_q"][l]
        for half in range(2):
            wt, wk = wload([(lambda w: w3(w, 8, 512), wq[:, half * 512:(half + 1) * 512].rearrange("(k p) c -> p k c", p=128))])
            wv = w3(wt, 8, 512)
            for cc in range(4):
                c = half * 4 + cc
                pb = nextps(0, 2)
                for k in range(8):
                    T.add("pe", C("matmul", ps[pb][:, 0:N], wv[:, k, cc * 128:(cc + 1) * 128], xT[:, k, 0:N],
                                                                    start=(k == 0), stop=(k == 7)),
                          reads=[wk] + [("xT", t) for t in range(nt)], writes=[PS(pb)])
                T.add("act", C("copy", qmT[:, c, 0:N], ps[pb][:, 0:N]), reads=[PS(pb)], writes=[("qmT", c)])
        for h in range(4):
            for kt in range(2):
                sbk = nextps(0, 2)
                for c2 in range(2):
                    T.add("pe", C("matmul", ps[sbk][:, 0:N], mkT[:, h * 2 + c2, kt * 128:(kt + 1) * 128],
                                                                   qmT[:, h * 2 + c2, 0:N], start=(c2 == 0), stop=(c2 == 1)),
                          reads=[("mkT", kt), ("qmT", h * 2), ("qmT", h * 2 + 1)], writes=[PS(sbk)])
                pti = sbk
                T.add("act", C("activation", out=PT[pti][:, 0:N], in_=ps[sbk][:, 0:N], func=AF.Exp, scale=1.0 / 16.0),
                      reads=[PS(sbk)], writes=[("PT", pti)])
                for tt in range(nt):
                    bk = 2 + tt
                    T.add("pe", C("matmul", ps[bk][0:R, 0:257], PT[pti][:, tt * R:(tt + 1) * R],
                                                                         mv_aug[:, kt, h * 257:(h + 1) * 257], start=(kt == 0), stop=(kt == 1)),
                          reads=[("PT", pti), ("mv", kt)], writes=[PS(bk)])
            for tt in range(nt):
                bk = 2 + tt
                T.add("dve", C("reciprocal", out=small[0:R, 0:1], in_=ps[bk][0:R, 256:257]), reads=[PS(bk)], writes=["small"])
                T.add("dve", C("tensor_scalar", out=y16[0:R, 0:256], in0=ps[bk][0:R, 0:256], scalar1=small[0:R, 0:1],
                                                              scalar2=None, op0=ALU.mult), reads=[PS(bk), "small"], writes=["y16"])
                pb2 = transposes(y16, "y16", R, [(0, 128), (128, 128)], None, None)
                evac("act", yT[:, 2 * h:2 * h + 2, tt * R:(tt + 1) * R],
                     pt[pb2][:, 0:256].rearrange("p (k c) -> p k c", c=128)[:, :, 0:R], [PTK(pb2)], [("yT", tt)])
        tok_proj(l, "w_mem_o", yT, lambda tt: [("yT", tt)], nt, R, 1.0)
        for tt in range(nt):
            layer_norm(l, 2, tt, R, None, None)

    def load_block(src_rows, nt, R):
        for tt in range(nt):
            sap, skey = src_rows(tt)
            T.add("sp", C("dma_start", out=xn[0:R, :], in_=sap), reads=[skey], writes=["xn"], dma=True)
            T.add("act", C("mul", x_tm[0:R, tt, :], xn[0:R, :], ALPHA), reads=["xn"], writes=[("x_tm", tt)])
            T.add("dve", C("tensor_copy", xbf[0:R, :], xn[0:R, :]), reads=["xn"], writes=["xbf"])
            b = transposes(xbf, "xbf", R, [(k * 128, 128) for k in range(8)], None, None)
            evac("act", xT[:, :, tt * R:(tt + 1) * R], pt[b][:, :].rearrange("p (k c) -> p k c", c=128)[:, :, 0:R],
                 reads=[PTK(b)], writes=[("xT", tt)])

    class _Stop(Exception):
        pass

    phc = {}

    def ph(name):
        phc[name] = phc.get(name, 0) + 1
        if stop is not None:
            sn, _, cnt = stop.partition("@")
            if name == sn and phc[name] >= int(cnt or 1):
                raise _Stop()

    try:
        for (kind, b) in units:
            prompt = kind == "p"
            nblk, nt, R = (4, 4, 128) if prompt else (1, 1, 64)
            for l in range(n_layers):
                layer_consts(l)
                ph("consts")
                T.barrier()
                mem_prep(l, kind, b)
                ph("memprep")
                if not prompt:
                    T.barrier()
                    ingest_cache(l, b)
                    ph("ingest")
                for tb in range(nblk):
                    if l == 0:
                        if prompt:
                            src = lambda tt, tb=tb: (I["x_prompt"][b, (tb * 4 + tt) * 128:(tb * 4 + tt + 1) * 128, :], "xin")
                        else:
                            src = lambda tt: (I["x_sample"][b, :, :], "xin")
                    else:
                        src = lambda tt, tb=tb, R=R: (xmid[(tb * 4 + tt) * 128:(tb * 4 + tt) * 128 + R, :], ("xmid", tb * 4 + tt))
                    load_block(src, nt, R)
                    ph("load")
                    if tb == 0:
                        T.barrier()
                    ffn(l, 1, nt, R, 0, None)
                    ph("ffn1")
                    T.barrier()
                    win_phase(l, kind, b, tb, nt, R)
                    ph("win")
                    attn_phase(l, kind, tb, nt, R)
                    ph("attn")
                    T.barrier()
                    merge_phase(l, nt, R)
                    ph("merge")
                    mem_phase(l, nt, R)
                    ph("mem")
                    T.barrier()
                    if l == n_layers - 1:
                        if prompt:
                            fd = lambda tt, tb=tb: (O["y_p"][b, (tb * 4 + tt) * 128:(tb * 4 + tt + 1) * 128, :], "yout")
                        else:
                            fd = lambda tt: (O["y_s"][b, :, :], "yout")
                    else:
                        fd = lambda tt, tb=tb, R=R: (xmid[(tb * 4 + tt) * 128:(tb * 4 + tt) * 128 + R, :], ("xmid", tb * 4 + tt))
                    ffn(l, 2, nt, R, 3, fd)
                    ph("ffn2")
    except _Stop:
        pass
    T.emit(nc)
    es.close()
    return nc


def _bias_table(a_rel_bias):
    L = a_rel_bias.shape[0]
    kl = np.arange(640)[:, None]
    ql = np.arange(128)[None, :]
    idx = np.clip(ql - kl + 512, -128, 128) + 128
    masked = ((ql < 64) & (kl >= 576)) | ((ql >= 64) & (kl < 64))
    tab = a_rel_bias[:, :, idx]
    tab = np.where(masked[None, None], np.float32(NEG), tab).astype(np.float32)
    tab = tab.reshape(L, 4, 5, 128, 128).transpose(0, 3, 2, 1, 4)
    return np.ascontiguousarray(tab.reshape(L, 128, 5 * 4 * 128))


def _rope_table():
    pos = np.arange(17 * 128, dtype=np.float32)
    inv = (500000.0 ** (-np.arange(8, dtype=np.float32) / 8)).astype(np.float32)
    ang = pos[:, None] * inv[None, :]
    return np.concatenate([np.cos(ang), np.sin(ang)], axis=1).astype(np.float32)


_OUT_ORDER = ["y_p", "y_s", "a_k_p", "a_v_p", "b_k_p", "b_v_p", "b_i_p", "c_k_p", "c_v_p", "m_k_p", "m_v_p",
              "a_k_s", "a_v_s", "b_k_s", "b_v_s", "b_i_s", "c_k_s", "c_v_s"]


def make_in_maps(inp):
    f = lambda a: np.ascontiguousarray(np.asarray(a, dtype=np.float32))
    shared = {k: f(inp[k]) for k in ("ln_g", "ln_b", "ffn1_w_gu", "ffn1_w_d", "ffn2_w_gu", "ffn2_w_d", "w_in",
                                     "w_branch_a", "w_branch_b", "w_branch_c", "w_out", "w_mem_q", "w_mem_k",
                                     "w_mem_v", "w_mem_o", "c_subln_g")}
    shared["c_lambda"] = f(inp["c_lambda"]).reshape(DEPTH, 256)
    shared["biasA"] = _bias_table(f(inp["a_rel_bias"]))
    shared["rope_cs"] = _rope_table()
    maps = []
    for c in range(NCORES):
        s = slice(4 * c, 4 * c + 4)
        m = dict(shared)
        m["x_prompt"] = f(inp["x_prompt"][s])
        m["x_sample"] = f(inp["x_sample"][s])
        m["mem_prompt"] = f(inp["mem_prompt"][s])
        m["cache_a_k"] = f(inp["cache_a_k"][:, s]).reshape(DEPTH, 4, 512, 256)
        m["cache_a_v"] = f(inp["cache_a_v"][:, s]).reshape(DEPTH, 4, 512, 256)
        m["cache_b_k"] = f(inp["cache_b_k"][:, s])
        m["cache_b_v"] = f(inp["cache_b_v"][:, s])
        m["cache_b_idx"] = f(inp["cache_b_idx"][:, s])
        m["cache_c_k"] = f(inp["cache_c_k"][:, s]).reshape(DEPTH, 4, PAST, 512)
        m["cache_c_v"] = f(inp["cache_c_v"][:, s]).reshape(DEPTH, 4, PAST, 512)
        m["cache_mem_k"] = f(inp["cache_mem_k"][:, s]).reshape(DEPTH, 4, 256, 1024)
        m["cache_mem_v"] = f(inp["cache_mem_v"][:, s]).reshape(DEPTH, 4, 256, 1024)
        maps.append(m)
    return maps


def assemble(results):
    def cat(name, axis):
        return np.concatenate([r[name] for r in results], axis=axis)
    B = 4 * len(results)
    out = {
        "y_p": cat("y_p", 0), "y_s": cat("y_s", 0),
        "a_k_p": cat("a_k_p", 1).reshape(DEPTH, B, 512, 4, 64), "a_v_p": cat("a_v_p", 1).reshape(DEPTH, B, 512, 4, 64),
        "b_k_p": cat("b_k_p", 1), "b_v_p": cat("b_v_p", 1), "b_i_p": cat("b_i_p", 1),
        "c_k_p": cat("c_k_p", 1).reshape(DEPTH, B, SEQ, 4, 128), "c_v_p": cat("c_v_p", 1).reshape(DEPTH, B, SEQ, 4, 128),
        "m_k_p": cat("m_k_p", 1).reshape(DEPTH, B, 256, 4, 256), "m_v_p": cat("m_v_p", 1).reshape(DEPTH, B, 256, 4, 256),
        "a_k_s": cat("a_k_s", 1).reshape(DEPTH, B, TS, 4, 64), "a_v_s": cat("a_v_s", 1).reshape(DEPTH, B, TS, 4, 64),
        "b_k_s": cat("b_k_s", 1), "b_v_s": cat("b_v_s", 1), "b_i_s": cat("b_i_s", 1),
        "c_k_s": cat("c_k_s", 1).reshape(DEPTH, B, TS, 4, 128), "c_v_s": cat("c_v_s", 1).reshape(DEPTH, B, TS, 4, 128),
    }
    return tuple(np.ascontiguousarray(out[k], dtype=np.float32) for k in _OUT_ORDER)


def kernel(**inputs):
    units = [("p", b) for b in range(4)] + [("s", b) for b in range(4)]
    nc = build_program(units)
    in_maps = make_in_maps(inputs)
    res = run_bass_kernel_spmd(nc, in_maps, core_ids=list(range(NCORES)))
    return assemble(res.results)
```
